# Optimizing a Trainium2 kernel written in Bass

```python
import jax, jax.numpy as jnp
from jax import lax
import numpy as np

D_MODEL = 1024
BATCH = 4
SEQ = 8192
DEPTH = 4
DEC_BATCH = 16
DEC_SEQ = 16
PAST_LEN = 4096

CHUNK = 64
N_MIXERS = 2
N_HEADS = 8
HEAD_DIM = 128
ATTN_WIDTH = N_HEADS * HEAD_DIM
CONV_WIDTH = D_MODEL
CONV_K = 3
Q_BLOCK = 128
NORM_EPS = 1e-6
FORGET_BIAS = 3.0

kernel_name = "fox_shortconv_hybrid_stream_step"


def rms_norm(x, g):
    xf = x.astype(jnp.float32)
    y = xf * lax.rsqrt(jnp.mean(xf * xf, axis=-1, keepdims=True) + NORM_EPS)
    return (y * g.astype(jnp.float32)).astype(x.dtype)


def fox_project(h, w_in, b_f, g_q, g_k):
    B, T, _ = h.shape
    z = h @ w_in
    q, k, v, gate = jnp.split(z[..., :4 * ATTN_WIDTH], 4, axis=-1)
    logf = jax.nn.log_sigmoid((z[..., 4 * ATTN_WIDTH:] + b_f).astype(jnp.float32))
    q = rms_norm(q.reshape(B, T, N_HEADS, HEAD_DIM), g_q)
    k = rms_norm(k.reshape(B, T, N_HEADS, HEAD_DIM), g_k)
    v = v.reshape(B, T, N_HEADS, HEAD_DIM)
    return q, k, v, gate, logf


def fox_attend(q, k, v, cq, ck, qpos, kpos):
    s = jnp.einsum("bqhd,bkhd->bhqk", q, k).astype(jnp.float32) * (HEAD_DIM ** -0.5)
    s = s + jnp.transpose(cq, (0, 2, 1))[:, :, :, None] - jnp.transpose(ck, (0, 2, 1))[:, :, None, :]
    mask = kpos[None, :] <= qpos[:, None]
    s = jnp.where(mask[None, None], s, -jnp.inf)
    p = jax.nn.softmax(s, axis=-1).astype(v.dtype)
    return jnp.einsum("bhqk,bkhd->bqhd", p, v)


def fox_out(x, o, gate, w_out):
    B, T = o.shape[0], o.shape[1]
    return x + (jax.nn.silu(gate) * o.reshape(B, T, ATTN_WIDTH)) @ w_out


def fox_prompt(x, g_norm, w_in, b_f, g_q, g_k, w_out):
    B, S, _ = x.shape
    q, k, v, gate, logf = fox_project(rms_norm(x, g_norm), w_in, b_f, g_q, g_k)
    c = jnp.cumsum(logf, axis=1)
    nb = S // Q_BLOCK
    pos = jnp.arange(S)
    qb = jnp.transpose(q.reshape(B, nb, Q_BLOCK, N_HEADS, HEAD_DIM), (1, 0, 2, 3, 4))
    cb = jnp.transpose(c.reshape(B, nb, Q_BLOCK, N_HEADS), (1, 0, 2, 3))
    pb = pos.reshape(nb, Q_BLOCK)
    ob = lax.map(lambda a: fox_attend(a[0], k, v, a[1], c, a[2], pos), (qb, cb, pb))
    o = jnp.transpose(ob, (1, 0, 2, 3, 4)).reshape(B, S, N_HEADS, HEAD_DIM)
    return fox_out(x, o, gate, w_out), k, v, logf


def fox_sample(x, cache_k, cache_v, cache_logf, g_norm, w_in, b_f, g_q, g_k, w_out):
    T = x.shape[1]
    P = cache_k.shape[1]
    q, k, v, gate, logf = fox_project(rms_norm(x, g_norm), w_in, b_f, g_q, g_k)
    k_all = jnp.concatenate([cache_k, k.astype(cache_k.dtype)], axis=1)
    v_all = jnp.concatenate([cache_v, v.astype(cache_v.dtype)], axis=1)
    c = jnp.cumsum(jnp.concatenate([cache_logf.astype(jnp.float32), logf], axis=1), axis=1)
    kpos = jnp.arange(P + T)
    qpos = P + jnp.arange(T)
    o = fox_attend(q, k_all, v_all, c[:, P:], c, qpos, kpos)
    return fox_out(x, o, gate, w_out), k, v, logf


def conv_branch(x, u_hist, g_norm, w_in, conv_w, w_out):
    T = x.shape[1]
    z = rms_norm(x, g_norm) @ w_in
    b, c, u, gate = jnp.split(z, 4, axis=-1)
    u_pad = jnp.concatenate([u_hist.astype(u.dtype), c * u], axis=1)
    conv = conv_w[0] * u_pad[:, 0:T]
    for j in range(1, CONV_K):
        conv = conv + conv_w[j] * u_pad[:, j:j + T]
    y = x + (jax.nn.silu(gate) * b * conv) @ w_out
    return y, u_pad[:, -(CONV_K - 1):]


def setup_inputs(seed: int = 0) -> dict:
    key = jax.random.key(seed)
    ks = iter(jax.random.split(key, 64))

    def nrm(shape, scale=1.0):
        return scale * jax.random.normal(next(ks), shape, jnp.float32)

    d = {}
    d["x_prompt"] = nrm((BATCH, SEQ, D_MODEL))
    d["x_sample"] = nrm((DEC_BATCH, DEC_SEQ, D_MODEL))
    for i in range(DEPTH):
        if i % N_MIXERS == 0:
            d[f"cache_k_l{i}"] = nrm((DEC_BATCH, PAST_LEN, N_HEADS, HEAD_DIM))
            d[f"cache_v_l{i}"] = nrm((DEC_BATCH, PAST_LEN, N_HEADS, HEAD_DIM))
            d[f"cache_logf_l{i}"] = jax.nn.log_sigmoid(FORGET_BIAS + nrm((DEC_BATCH, PAST_LEN, N_HEADS)))
        else:
            d[f"state_conv_l{i}"] = nrm((DEC_BATCH, CONV_K - 1, CONV_WIDTH))
    for i in range(DEPTH):
        d[f"norm_l{i}"] = 1.0 + nrm((D_MODEL,), 0.02)
        if i % N_MIXERS == 0:
            d[f"w_in_l{i}"] = nrm((D_MODEL, 4 * ATTN_WIDTH + N_HEADS), D_MODEL ** -0.5)
            d[f"b_f_l{i}"] = FORGET_BIAS + nrm((N_HEADS,), 0.1)
            d[f"qnorm_l{i}"] = 1.0 + nrm((HEAD_DIM,), 0.02)
            d[f"knorm_l{i}"] = 1.0 + nrm((HEAD_DIM,), 0.02)
            d[f"w_out_l{i}"] = nrm((ATTN_WIDTH, D_MODEL), ATTN_WIDTH ** -0.5)
        else:
            d[f"w_in_l{i}"] = nrm((D_MODEL, 4 * CONV_WIDTH), D_MODEL ** -0.5)
            d[f"conv_w_l{i}"] = nrm((CONV_K, CONV_WIDTH), CONV_K ** -0.5)
            d[f"w_out_l{i}"] = nrm((CONV_WIDTH, D_MODEL), CONV_WIDTH ** -0.5)
    return d


def reference(x_prompt, x_sample,
              cache_k_l0, cache_v_l0, cache_logf_l0, state_conv_l1,
              cache_k_l2, cache_v_l2, cache_logf_l2, state_conv_l3,
              norm_l0, w_in_l0, b_f_l0, qnorm_l0, knorm_l0, w_out_l0,
              norm_l1, w_in_l1, conv_w_l1, w_out_l1,
              norm_l2, w_in_l2, b_f_l2, qnorm_l2, knorm_l2, w_out_l2,
              norm_l3, w_in_l3, conv_w_l3, w_out_l3):
    caches = [(cache_k_l0, cache_v_l0, cache_logf_l0), (state_conv_l1,),
              (cache_k_l2, cache_v_l2, cache_logf_l2), (state_conv_l3,)]
    params = [(norm_l0, w_in_l0, b_f_l0, qnorm_l0, knorm_l0, w_out_l0),
              (norm_l1, w_in_l1, conv_w_l1, w_out_l1),
              (norm_l2, w_in_l2, b_f_l2, qnorm_l2, knorm_l2, w_out_l2),
              (norm_l3, w_in_l3, conv_w_l3, w_out_l3)]
    yp, ys = x_prompt, x_sample
    new = []
    for i in range(DEPTH):
        if i % N_MIXERS == 0:
            yp, kp, vp, lfp = fox_prompt(yp, *params[i])
            ys, ks, vs, lfs = fox_sample(ys, *caches[i], *params[i])
            new.append((kp, vp, lfp, ks, vs, lfs))
        else:
            zero_hist = jnp.zeros((yp.shape[0], CONV_K - 1, CONV_WIDTH), yp.dtype)
            yp, cp = conv_branch(yp, zero_hist, *params[i])
            ys, cs = conv_branch(ys, caches[i][0], *params[i])
            new.append((cp, cs))
    k_p0, v_p0, lf_p0, k_s0, v_s0, lf_s0 = new[0]
    c_p1, c_s1 = new[1]
    k_p2, v_p2, lf_p2, k_s2, v_s2, lf_s2 = new[2]
    c_p3, c_s3 = new[3]
    return (yp, ys,
            k_p0, v_p0, lf_p0, k_s0, v_s0, lf_s0,
            c_p1, c_s1,
            k_p2, v_p2, lf_p2, k_s2, v_s2, lf_s2,
            c_p3, c_s3)
```

```python
import numpy as np
from contextlib import ExitStack
import concourse.bass as bass
import concourse.mybir as mybir
from concourse.bass_utils import run_bass_kernel_spmd

F32 = mybir.dt.float32
BF16 = mybir.dt.bfloat16
AF = mybir.ActivationFunctionType
ALU = mybir.AluOpType
AX = mybir.AxisListType

D = 1024
H = 8
DH = 128
EPS = 1e-6
SCALE = DH ** -0.5
ENGS = ("sync", "scalar", "vector", "gpsimd", "tensor")
DEBUG = False
STQ = "gpsimd"
SEM_LIMIT = 8000
DMA_LIMIT = 1500


class Tok:
    __slots__ = ("eng", "sem", "val", "needed", "dma", "phase")

    def __init__(self, eng, phase, dma=False):
        self.eng = eng
        self.sem = None
        self.val = None
        self.needed = dma
        self.dma = dma
        self.phase = phase


class Buf:
    __slots__ = ("name", "w", "r", "rp", "psum")

    def __init__(self, name="", psum=False):
        self.name = name
        self.w = []
        self.r = []
        self.rp = []
        self.psum = psum


class Prog:
    def __init__(self, nc, es):
        self.nc = nc
        self.es = es
        self.nsem = 0
        self.phase = 0
        self.cur = {}
        self.waited = {e: {} for e in ENGS}
        self.ring = []
        self.ring_i = 0
        self.ops = {e: [] for e in ENGS}
        self.last = {e: None for e in ENGS}

    def _newsem(self):
        self.nsem += 1
        return self.es.enter_context(self.nc.semaphore("s%d" % self.nsem))

    def _deps(self, r, w, a):
        deps = []
        raw = []
        for b in r:
            deps.extend(b.w)
            raw.extend(b.w)
            if b.psum:
                deps.extend(b.r)
        for b in w:
            deps.extend(b.w)
            deps.extend(b.r)
        for b in a:
            if b.r:
                b.rp = b.r
                b.r = []
                b.w = []
            deps.extend(b.rp)
        return deps, raw

    def _commit(self, tok, r, w, a):
        for b in r:
            b.r.append(tok)
        for b in w:
            b.w = [tok]
            b.r = []
            b.rp = []
        for b in a:
            b.w.append(tok)

    def op(self, eng, fn, r=(), w=(), a=()):
        deps, raw = self._deps(r, w, a)
        tok = Tok(eng, self.phase)
        waits = []
        seen = set()
        for d in deps:
            if id(d) in seen or d.phase != self.phase:
                continue
            seen.add(id(d))
            if d.eng == eng and not d.dma:
                if eng == "tensor":
                    continue
                if not any(d is x for x in raw):
                    continue
            d.needed = True
            waits.append(d)
        self.ops[eng].append((fn, waits, tok))
        self._commit(tok, r, w, a)
        self.last[eng] = tok
        return tok

    def dma(self, eng, fn, r=(), w=(), a=()):
        deps, _ = self._deps(r, w, a)
        tok = Tok(eng, self.phase, dma=True)
        if len(self.ring) < 14:
            self.ring.append([self._newsem(), 0, None])
            slot = self.ring[-1]
        else:
            slot = self.ring[self.ring_i % len(self.ring)]
            self.ring_i += 1
        if slot[1] >= DMA_LIMIT:
            old = slot[2]
            slot[0] = self._newsem()
            slot[1] = 0
            if old is not None:
                deps.append(old)
            slot[2] = None
        if slot[2] is not None:
            deps.append(slot[2])
        slot[1] += 1
        tok.sem = slot[0]
        tok.val = 16 * slot[1]
        slot[2] = tok
        waits = []
        seen = set()
        for d in deps:
            if id(d) in seen or d.phase != self.phase:
                continue
            seen.add(id(d))
            d.needed = True
            waits.append(d)
        self.ops[eng].append((fn, waits, tok))
        self._commit(tok, r, w, a)
        return tok

    def barrier(self):
        toks = [t for t in self.last.values() if t is not None and t.phase == self.phase]
        toks += [sl[2] for sl in self.ring if sl[2] is not None and sl[2].phase == self.phase]
        for t in toks:
            t.needed = True
        for e in ENGS:
            self.ops[e].append((None, list(toks), None))

    def flush(self):
        for e in ENGS:
            for fn, waits, tok in self.ops[e]:
                if tok is None or tok.dma or not tok.needed:
                    continue
                cur = self.cur.get(e)
                if cur is None or cur[1] >= SEM_LIMIT:
                    cur = [self._newsem(), 0]
                    self.cur[e] = cur
                cur[1] += 1
                tok.sem = cur[0]
                tok.val = cur[1]
        ops = self.ops
        waited = self.waited

        def run(ename):
            def body(eng):
                wd = waited[ename]
                for fn, waits, tok in ops[ename]:
                    best = {}
                    for d in waits:
                        key = id(d.sem)
                        if key not in best or d.val > best[key].val:
                            best[key] = d
                    for key, d in best.items():
                        if wd.get(key, 0) >= d.val:
                            continue
                        eng.wait_ge(d.sem, d.val)
                        wd[key] = d.val
                    if fn is None:
                        continue
                    ins = fn(eng)
                    if tok is not None and tok.needed:
                        ins.then_inc(tok.sem, 16 if tok.dma else 1)
            return body

        with self.nc.Block() as blk:
            blk.sync(run("sync"))
            blk.scalar(run("scalar"))
            blk.vector(run("vector"))
            blk.gpsimd(run("gpsimd"))
            blk.tensor(run("tensor"))
        self.ops = {e: [] for e in ENGS}
        self.phase += 1


class Rot:
    def __init__(self, tiles, psum=False):
        self.tiles = tiles
        self.bufs = [Buf(psum=psum) for _ in tiles]
        self.i = 0

    def next(self):
        k = self.i % len(self.tiles)
        self.i += 1
        return self.tiles[k], self.bufs[k]


def build(S, PAST, NL=4, with_sample=True):
    NT = S // 128
    NM = S // 512
    NQ = S // 256
    PT = PAST // 128
    NS = 32
    WIN_F = 4 * D + H
    assert PT * 16 <= 512 and PT % 4 == 0

    nc = bass.Bass("TRN2", target_bir_lowering=False)
    _uid = [0]

    def uq(name):
        _uid[0] += 1
        return "%s_%d" % (name, _uid[0])

    def din(name, shape, dt=F32):
        return nc.dram_tensor(name, list(shape), dt, kind="ExternalInput").ap()

    def dout(name, shape, dt=F32):
        return nc.dram_tensor(name, list(shape), dt, kind="ExternalOutput").ap()

    def dint(name, shape, dt=F32):
        return nc.dram_tensor(name, list(shape), dt, kind="Internal").ap()

    xp = din("xp", [S, D])
    xs = din("xs", [NS, D])
    w_in, w_out, g_norm, b_f, g_q, g_k, conv_w = {}, {}, {}, {}, {}, {}, {}
    ck, cv, clf, sconv = {}, {}, {}, {}
    for l in range(4):
        fox = l % 2 == 0
        g_norm[l] = din("norm%d" % l, [D])
        w_in[l] = din("w_in%d" % l, [D, WIN_F if fox else 4 * D])
        w_out[l] = din("w_out%d" % l, [D, D])
        if fox:
            b_f[l] = din("b_f%d" % l, [H])
            g_q[l] = din("qn%d" % l, [DH])
            g_k[l] = din("kn%d" % l, [DH])
            ck[l] = din("ck%d" % l, [2, PAST, D])
            cv[l] = din("cv%d" % l, [2, PAST, D])
            clf[l] = din("clf%d" % l, [2, PAST, H])
        else:
            conv_w[l] = din("cw%d" % l, [3, D])
            sconv[l] = din("sc%d" % l, [2, 2, D])

    o_yp = dout("o_yp", [S, D])
    o_ys = dout("o_ys", [NS, D])
    o_kp, o_vp, o_lfp, o_ks, o_vs, o_lfs, o_cp, o_cs = {}, {}, {}, {}, {}, {}, {}, {}
    for l in range(4):
        if l % 2 == 0:
            o_kp[l] = dout("o_kp%d" % l, [S, D])
            o_vp[l] = dout("o_vp%d" % l, [S, D])
            o_lfp[l] = dout("o_lfp%d" % l, [S, H])
            o_ks[l] = dout("o_ks%d" % l, [NS, D])
            o_vs[l] = dout("o_vs%d" % l, [NS, D])
            o_lfs[l] = dout("o_lfs%d" % l, [NS, H])
        else:
            o_cp[l] = dout("o_cp%d" % l, [2, D])
            o_cs[l] = dout("o_cs%d" % l, [2, 2, D])

    DBG = {}
    if DEBUG:
        DBG["d_cnew"] = dout("d_cnew", [2, 16, H])
        DBG["d_cref"] = dout("d_cref", [2, 128, H])
        DBG["d_rls"] = dout("d_rls", [2, 16, H])
        DBG["d_pTn"] = dout("d_pTn", [2, 16, H * 16])
        DBG["d_cc"] = dout("d_cc", [2, 128, PT * H])
        DBG["d_lfn"] = dout("d_lfn", [2, 16, H])
        DBG["d_cbn"] = dout("d_cbn", [2, 16, H])
        DBG["d_os"] = dout("d_os", [2, 16, D], BF16)
        DBG["d_pT"] = dout("d_pT", [2, 128, H * PT * 16])
    y_sc = [dint("y_a", [S, D]), dint("y_b", [S, D])]
    ys_sc = [dint("ys_a", [NS, D]), dint("ys_b", [NS, D])]
    qT_s = dint("qT_s", [H, 128, S], BF16)
    kT_s = dint("kT_s", [H, 128, S], BF16)
    sgT_s = dint("sgT_s", [H, 128, S], BF16)
    ogT_s = dint("ogT_s", [H, 128, S], BF16)
    v16_s = dint("v16_s", [H, 128, NT * 129], BF16)

    es = ExitStack()
    with es:
        P = Prog(nc, es)

        def sb(name, shape, dt):
            return es.enter_context(nc.sbuf_tensor(uq(name), list(shape), dt))

        ident = sb("ident", [128, 128], BF16)
        identf = sb("identf", [128, 128], F32)
        tri = sb("tri", [128, 128], F32)
        trix = sb("trix", [128, 128], F32)
        trib = sb("trib", [128, 128], BF16)
        sel_last = sb("sel_last", [128, 128], F32)
        ones_f = sb("ones_f", [128, 128], F32)
        sel15 = sb("sel15", [128, 1], F32)
        s_qT = sb("s_qT", [128, H, NS], BF16)
        s_kT = sb("s_kT", [128, H, NS], BF16)
        s_sgT = sb("s_sgT", [128, H, NS], BF16)
        s_ogT = sb("s_ogT", [128, H, NS], BF16)
        s_lf = sb("s_lf", [128, H], F32)
        c_all = sb("c_all", [128, NT, H], F32)
        cref_all = sb("cref_all", [128, NT, H], F32)

        def mk_consts(e):
            def sel(t, pat, op, base, cm):
                e.memset(t[:], 1.0)
                return e.affine_select(out=t[:], in_=t[:], pattern=pat, compare_op=op, fill=0.0, base=base,
                                       channel_multiplier=cm)
            sel(ident, [[1, 128]], ALU.is_equal, 0, -1)
            sel(identf, [[1, 128]], ALU.is_equal, 0, -1)
            sel(tri, [[1, 128]], ALU.is_ge, 0, -1)
            sel(trix, [[1, 128]], ALU.is_gt, 0, -1)
            sel(trib, [[1, 128]], ALU.is_ge, 0, -1)
            sel(sel_last, [[0, 128]], ALU.is_equal, -127, 1)
            sel(sel15, [[0, 1]], ALU.is_equal, -15, 1)
            return e.memset(ones_f[:], 1.0)

        def consts_phase():
            P.op("gpsimd", mk_consts)
            P.barrier()
            P.flush()

        consts_phase()

        def load_weight_bf16(wdram, rows, cols, w16, B_w, stage_rot, col_chunk):
            k = 0
            for kc in range(rows // 128):
                c0 = 0
                while c0 < cols:
                    cw_ = min(col_chunk, cols - c0)
                    st, bst = stage_rot.next()
                    P.dma("sync", lambda e, st=st, kc=kc, c0=c0, cw_=cw_: e.dma_start(
                        out=st[:, 0:cw_], in_=wdram[kc * 128:(kc + 1) * 128, c0:c0 + cw_]), w=[bst])
                    eng = "gpsimd" if k % 2 == 0 else "vector"
                    P.op(eng, lambda e, st=st, kc=kc, c0=c0, cw_=cw_: e.tensor_copy(
                        out=w16[:, kc, c0:c0 + cw_], in_=st[:, 0:cw_]), r=[bst], a=[B_w])
                    k += 1
                    c0 += cw_

        def rms_tile(x_t, Bx, nt, gtile, Bg, h16, Bh, junk, Bjunk, st_small, Bst):
            P.op("vector", lambda e: e.memset(st_small[0:nt, 0:1], 0.0), w=[Bst])
            P.op("scalar", lambda e: e.activation(out=junk[0:nt, :], in_=x_t[0:nt, :], func=AF.Square,
                                                  accum_out=st_small[0:nt, 0:1]), r=[Bx, Bst], w=[Bjunk, Bst])
            P.op("scalar", lambda e: e.activation(out=st_small[0:nt, 1:2], in_=st_small[0:nt, 0:1], func=AF.Ln,
                                                  bias=EPS, scale=1.0 / D), r=[Bst], w=[Bst])
            P.op("scalar", lambda e: e.activation(out=st_small[0:nt, 2:3], in_=st_small[0:nt, 1:2], func=AF.Exp,
                                                  scale=-0.5), r=[Bst], w=[Bst])
            P.op("vector", lambda e: e.scalar_tensor_tensor(out=h16[0:nt, :], in0=x_t[0:nt, :],
                                                            scalar=st_small[0:nt, 2:3], in1=gtile[0:nt, :],
                                                            op0=ALU.mult, op1=ALU.mult),
                 r=[Bx, Bst, Bg], w=[Bh])

        def transpose_rows(src16, Bsrc, nt, nchunks, pst, Bpst, dst_ap, Bdst, evac_eng, acc=False):
            for c in range(nchunks):
                P.op("tensor", lambda e, c=c: e.transpose(pst[:, c * 128:c * 128 + nt],
                                                          src16[0:nt, c * 128:(c + 1) * 128], ident[0:nt, 0:nt]),
                     r=[Bsrc], w=[Bpst] if c == 0 else [], a=[] if c == 0 else [Bpst])
            src_ap = pst[:, 0:nchunks * 128].rearrange("p (c t) -> p c t", t=128)[:, :, 0:nt]
            kw = dict(r=[Bpst], a=[Bdst]) if acc else dict(r=[Bpst], w=[Bdst])
            if evac_eng == "scalar":
                return P.op("scalar", lambda e: e.copy(out=dst_ap, in_=src_ap), **kw)
            return P.op(evac_eng, lambda e: e.tensor_copy(out=dst_ap, in_=src_ap), **kw)

        def fox_layer(l, src_p, Bsrc_p, src_s, Bsrc_s, dst_p, Bdst_p, dst_s, Bdst_s):
            B_qT = [Buf() for _ in range(NQ)]
            B_v16 = [Buf() for _ in range(NT)]
            B_ogT = [Buf() for _ in range(NM)]
            B_vs_out = Buf()
            B_sqT, B_skT, B_ssgT, B_sogT, B_slf, B_call, B_cref = (Buf() for _ in range(7))

            with ExitStack() as ph:
                def sbp(name, shape, dt):
                    return ph.enter_context(nc.sbuf_tensor(uq(name), list(shape), dt))

                def psp(name, shape, dt=F32):
                    return ph.enter_context(nc.psum_tensor(uq(name), list(shape), dt))

                w16 = sbp("w16", [128, 8, WIN_F], BF16)
                B_w = Buf()
                stage = Rot([sbp("wst%d" % i, [128, 1026], F32) for i in range(2)])
                gtile = sbp("gtile", [128, D], F32)
                gq = sbp("gq", [128, 512], F32)
                gk = sbp("gk", [128, 512], F32)
                bft = sbp("bft", [128, H], F32)
                B_g = Buf()
                P.dma("sync", lambda e: e.dma_start(out=gtile[:], in_=g_norm[l].partition_broadcast(128)), a=[B_g])
                for j in range(4):
                    P.dma("sync", lambda e, j=j: e.dma_start(out=gq[:, j * 128:(j + 1) * 128],
                                                             in_=g_q[l].partition_broadcast(128)), a=[B_g])
                    P.dma("sync", lambda e, j=j: e.dma_start(out=gk[:, j * 128:(j + 1) * 128],
                                                             in_=g_k[l].partition_broadcast(128)), a=[B_g])
                P.dma("sync", lambda e: e.dma_start(out=bft[:], in_=b_f[l].partition_broadcast(128)), a=[B_g])
                load_weight_bf16(w_in[l], D, WIN_F, w16, B_w, stage, 1026)

                xrot = Rot([sbp("x%d" % i, [128, D], F32) for i in range(3)])
                junk = sbp("junk", [128, D], BF16)
                Bjunk = Buf()
                strot = Rot([sbp("st%d" % i, [128, 4], F32) for i in range(2)])
                hrot = Rot([sbp("h%d" % i, [128, D], BF16) for i in range(2)])
                hTrot = Rot([sbp("hT%d" % i, [128, 8, 128], BF16) for i in range(2)])
                sqrot = Rot([sbp("sq%d" % i, [128, 512], F32) for i in range(2)])
                ssrot = Rot([sbp("ss%d" % i, [128, 12], F32) for i in range(4)])
                kfrot = Rot([sbp("kf%d" % i, [128, D], F32) for i in range(2)])
                vfrot = Rot([sbp("vf%d" % i, [128, D], F32) for i in range(2)])
                q16rot = Rot([sbp("q16_%d" % i, [128, D], BF16) for i in range(2)])
                k16rot = Rot([sbp("k16_%d" % i, [128, D], BF16) for i in range(2)])
                sg16rot = Rot([sbp("sg16_%d" % i, [128, D], BF16) for i in range(2)])
                v16rot = Rot([sbp("v16_%d" % i, [128, H, 129], BF16) for i in range(2)])
                erot = Rot([sbp("e%d" % i, [128, 512], F32) for i in range(2)])
                qTst = Rot([sbp("qTst%d" % i, [128, H, 256], BF16) for i in range(2)])
                kTst = Rot([sbp("kTst%d" % i, [128, H, 256], BF16) for i in range(2)])
                sgTst = Rot([sbp("sgTst%d" % i, [128, H, 256], BF16) for i in range(2)])
                z_all = sbp("z_all", [128, NT + 1, H], F32)
                lf_all = sbp("lf_all", [128, NT + 1, H], F32)
                tsum = sbp("tsum", [1, NT + 1, H], F32)
                B_z = Buf()
                B_lf = Buf()
                B_ts = Buf()
                pmm = Rot([psp("pmm%d" % i, [128, 512], F32) for i in range(4)], psum=True)
                ptr = Rot([psp("ptr%d" % i, [128, 1024], BF16) for i in range(3)], psum=True)
                pz = psp("pz", [128, 512], F32)
                B_pz = Buf(psum=True)
                assert NT * H <= 512

                for i in range(2):
                    P.op("gpsimd", lambda e, t_=v16rot.tiles[i]: e.memset(t_[:, :, 128:129], 1.0),
                         a=[v16rot.bufs[i]])
                P.op("gpsimd", lambda e: e.memset(z_all[:, :, :], 0.0), w=[B_z])

                tiles = [(t, 128) for t in range(NT)] + ([(NT, NS)] if with_sample else [])
                cur_box = [None]

                def tile_body(t, nt):
                    is_s = t == NT
                    x_t, Bx = xrot.next()
                    if is_s:
                        P.dma("sync", lambda e, x_t=x_t: e.dma_start(out=x_t[0:NS, :], in_=src_s[:, :]),
                              r=[Bsrc_s], w=[Bx])
                    else:
                        P.dma("sync", lambda e, x_t=x_t, t=t: e.dma_start(out=x_t[:, :],
                                                                          in_=src_p[t * 128:(t + 1) * 128, :]),
                              r=[Bsrc_p[t]], w=[Bx])
                    st_s, Bst = strot.next()
                    h16, Bh = hrot.next()
                    rms_tile(x_t, Bx, nt, gtile, B_g, h16, Bh, junk, Bjunk, st_s, Bst)
                    yield "fa"
                    hT, BhT = hTrot.next()
                    pt, Bpt = ptr.next()
                    transpose_rows(h16, Bh, nt, 8, pt, Bpt, hT[:, :, 0:nt], BhT, "vector")
                    yield "fb"

                    def mm_slab(c0, ncols, pso, Bpso):
                        for kc in range(8):
                            P.op("tensor", lambda e, kc=kc: e.matmul(pso[0:nt, 0:ncols], lhsT=hT[:, kc, 0:nt],
                                                                      rhs=w16[:, kc, c0:c0 + ncols],
                                                                      start=(kc == 0), stop=(kc == 7)),
                                 r=[BhT, B_w], w=[Bpso])

                    kf, Bkf = kfrot.next()
                    vf, Bvf = vfrot.next()
                    q16, Bq16 = q16rot.next()
                    k16, Bk16 = k16rot.next()
                    sg16, Bsg16 = sg16rot.next()
                    v16t, Bv16 = v16rot.next()
                    for which in range(2):
                        for half in range(2):
                            pso, Bpso = pmm.next()
                            mm_slab(which * D + half * 512, 512, pso, Bpso)
                            sq, Bsq = sqrot.next()
                            ss, Bss = ssrot.next()
                            P.op("scalar", lambda e, pso=pso, sq=sq: e.activation(out=sq[0:nt, :], in_=pso[0:nt, :],
                                                                                  func=AF.Square),
                                 r=[Bpso], w=[Bsq])
                            P.op("vector", lambda e, sq=sq, ss=ss: e.tensor_reduce(
                                out=ss[0:nt, 0:4], in_=sq[0:nt, :].rearrange("p (h d) -> p h d", d=128),
                                axis=AX.X, op=ALU.add), r=[Bsq], w=[Bss])
                            P.op("scalar", lambda e, ss=ss: e.activation(out=ss[0:nt, 4:8], in_=ss[0:nt, 0:4],
                                                                         func=AF.Ln, bias=EPS, scale=1.0 / DH),
                                 r=[Bss], w=[Bss])
                            P.op("scalar", lambda e, ss=ss: e.activation(out=ss[0:nt, 8:12], in_=ss[0:nt, 4:8],
                                                                         func=AF.Exp, scale=-0.5),
                                 r=[Bss], w=[Bss])
                            for hh in range(4):
                                c0_ = half * 512 + hh * 128
                                if which == 0:
                                    P.op("vector", lambda e, pso=pso, ss=ss, hh=hh, c0_=c0_: e.scalar_tensor_tensor(
                                        out=q16[0:nt, c0_:c0_ + 128], in0=pso[0:nt, hh * 128:(hh + 1) * 128],
                                        scalar=ss[0:nt, 8 + hh:9 + hh], in1=gq[0:nt, 0:128], op0=ALU.mult,
                                        op1=ALU.mult), r=[Bpso, Bss, B_g], a=[Bq16])
                                else:
                                    P.op("vector", lambda e, pso=pso, ss=ss, hh=hh, c0_=c0_: e.scalar_tensor_tensor(
                                        out=kf[0:nt, c0_:c0_ + 128], in0=pso[0:nt, hh * 128:(hh + 1) * 128],
                                        scalar=ss[0:nt, 8 + hh:9 + hh], in1=gk[0:nt, 0:128], op0=ALU.mult,
                                        op1=ALU.mult), r=[Bpso, Bss, B_g], a=[Bkf])
                    P.op("scalar", lambda e: e.copy(out=k16[0:nt, :], in_=kf[0:nt, :]), r=[Bkf], w=[Bk16])
                    yield "A"
                    for half in range(2):
                        pso, Bpso = pmm.next()
                        mm_slab(2 * D + half * 512, 512, pso, Bpso)
                        P.op("scalar", lambda e, pso=pso, half=half: e.copy(out=vf[0:nt, half * 512:(half + 1) * 512],
                                                                           in_=pso[0:nt, :]), r=[Bpso], a=[Bvf])
                        P.op("vector", lambda e, pso=pso, half=half: e.tensor_copy(
                            out=v16t[0:nt, half * 4:(half + 1) * 4, 0:128],
                            in_=pso[0:nt, :].rearrange("p (h d) -> p h d", d=128)), r=[Bpso], a=[Bv16])
                    for half in range(2):
                        pso, Bpso = pmm.next()
                        mm_slab(3 * D + half * 512, 512, pso, Bpso)
                        ee, Be = erot.next()
                        P.op("scalar", lambda e, pso=pso, ee=ee: e.activation(out=ee[0:nt, :], in_=pso[0:nt, :],
                                                                             func=AF.Exp, scale=-1.0),
                             r=[Bpso], w=[Be])
                        P.op("scalar", lambda e, ee=ee: e.activation(out=ee[0:nt, :], in_=ee[0:nt, :], func=AF.Ln,
                                                                    bias=1.0, scale=1.0), r=[Be], w=[Be])
                        P.op("scalar", lambda e, ee=ee: e.activation(out=ee[0:nt, :], in_=ee[0:nt, :], func=AF.Exp,
                                                                    scale=-1.0), r=[Be], w=[Be])
                        P.op("vector", lambda e, pso=pso, ee=ee, half=half: e.tensor_tensor(
                            out=sg16[0:nt, half * 512:(half + 1) * 512], in0=pso[0:nt, :], in1=ee[0:nt, :],
                            op=ALU.mult), r=[Bpso, Be], a=[Bsg16])
                        if half == 0:
                            yield "B1"
                    pso, Bpso = pmm.next()
                    mm_slab(4 * D, H, pso, Bpso)
                    P.op("scalar", lambda e, pso=pso, t=t: e.copy(out=z_all[0:nt, t, :], in_=pso[0:nt, 0:H]),
                         r=[Bpso], a=[B_z])

                    if is_s:
                        P.dma(STQ, lambda e: e.dma_start(out=o_ks[l][:, :], in_=kf[0:NS, :]), r=[Bkf])
                        P.dma(STQ, lambda e: e.dma_start(out=o_vs[l][:, :], in_=vf[0:NS, :]), r=[Bvf],
                              w=[B_vs_out])
                    else:
                        P.dma(STQ, lambda e, t=t: e.dma_start(out=o_kp[l][t * 128:(t + 1) * 128, :], in_=kf[:, :]),
                              r=[Bkf])
                        P.dma(STQ, lambda e, t=t: e.dma_start(out=o_vp[l][t * 128:(t + 1) * 128, :], in_=vf[:, :]),
                              r=[Bvf])
                        P.dma(STQ, lambda e, t=t: e.dma_start(
                            out=v16_s[:, :, t * 129:(t + 1) * 129].rearrange("h p c -> p h c"), in_=v16t[:, :, :]),
                              r=[Bv16], w=[B_v16[t]])
                    yield "B2"
                    if is_s:
                        for k_, (src, Bs_, dst, Bd) in enumerate(((q16, Bq16, s_qT, B_sqT), (k16, Bk16, s_kT, B_skT),
                                                                  (sg16, Bsg16, s_sgT, B_ssgT))):
                            pt, Bpt = ptr.next()
                            transpose_rows(src, Bs_, nt, 8, pt, Bpt, dst[:, :, 0:nt], Bd, "vector")
                    else:
                        sub = t % 2
                        if sub == 0:
                            cur_box[0] = (qTst.next(), kTst.next(), sgTst.next())
                        cur = cur_box[0]
                        for k_, (src, Bs_) in enumerate(((q16, Bq16), (k16, Bk16), (sg16, Bsg16))):
                            pt, Bpt = ptr.next()
                            stt, Bstt = cur[k_]
                            transpose_rows(src, Bs_, nt, 8, pt, Bpt, stt[:, :, sub * 128:(sub + 1) * 128], Bstt,
                                           "scalar" if k_ == 1 else "vector", acc=True)
                        if sub == 1:
                            m = t // 2
                            for k_, dst in enumerate((qT_s, kT_s, sgT_s)):
                                stt, Bstt = cur[k_]
                                P.dma(STQ, lambda e, dst=dst, stt=stt, m=m: e.dma_start(
                                    out=dst[:, :, m * 256:(m + 1) * 256].rearrange("h p c -> p h c"), in_=stt[:, :, :]),
                                      r=[Bstt], a=[B_qT[m]])


                gens = [tile_body(t_, nt_) for (t_, nt_) in tiles]
                ng = len(gens)

                def adv(ti, want):
                    if 0 <= ti < ng:
                        got = next(gens[ti])
                        assert got == want, (got, want)

                def fin_(ti):
                    if 0 <= ti < ng:
                        for _ in gens[ti]:
                            pass
                adv(0, "fa")
                adv(0, "fb")
                adv(1, "fa")
                for ti in range(ng):
                    adv(ti, "A")
                    adv(ti + 2, "fa")
                    fin_(ti - 1)
                    adv(ti, "B1")
                    adv(ti + 1, "fb")
                    adv(ti, "B2")
                fin_(ng - 1)
                nz = NT + (1 if with_sample else 0)
                P.op("vector", lambda e: e.tensor_tensor(out=z_all[:, 0:nz, :], in0=z_all[:, 0:nz, :],
                                                         in1=bft[:, :].unsqueeze(1).to_broadcast([128, nz, H]),
                                                         op=ALU.add), r=[B_z, B_g], w=[B_z])
                P.op("scalar", lambda e: e.activation(out=lf_all[:, 0:nz, :], in_=z_all[:, 0:nz, :], func=AF.Exp,
                                                      scale=-1.0), r=[B_z], w=[B_lf])
                P.op("scalar", lambda e: e.activation(out=lf_all[:, 0:nz, :], in_=lf_all[:, 0:nz, :], func=AF.Ln,
                                                      bias=1.0, scale=1.0), r=[B_lf], w=[B_lf])
                P.op("vector", lambda e: e.tensor_scalar(out=lf_all[:, 0:nz, :], in0=lf_all[:, 0:nz, :],
                                                         scalar1=-1.0, scalar2=None, op0=ALU.mult),
                     r=[B_lf], w=[B_lf])
                nchunk = max(1, NT // 8)
                for c0 in range(0, NT, nchunk):
                    P.dma(STQ, lambda e, c0=c0: e.dma_start(
                        out=o_lfp[l][c0 * 128:(c0 + nchunk) * 128, :].rearrange("(t p) h -> p t h", p=128),
                        in_=lf_all[:, c0:c0 + nchunk, :]), r=[B_lf])
                if with_sample:
                    P.dma(STQ, lambda e: e.dma_start(out=o_lfs[l][:, :], in_=lf_all[0:NS, NT, :]), r=[B_lf])
                    P.op("gpsimd", lambda e: e.tensor_copy(out=s_lf[0:NS, :], in_=lf_all[0:NS, NT, :]),
                         r=[B_lf], w=[B_slf])
                NF = NT * H
                lf_flat = lf_all[:, 0:NT, :].rearrange("p t h -> p (t h)")
                P.op("tensor", lambda e: e.matmul(pz[0:1, 0:NF], lhsT=ones_f[:, 0:1], rhs=lf_flat,
                                                  start=True, stop=True), r=[B_lf], w=[B_pz])
                P.op("vector", lambda e: e.memset(tsum[:, 0, :], 0.0), w=[B_ts])
                P.op("vector", lambda e: e.tensor_copy(out=tsum[:, 1:NT + 1, :].rearrange("p t h -> p (t h)"),
                                                       in_=pz[0:1, 0:NF]), r=[B_pz], a=[B_ts])
                for t in range(1, NT):
                    P.op("vector", lambda e, t=t: e.tensor_tensor(out=tsum[:, t, :], in0=tsum[:, t, :],
                                                                  in1=tsum[:, t - 1, :], op=ALU.add),
                         r=[B_ts], w=[B_ts])
                P.op("tensor", lambda e: e.matmul(pz[:, 0:NF], lhsT=tri[:, :], rhs=lf_flat, start=True, stop=False),
                     r=[B_lf, B_ts], w=[B_pz])
                P.op("tensor", lambda e: e.matmul(pz[:, 0:NF], lhsT=ones_f[0:1, :],
                                                  rhs=tsum[:, 0:NT, :].rearrange("p t h -> p (t h)"),
                                                  start=False, stop=True), r=[B_ts], w=[B_pz])
                P.op("vector", lambda e: e.tensor_copy(out=c_all[:, :, :].rearrange("p t h -> p (t h)"),
                                                       in_=pz[:, 0:NF]), r=[B_pz], w=[B_call])
                P.op("tensor", lambda e: e.matmul(pz[:, 0:NF], lhsT=sel_last[:, :],
                                                  rhs=c_all[:, :, :].rearrange("p t h -> p (t h)"),
                                                  start=True, stop=True), r=[B_call], w=[B_pz])
                P.op("vector", lambda e: e.tensor_copy(out=cref_all[:, :, :].rearrange("p t h -> p (t h)"),
                                                       in_=pz[:, 0:NF]), r=[B_pz], w=[B_cref])
                P.barrier()
                P.flush()

            with ExitStack() as ph:
                def sbp(name, shape, dt):
                    return ph.enter_context(nc.sbuf_tensor(uq(name), list(shape), dt))

                def psp(name, shape, dt=F32):
                    return ph.enter_context(nc.psum_tensor(uq(name), list(shape), dt))

                hd = Rot([(sbp("KT%d" % i, [128, S], BF16), sbp("QT%d" % i, [128, S], BF16),
                           sbp("SG%d" % i, [128, S], BF16), sbp("V%d" % i, [128, NT * 129], BF16)) for i in range(2)])
                cbrot = Rot([sbp("cb%d" % i, [128, NT], F32) for i in range(2)])
                prot = Rot([sbp("p%d" % i, [128, 512], BF16) for i in range(4)])
                pss = Rot([psp("pss%d" % i, [128, 512], F32) for i in range(3)], psum=True)
                pso_r = Rot([psp("pso%d" % i, [128, 2, 256], F32) for i in range(4)], psum=True)
                pst = psp("pst", [128, 512], BF16)
                B_pst = Buf(psum=True)
                rlrot = Rot([sbp("rl%d" % i, [128, 4], F32) for i in range(2)])
                o16rot = Rot([sbp("o16_%d" % i, [128, 512], BF16) for i in range(2)])
                ogrot = Rot([sbp("og%d" % i, [128, 512], BF16) for i in range(2)])

                LA = 2
                blocks = [(h, i, j) for h in range(H) for i in range(NM) for j in range(4 * (i + 1))]
                heads = {}
                qstate = {}
                bstate = {}
                deferred = []

                def load_head(h):
                    (KT, QT, SG, V), Bhd = hd.next()
                    P.dma("sync", lambda e: e.dma_start(out=KT[:, :], in_=kT_s[h]), a=[Bhd])
                    P.dma("sync", lambda e: e.dma_start(out=QT[:, :], in_=qT_s[h]), a=[Bhd])
                    P.dma("sync", lambda e: e.dma_start(out=SG[:, :], in_=sgT_s[h]), a=[Bhd])
                    P.dma("sync", lambda e: e.dma_start(out=V[:, :], in_=v16_s[h]), a=[Bhd])
                    heads[h] = (KT, QT, SG, V, Bhd)

                def stage_a(h, i, j):
                    KT, QT, SG, V, Bhd = heads[h]
                    nk = 4 * (i + 1)
                    if j == 0:
                        cb, Bcb = cbrot.next()
                        P.op("vector", lambda e: e.tensor_scalar(
                            out=cb[:, 0:nk], in0=c_all[:, 0:nk, h], scalar1=-1.0,
                            scalar2=cref_all[:, 4 * i + 3, h:h + 1], op0=ALU.mult, op1=ALU.add), w=[Bcb])
                        qstate[(h, i)] = (cb, Bcb, [pso_r.next(), pso_r.next()])
                    cb, Bcb, po = qstate[(h, i)]
                    m = j - 4 * i
                    c_lo = 128 * m if m > 0 else 0
                    s_ps, Bs_ps = pss.next()
                    P.op("tensor", lambda e: e.matmul(
                        s_ps[:, c_lo:512], lhsT=KT[:, j * 128:(j + 1) * 128],
                        rhs=QT[:, i * 512 + c_lo:(i + 1) * 512], start=True, stop=True), r=[Bhd], w=[Bs_ps])
                    pt_, Bp = prot.next()
                    P.op("scalar", lambda e: e.activation(
                        out=pt_[:, c_lo:512], in_=s_ps[:, c_lo:512], func=AF.Exp, bias=cb[:, j:j + 1],
                        scale=SCALE), r=[Bs_ps, Bcb], w=[Bp])
                    if m >= 0:
                        P.op("gpsimd", lambda e: e.tensor_tensor(
                            out=pt_[:, c_lo:c_lo + 128], in0=pt_[:, c_lo:c_lo + 128], in1=trib[:, :],
                            op=ALU.mult), r=[Bp], w=[Bp])
                    bstate[(h, i, j)] = (pt_, Bp)

                def stage_b(h, i, j, step):
                    KT, QT, SG, V, Bhd = heads[h]
                    cb, Bcb, po = qstate[(h, i)]
                    pt_, Bp = bstate.pop((h, i, j))
                    m = j - 4 * i
                    for c in range(max(m, 0), 4):
                        (pot, Bpo) = po[c // 2]
                        first = (j == 0 and c % 2 == 0)
                        P.op("tensor", lambda e, pot=pot, c=c, first=first: e.matmul(
                            pot[:, c % 2, 0:129], lhsT=pt_[:, c * 128:(c + 1) * 128],
                            rhs=V[:, j * 129:(j + 1) * 129], start=first, stop=(j == 4 * i + c),
                            skip_group_check=True), r=[Bp, Bhd], w=[Bpo] if first else [], a=[] if first else [Bpo])
                    if j != 4 * i + 3:
                        return
                    rl, Brl = rlrot.next()
                    o16, Bo16 = o16rot.next()
                    for c in range(4):
                        (pot, Bpo) = po[c // 2]
                        P.op("vector", lambda e, pot=pot, c=c: e.reciprocal(
                            out=rl[:, c:c + 1], in_=pot[:, c % 2, 128:129]), r=[Bpo], a=[Brl])
                    for c in range(4):
                        (pot, Bpo) = po[c // 2]
                        P.op("vector", lambda e, pot=pot, c=c: e.tensor_scalar(
                            out=o16[:, c * 128:(c + 1) * 128], in0=pot[:, c % 2, 0:128], scalar1=rl[:, c:c + 1],
                            scalar2=None, op0=ALU.mult), r=[Bpo, Brl], a=[Bo16])
                    del qstate[(h, i)]

                    def fin():
                        for c in range(4):
                            P.op("tensor", lambda e, c=c: e.transpose(
                                pst[:, c * 128:(c + 1) * 128], o16[:, c * 128:(c + 1) * 128], ident[:, :]),
                                 r=[Bo16], w=[B_pst] if c == 0 else [], a=[] if c == 0 else [B_pst])
                        og, Bog = ogrot.next()
                        P.op("vector", lambda e: e.tensor_tensor(
                            out=og[:, :], in0=pst[:, :], in1=SG[:, i * 512:(i + 1) * 512], op=ALU.mult),
                             r=[B_pst, Bhd], w=[Bog])
                        P.dma("sync", lambda e: e.dma_start(
                            out=ogT_s[h, :, i * 512:(i + 1) * 512], in_=og[:, :]), r=[Bog], a=[B_ogT[i]])
                    deferred.append((step + 3, fin))

                NB = len(blocks)
                per_head = NB // H
                assert per_head > LA + 5
                load_head(0)
                load_head(1)
                for step in range(NB + LA + 4):
                    if step < NB:
                        stage_a(*blocks[step])
                    if 0 <= step - LA < NB:
                        stage_b(*blocks[step - LA], step)
                    while deferred and deferred[0][0] <= step:
                        deferred.pop(0)[1]()
                    hh, off = divmod(step, per_head)
                    if off == LA + 4 and 1 <= hh and hh + 1 < H:
                        load_head(hh + 1)
                assert not deferred and not bstate
                P.barrier()
                P.flush()

            if with_sample:
                sample_attention(l, B_sqT, B_skT, B_ssgT, B_sogT, B_slf, B_vs_out)

            with ExitStack() as ph:
                def sbp(name, shape, dt):
                    return ph.enter_context(nc.sbuf_tensor(uq(name), list(shape), dt))

                def psp(name, shape, dt=F32):
                    return ph.enter_context(nc.psum_tensor(uq(name), list(shape), dt))

                wo16 = sbp("wo16", [128, 8, D], BF16)
                B_wo = Buf()
                stage = Rot([sbp("wst%d" % i, [128, 1024], F32) for i in range(2)])
                load_weight_bf16(w_out[l], D, D, wo16, B_wo, stage, 1024)
                ogmrot = Rot([sbp("ogm%d" % i, [128, H, 512], BF16) for i in range(2)])
                xrot = Rot([sbp("xr%d" % i, [128, D], F32) for i in range(3)])
                yrot = Rot([sbp("yr%d" % i, [128, D], F32) for i in range(3)])
                pmm = Rot([psp("pmo%d" % i, [128, 512], F32) for i in range(4)], psum=True)

                def outproj_tile(ogm, Bogm, c0, nt, x_t, Bx, y_t, By):
                    for half in range(2):
                        pso, Bpso = pmm.next()
                        for kc in range(8):
                            P.op("tensor", lambda e, kc=kc, pso=pso, half=half: e.matmul(
                                pso[0:nt, :], lhsT=ogm[:, kc, c0:c0 + nt], rhs=wo16[:, kc, half * 512:(half + 1) * 512],
                                start=(kc == 0), stop=(kc == 7)), r=[Bogm, B_wo], w=[Bpso])
                        P.op("vector", lambda e, pso=pso, half=half: e.tensor_tensor(
                            out=y_t[0:nt, half * 512:(half + 1) * 512], in0=pso[0:nt, :],
                            in1=x_t[0:nt, half * 512:(half + 1) * 512], op=ALU.add), r=[Bpso, Bx], a=[By])

                for m in range(NM):
                    ogm, Bogm = ogmrot.next()
                    P.dma("sync", lambda e, ogm=ogm, m=m: e.dma_start(
                        out=ogm[:, :, :], in_=ogT_s[:, :, m * 512:(m + 1) * 512].rearrange("h p c -> p h c")),
                          r=[B_ogT[m]], w=[Bogm])
                    for sub in range(4):
                        t = m * 4 + sub
                        x_t, Bx = xrot.next()
                        P.dma("sync", lambda e, x_t=x_t, t=t: e.dma_start(out=x_t[:, :],
                                                                          in_=src_p[t * 128:(t + 1) * 128, :]),
                              r=[Bsrc_p[t]], w=[Bx])
                        y_t, By = yrot.next()
                        outproj_tile(ogm, Bogm, sub * 128, 128, x_t, Bx, y_t, By)
                        P.dma(STQ, lambda e, y_t=y_t, t=t: e.dma_start(out=dst_p[t * 128:(t + 1) * 128, :],
                                                                          in_=y_t[:, :]), r=[By], w=[Bdst_p[t]])
                if with_sample:
                    x_t, Bx = xrot.next()
                    P.dma("sync", lambda e, x_t=x_t: e.dma_start(out=x_t[0:NS, :], in_=src_s[:, :]),
                          r=[Bsrc_s], w=[Bx])
                    y_t, By = yrot.next()
                    outproj_tile(s_ogT, B_sogT, 0, NS, x_t, Bx, y_t, By)
                    P.dma(STQ, lambda e, y_t=y_t: e.dma_start(out=dst_s[:, :], in_=y_t[0:NS, :]), r=[By],
                          w=[Bdst_s])
                P.barrier()
                P.flush()

        def sample_attention(l, B_sqT, B_skT, B_ssgT, B_sogT, B_slf, B_vs_out):
            with ExitStack() as ph:
                def sbp(name, shape, dt):
                    return ph.enter_context(nc.sbuf_tensor(uq(name), list(shape), dt))

                def psp(name, shape, dt=F32):
                    return ph.enter_context(nc.psum_tensor(uq(name), list(shape), dt))

                CH = 4
                NCH = PT // CH
                kch = Rot([sbp("kch%d" % i, [128, CH, D], F32) for i in range(2)])
                vch = Rot([sbp("vch%d" % i, [128, CH, D], F32) for i in range(2)])
                KTs = sbp("KTs", [128, H, PT * 128], BF16)
                B_KTs = Buf()
                clf_t = sbp("clf_t", [128, PT, H], F32)
                B_clf = Buf()
                ccache = sbp("ccache", [128, PT, H], F32)
                B_cc = Buf()
                cb_c = sbp("cb_c", [128, H, PT], F32)
                B_cbc = Buf()
                pT_all = sbp("pT_all", [128, H, PT * 16], F32)
                B_pT = Buf()
                pT_new = sbp("pT_new", [16, H, 16], F32)
                B_pTn = Buf()
                vnew = sbp("vnew", [16, D], F32)
                B_vnew = Buf()
                lfn = sbp("lfn", [16, H], F32)
                B_lfn = Buf()
                small = sbp("small", [128, 16], F32)
                B_small = Buf()
                cnew = sbp("cnew", [16, H], F32)
                B_cnew = Buf()
                cref_s = sbp("cref_s", [128, H], F32)
                B_crefs = Buf()
                cbn = sbp("cbn", [16, H], F32)
                B_cbn = Buf()
                tmpS = sbp("tmpS", [128, PT * 16], F32)
                B_tmpS = Buf()
                o_s = sbp("o_s", [16, D], BF16)
                B_os = Buf()
                rls = sbp("rls", [16, H], F32)
                B_rls = Buf()
                ptr = Rot([psp("pstr%d" % i, [128, 512], F32) for i in range(2)], psum=True)
                psS = Rot([psp("psS%d" % i, [128, 512], F32) for i in range(2)], psum=True)
                psO = [psp("psO%d" % i, [16, 4, 128], F32) for i in range(2)]
                B_psO = [Buf(psum=True), Buf(psum=True)]
                psM = psp("psM", [16, 512], F32)
                psL = psM[:, 0:H]
                psN = psM[:, 128:256].rearrange("p (h q) -> p h q", q=16)
                B_psL = Buf(psum=True)
                B_psN = B_psL
                psT = psp("psT", [128, 512], BF16)
                B_psT = Buf(psum=True)
                ckv = lambda a, s: a[l][s].rearrange("(p j) f -> p j f", j=PT)

                for s in range(2):
                    P.dma("sync", lambda e, s=s: e.dma_start(out=clf_t[:, :, :],
                                                             in_=clf[l][s].rearrange("(p j) h -> p j h", j=PT)),
                          w=[B_clf])
                    P.op("vector", lambda e: e.tensor_copy(out=ccache[:, 0, :], in_=clf_t[:, 0, :]), r=[B_clf],
                         w=[B_cc])
                    for j in range(1, PT):
                        P.op("vector", lambda e, j=j: e.tensor_tensor(out=ccache[:, j, :], in0=ccache[:, j - 1, :],
                                                                      in1=clf_t[:, j, :], op=ALU.add),
                             r=[B_clf, B_cc], w=[B_cc])
                    ps1, Bps1 = psS.next()
                    P.op("tensor", lambda e, ps1=ps1: e.matmul(ps1[:, 0:H], lhsT=trix[:, :], rhs=ccache[:, PT - 1, :],
                                                              start=True, stop=True), r=[B_cc], w=[Bps1])
                    P.op("vector", lambda e, ps1=ps1: e.tensor_copy(out=small[:, 0:H], in_=ps1[:, 0:H]),
                         r=[Bps1], w=[B_small])
                    P.op("vector", lambda e: e.tensor_tensor(
                        out=ccache[:, :, :], in0=ccache[:, :, :],
                        in1=small[:, 0:H].unsqueeze(1).to_broadcast([128, PT, H]), op=ALU.add),
                         r=[B_small, B_cc], w=[B_cc])
                    P.dma("sync", lambda e, s=s: e.dma_start(out=lfn[:, :], in_=s_lf[s * 16:(s + 1) * 16, :]),
                          r=[B_slf], w=[B_lfn])
                    ps2, Bps2 = psS.next()
                    P.op("tensor", lambda e, ps2=ps2: e.matmul(ps2[0:16, 0:H], lhsT=sel_last[:, 0:16],
                                                              rhs=ccache[:, PT - 1, :], start=True, stop=False),
                         r=[B_cc], w=[Bps2])
                    P.op("tensor", lambda e, ps2=ps2: e.matmul(ps2[0:16, 0:H], lhsT=tri[0:16, 0:16], rhs=lfn[:, :],
                                                              start=False, stop=True), r=[B_lfn], a=[Bps2])
                    P.op("vector", lambda e, ps2=ps2: e.tensor_copy(out=cnew[:, :], in_=ps2[0:16, 0:H]),
                         r=[Bps2], w=[B_cnew])
                    P.op("vector", lambda e: e.tensor_scalar(out=small[0:16, 8:8 + H], in0=cnew[:, :],
                                                             scalar1=sel15[0:16, 0:1], scalar2=None, op0=ALU.mult),
                         r=[B_cnew], w=[B_small])
                    ps3, Bps3 = psS.next()
                    P.op("tensor", lambda e, ps3=ps3: e.matmul(ps3[:, 0:H], lhsT=ones_f[0:16, :],
                                                              rhs=small[0:16, 8:8 + H], start=True, stop=True),
                         r=[B_small], w=[Bps3])
                    P.op("vector", lambda e, ps3=ps3: e.tensor_copy(out=cref_s[:, :], in_=ps3[:, 0:H]),
                         r=[Bps3], w=[B_crefs])
                    P.op("vector", lambda e: e.tensor_tensor(
                        out=cb_c[:, :, :], in0=cref_s[:, :].unsqueeze(2).to_broadcast([128, H, PT]),
                        in1=ccache[:, :, :].rearrange("p j h -> p h j"), op=ALU.subtract),
                         r=[B_crefs, B_cc], w=[B_cbc])
                    P.op("vector", lambda e: e.tensor_tensor(out=cbn[:, :], in0=cref_s[0:16, :], in1=cnew[:, :],
                                                             op=ALU.subtract), r=[B_crefs, B_cnew], w=[B_cbn])
                    for ch in range(NCH):
                        kc_t, Bkc = kch.next()
                        P.dma("sync", lambda e, kc_t=kc_t, ch=ch, s=s: e.dma_start(
                            out=kc_t[:, :, :], in_=ckv(ck, s)[:, ch * CH:(ch + 1) * CH, :]), w=[Bkc])
                        for jj in range(CH):
                            j = ch * CH + jj
                            for hg in range(2):
                                pt, Bpt = ptr.next()
                                for hh in range(4):
                                    h = hg * 4 + hh
                                    P.op("tensor", lambda e, pt=pt, hh=hh, h=h, jj=jj, kc_t=kc_t: e.transpose(
                                        pt[:, hh * 128:(hh + 1) * 128], kc_t[:, jj, h * 128:(h + 1) * 128],
                                        identf[:, :]), r=[Bkc], w=[Bpt] if hh == 0 else [],
                                         a=[] if hh == 0 else [Bpt])
                                eng = "vector" if (j + hg) % 2 == 0 else "scalar"
                                dst = KTs[:, hg * 4:(hg + 1) * 4, j * 128:(j + 1) * 128]
                                srcp = pt[:, :].rearrange("p (h k) -> p h k", k=128)
                                if eng == "vector":
                                    P.op("vector", lambda e, dst=dst, srcp=srcp: e.tensor_copy(out=dst, in_=srcp),
                                         r=[Bpt], a=[B_KTs])
                                else:
                                    P.op("scalar", lambda e, dst=dst, srcp=srcp: e.copy(out=dst, in_=srcp),
                                         r=[Bpt], a=[B_KTs])
                    P.dma("sync", lambda e, s=s: e.dma_start(out=vnew[:, :], in_=o_vs[l][s * 16:(s + 1) * 16, :]),
                          r=[B_vs_out], w=[B_vnew])
                    for h in range(H):
                        pS, BpS = psS.next()
                        for j in range(PT):
                            P.op("tensor", lambda e, pS=pS, j=j, h=h, s=s: e.matmul(
                                pS[:, j * 16:(j + 1) * 16], lhsT=KTs[:, h, j * 128:(j + 1) * 128],
                                rhs=s_qT[:, h, s * 16:(s + 1) * 16], start=True, stop=True),
                                 r=[B_KTs, B_sqT], w=[BpS] if j == 0 else [], a=[] if j == 0 else [BpS])
                        P.op("vector", lambda e, pS=pS, h=h: e.scalar_tensor_tensor(
                            out=tmpS[:, :].rearrange("p (j q) -> p j q", q=16),
                            in0=pS[:, 0:PT * 16].rearrange("p (j q) -> p j q", q=16), scalar=SCALE,
                            in1=cb_c[:, h, :].unsqueeze(2).to_broadcast([128, PT, 16]), op0=ALU.mult, op1=ALU.add),
                             r=[BpS, B_cbc], w=[B_tmpS])
                        P.op("scalar", lambda e, h=h: e.activation(out=pT_all[:, h, :], in_=tmpS[:, :], func=AF.Exp),
                             r=[B_tmpS], a=[B_pT])
                        P.op("tensor", lambda e, h=h, s=s: e.matmul(
                            psN[:, h, :], lhsT=s_kT[:, h, s * 16:(s + 1) * 16], rhs=s_qT[:, h, s * 16:(s + 1) * 16],
                            start=True, stop=True), r=[B_skT, B_sqT], a=[B_psN])
                    for h in range(H):
                        P.op("scalar", lambda e, h=h: e.activation(out=pT_new[:, h, :], in_=psN[:, h, :], func=AF.Exp,
                                                                   bias=cbn[:, h:h + 1], scale=SCALE),
                             r=[B_psN, B_cbn], a=[B_pTn])
                    P.op("vector", lambda e: e.tensor_tensor(
                        out=pT_new[:, :, :], in0=pT_new[:, :, :],
                        in1=tri[0:16, 0:16].unsqueeze(1).to_broadcast([16, H, 16]), op=ALU.mult),
                         r=[B_pTn], w=[B_pTn])
                    for h in range(H):
                        for j in range(PT):
                            P.op("tensor", lambda e, h=h, j=j: e.matmul(
                                psL[:, h:h + 1], lhsT=pT_all[:, h, j * 16:(j + 1) * 16], rhs=ones_f[:, 0:1],
                                start=(j == 0 and h == 0), stop=False, skip_group_check=True), r=[B_pT],
                                     w=[B_psL] if (j == 0 and h == 0) else [], a=[] if (j == 0 and h == 0) else [B_psL])
                        P.op("tensor", lambda e, h=h: e.matmul(psL[:, h:h + 1], lhsT=pT_new[:, h, :],
                                                                rhs=ones_f[0:16, 0:1], start=False, stop=True,
                                                                skip_group_check=True),
                             r=[B_pTn], a=[B_psL])
                    for ch in range(NCH):
                        vc_t, Bvc = vch.next()
                        P.dma("sync", lambda e, vc_t=vc_t, ch=ch, s=s: e.dma_start(
                            out=vc_t[:, :, :], in_=ckv(cv, s)[:, ch * CH:(ch + 1) * CH, :]), w=[Bvc])
                        for jj in range(CH):
                            j = ch * CH + jj
                            for h in range(H):
                                P.op("tensor", lambda e, h=h, j=j, jj=jj, vc_t=vc_t: e.matmul(
                                    psO[h // 4][:, h % 4, :], lhsT=pT_all[:, h, j * 16:(j + 1) * 16],
                                    rhs=vc_t[:, jj, h * 128:(h + 1) * 128], start=(j == 0 and h % 4 == 0), stop=False,
                                    skip_group_check=True),
                                     r=[B_pT, Bvc], a=[B_psO[h // 4]])
                    for h in range(H):
                        P.op("tensor", lambda e, h=h: e.matmul(
                            psO[h // 4][:, h % 4, :], lhsT=pT_new[:, h, :], rhs=vnew[:, h * 128:(h + 1) * 128],
                            start=False, stop=True, skip_group_check=True), r=[B_pTn, B_vnew], a=[B_psO[h // 4]])
                    P.op("vector", lambda e: e.reciprocal(out=rls[:, :], in_=psL), r=[B_psL], w=[B_rls])
                    for hg in range(2):
                        P.op("vector", lambda e, hg=hg: e.tensor_tensor(
                            out=o_s[:, hg * 512:(hg + 1) * 512].rearrange("p (h d) -> p h d", d=128),
                            in0=psO[hg][:, :, :],
                            in1=rls[:, hg * 4:(hg + 1) * 4].unsqueeze(2).to_broadcast([16, 4, 128]), op=ALU.mult),
                             r=[B_psO[hg], B_rls], a=[B_os])
                    if DEBUG and l == 0:
                        for nm, t_, B_ in (("d_cnew", cnew[:, :], B_cnew), ("d_cref", cref_s[:, :], B_crefs),
                                           ("d_rls", rls[:, :], B_rls),
                                           ("d_pTn", pT_new[:, :, :].rearrange("p h q -> p (h q)"), B_pTn),
                                           ("d_cc", ccache[:, :, :].rearrange("p j h -> p (j h)"), B_cc),
                                           ("d_lfn", lfn[:, :], B_lfn), ("d_cbn", cbn[:, :], B_cbn),
                                           ("d_os", o_s[:, :], B_os),
                                           ("d_pT", pT_all[:, :, :].rearrange("p h q -> p (h q)"), B_pT)):
                            P.dma("sync", lambda e, nm=nm, t_=t_, s=s: e.dma_start(out=DBG[nm][s], in_=t_), r=[B_])
                    for hg in range(2):
                        for hh in range(4):
                            h = hg * 4 + hh
                            P.op("tensor", lambda e, hh=hh, h=h: e.transpose(
                                psT[:, hh * 128:hh * 128 + 16], o_s[:, h * 128:(h + 1) * 128], ident[0:16, 0:16]),
                                 r=[B_os], w=[B_psT] if hh == 0 else [], a=[] if hh == 0 else [B_psT])
                        P.op("vector", lambda e, hg=hg, s=s: e.tensor_tensor(
                            out=s_ogT[:, hg * 4:(hg + 1) * 4, s * 16:(s + 1) * 16],
                            in0=psT[:, :].rearrange("p (h k) -> p h k", k=128)[:, :, 0:16],
                            in1=s_sgT[:, hg * 4:(hg + 1) * 4, s * 16:(s + 1) * 16], op=ALU.mult),
                             r=[B_psT, B_ssgT], a=[B_sogT])
                P.barrier()
                P.flush()

        def conv_layer(l, src_p, Bsrc_p, src_s, Bsrc_s, dst_p, Bdst_p, dst_s, Bdst_s):
            with ExitStack() as ph:
                def sbp(name, shape, dt):
                    return ph.enter_context(nc.sbuf_tensor(uq(name), list(shape), dt))

                def psp(name, shape, dt=F32):
                    return ph.enter_context(nc.psum_tensor(uq(name), list(shape), dt))

                w16 = sbp("cw16", [128, 8, 4 * D], BF16)
                B_w = Buf()
                wo16 = sbp("cwo16", [128, 8, D], BF16)
                B_wo = Buf()
                stage = Rot([sbp("cwst%d" % i, [128, 256], F32) for i in range(2)])
                gtile = sbp("cgtile", [128, D], F32)
                cwt = sbp("cwt", [128, 8, 3], F32)
                B_g = Buf()
                P.dma("sync", lambda e: e.dma_start(out=gtile[:], in_=g_norm[l].partition_broadcast(128)), a=[B_g])
                for j in range(3):
                    P.dma("sync", lambda e, j=j: e.dma_start(out=cwt[:, :, j],
                                                             in_=conv_w[l][j].rearrange("(c p) -> p c", p=128),
                                                             allow_slow_non_contiguous=True), a=[B_g])
                load_weight_bf16(w_in[l], D, 4 * D, w16, B_w, stage, 256)
                load_weight_bf16(w_out[l], D, D, wo16, B_wo, stage, 256)

                xm_rot = Rot([sbp("cxm%d" % i, [128, 4, D], F32) for i in range(2)])
                junk = sbp("cjunk", [128, D], BF16)
                Bjunk = Buf()
                strot = Rot([sbp("cst%d" % i, [128, 4], F32) for i in range(2)])
                hrot = Rot([sbp("ch%d" % i, [128, D], BF16) for i in range(2)])
                hTrot = Rot([sbp("chT%d" % i, [128, 8, 512], BF16) for i in range(2)])
                cu = sbp("cu", [128, 8, 2 + 512], F32)
                B_cu = [Buf() for _ in range(8)]
                cu_s = sbp("cu_s", [128, 8, 2, 2 + 16], F32)
                B_cus = Buf()
                c_sb_rot = Rot([sbp("c_sb%d" % i, [128, 512], F32) for i in range(2)])
                cv_rot = Rot([sbp("cvv%d" % i, [128, 512], F32) for i in range(2)])
                e_rot = Rot([sbp("ce%d" % i, [128, 512], F32) for i in range(2)])
                b_rot = Rot([sbp("cb_sb%d" % i, [128, 512], F32) for i in range(2)])
                ogT_rot = Rot([sbp("cogT%d" % i, [128, 8, 512], BF16) for i in range(2)])
                y_rot = Rot([sbp("cy%d" % i, [128, D], F32) for i in range(2)])
                pmm = Rot([psp("cpm%d" % i, [128, 512], F32) for i in range(6)], psum=True)
                ptr = Rot([psp("cpt%d" % i, [128, 1024], BF16) for i in range(2)], psum=True)

                for cc in range(8):
                    P.op("gpsimd", lambda e, cc=cc: e.memset(cu[:, cc, 0:2], 0.0), w=[B_cu[cc]])

                def macro(ntok, nsub, segs, cu_t, B_cu_l, is_s, m):
                    xm, Bxm = xm_rot.next()
                    if is_s:
                        P.dma("sync", lambda e: e.dma_start(out=xm[0:NS, 0, :], in_=src_s[:, :]), r=[Bsrc_s], w=[Bxm])
                    else:
                        P.dma("sync", lambda e: e.dma_start(
                            out=xm[:, :, :],
                            in_=src_p[m * 512:(m + 1) * 512, :].rearrange("(s p) f -> p s f", p=128)),
                              r=[Bsrc_p[m * 4 + k] for k in range(4)], w=[Bxm])
                    hT, BhT = hTrot.next()
                    for sub in range(nsub):
                        nt = min(128, ntok - sub * 128)
                        st_s, Bst = strot.next()
                        h16, Bh = hrot.next()
                        rms_tile(xm[:, sub, :], Bxm, nt, gtile, B_g, h16, Bh, junk, Bjunk, st_s, Bst)
                        pt, Bpt = ptr.next()
                        transpose_rows(h16, Bh, nt, 8, pt, Bpt, hT[:, :, sub * 128:sub * 128 + nt], BhT, "vector",
                                       acc=True)
                    yield "front"
                    ogT, BogT = ogT_rot.next()
                    for cc in range(8):
                        def mm_feat(f, pso, Bpso):
                            for kc in range(8):
                                P.op("tensor", lambda e, kc=kc: e.matmul(
                                    pso[:, 0:ntok], lhsT=w16[:, kc, f * 128:(f + 1) * 128], rhs=hT[:, kc, 0:ntok],
                                    start=(kc == 0), stop=(kc == 7)), r=[BhT, B_w], w=[Bpso])
                        pc, Bpc = pmm.next()
                        mm_feat(8 + cc, pc, Bpc)
                        pu, Bpu = pmm.next()
                        mm_feat(16 + cc, pu, Bpu)
                        pb, Bpb = pmm.next()
                        mm_feat(cc, pb, Bpb)
                        pg, Bpg = pmm.next()
                        mm_feat(24 + cc, pg, Bpg)
                        c_sb, Bc_sb = c_sb_rot.next()
                        b_sb, Bb_sb = b_rot.next()
                        ee, Be = e_rot.next()
                        cvv, Bcv = cv_rot.next()
                        P.op("scalar", lambda e, pc=pc, c_sb=c_sb: e.copy(out=c_sb[:, 0:ntok], in_=pc[:, 0:ntok]),
                             r=[Bpc], w=[Bc_sb])
                        P.op("scalar", lambda e, pb=pb, b_sb=b_sb: e.copy(out=b_sb[:, 0:ntok], in_=pb[:, 0:ntok]),
                             r=[Bpb], w=[Bb_sb])
                        P.op("scalar", lambda e, pg=pg, ee=ee: e.activation(out=ee[:, 0:ntok], in_=pg[:, 0:ntok],
                                                                           func=AF.Exp, scale=-1.0), r=[Bpg], w=[Be])
                        P.op("scalar", lambda e, ee=ee: e.activation(out=ee[:, 0:ntok], in_=ee[:, 0:ntok], func=AF.Ln,
                                                                    bias=1.0, scale=1.0), r=[Be], w=[Be])
                        P.op("scalar", lambda e, ee=ee: e.activation(out=ee[:, 0:ntok], in_=ee[:, 0:ntok], func=AF.Exp,
                                                                    scale=-1.0), r=[Be], w=[Be])
                        for (c0, ln, seg_i) in segs:
                            base = (lambda o, seg_i=seg_i, ln=ln, cc=cc: cu_t[:, cc, seg_i, o:o + ln]) if is_s else \
                                   (lambda o, ln=ln, cc=cc: cu_t[:, cc, o:o + ln])
                            P.op("vector", lambda e, pu=pu, c_sb=c_sb, c0=c0, ln=ln, base=base: e.tensor_tensor(
                                out=base(2), in0=pu[:, c0:c0 + ln], in1=c_sb[:, c0:c0 + ln], op=ALU.mult),
                                 r=[Bpu, Bc_sb], a=[B_cu_l[cc]])
                            P.op("vector", lambda e, cvv=cvv, base=base, c0=c0, ln=ln, cc=cc: e.tensor_scalar(
                                out=cvv[:, c0:c0 + ln], in0=base(2), scalar1=cwt[:, cc, 2:3], scalar2=None,
                                op0=ALU.mult), r=[B_cu_l[cc], B_g], a=[Bcv])
                            P.op("vector", lambda e, cvv=cvv, base=base, c0=c0, ln=ln, cc=cc: e.scalar_tensor_tensor(
                                out=cvv[:, c0:c0 + ln], in0=base(1), scalar=cwt[:, cc, 1:2], in1=cvv[:, c0:c0 + ln],
                                op0=ALU.mult, op1=ALU.add), r=[B_cu_l[cc], B_g, Bcv], a=[Bcv])
                            P.op("vector", lambda e, cvv=cvv, base=base, c0=c0, ln=ln, cc=cc: e.scalar_tensor_tensor(
                                out=cvv[:, c0:c0 + ln], in0=base(0), scalar=cwt[:, cc, 0:1], in1=cvv[:, c0:c0 + ln],
                                op0=ALU.mult, op1=ALU.add), r=[B_cu_l[cc], B_g, Bcv], a=[Bcv])
                        if not is_s:
                            P.op("gpsimd", lambda e, cc=cc: e.tensor_copy(out=cu_t[:, cc, 0:2],
                                                                          in_=cu_t[:, cc, 512:514]),
                                 r=[B_cu_l[cc]], w=[B_cu_l[cc]])
                        P.op("vector", lambda e, ee=ee, pg=pg: e.tensor_tensor(out=ee[:, 0:ntok], in0=pg[:, 0:ntok],
                                                                               in1=ee[:, 0:ntok], op=ALU.mult),
                             r=[Be, Bpg], w=[Be])
                        P.op("vector", lambda e, ee=ee, cvv=cvv: e.tensor_tensor(out=ee[:, 0:ntok], in0=ee[:, 0:ntok],
                                                                                 in1=cvv[:, 0:ntok], op=ALU.mult),
                             r=[Be, Bcv], w=[Be])
                        P.op("vector", lambda e, ee=ee, b_sb=b_sb, cc=cc: e.tensor_tensor(
                            out=ogT[:, cc, 0:ntok], in0=b_sb[:, 0:ntok], in1=ee[:, 0:ntok], op=ALU.mult),
                             r=[Be, Bb_sb], a=[BogT])
                    for sub in range(nsub):
                        nt = min(128, ntok - sub * 128)
                        y_t, By = y_rot.next()
                        for half in range(2):
                            pso, Bpso = pmm.next()
                            for kc in range(8):
                                P.op("tensor", lambda e, kc=kc, pso=pso, half=half, sub=sub, nt=nt: e.matmul(
                                    pso[0:nt, :], lhsT=ogT[:, kc, sub * 128:sub * 128 + nt],
                                    rhs=wo16[:, kc, half * 512:(half + 1) * 512], start=(kc == 0), stop=(kc == 7)),
                                     r=[BogT, B_wo], w=[Bpso])
                            P.op("vector", lambda e, pso=pso, half=half, sub=sub, nt=nt, y_t=y_t: e.tensor_tensor(
                                out=y_t[0:nt, half * 512:(half + 1) * 512], in0=pso[0:nt, :],
                                in1=xm[0:nt, sub, half * 512:(half + 1) * 512], op=ALU.add),
                                 r=[Bpso, Bxm], a=[By])
                        if is_s:
                            P.dma(STQ, lambda e, y_t=y_t: e.dma_start(out=dst_s[:, :], in_=y_t[0:NS, :]),
                                  r=[By], w=[Bdst_s])
                        else:
                            t = m * 4 + sub
                            P.dma(STQ, lambda e, y_t=y_t, t=t: e.dma_start(out=dst_p[t * 128:(t + 1) * 128, :],
                                                                              in_=y_t[:, :]), r=[By], w=[Bdst_p[t]])

                mg = [macro(512, 4, [(0, 512, 0)], cu, B_cu, False, m) for m in range(NM)]
                if with_sample:
                    def sample_pre():
                        for s_ in range(2):
                            for j in range(2):
                                P.dma("sync", lambda e, s_=s_, j=j: e.dma_start(
                                    out=cu_s[:, :, s_, j], in_=sconv[l][s_, j].rearrange("(c p) -> p c", p=128),
                                    allow_slow_non_contiguous=True), a=[B_cus])
                    sample_pre()
                    mg.append(macro(NS, 1, [(0, 16, 0), (16, 16, 1)], cu_s, [B_cus] * 8, True, 0))
                next(mg[0])
                for mi in range(len(mg)):
                    if mi + 1 < len(mg):
                        next(mg[mi + 1])
                    for _ in mg[mi]:
                        pass
                    if mi == NM - 1:
                        for j in range(2):
                            P.dma(STQ, lambda e, j=j: e.dma_start(out=o_cp[l][j].rearrange("(c p) -> p c", p=128),
                                                                     in_=cu[:, :, j], allow_slow_non_contiguous=True),
                                  r=B_cu)
                if with_sample:
                    for s_ in range(2):
                        for j in range(2):
                            P.dma(STQ, lambda e, s_=s_, j=j: e.dma_start(
                                out=o_cs[l][s_, j].rearrange("(c p) -> p c", p=128), in_=cu_s[:, :, s_, 16 + j],
                                allow_slow_non_contiguous=True), r=[B_cus])
                P.barrier()
                P.flush()

        B_y = [[Buf() for _ in range(NT)] for _ in range(2)]
        B_ys = [Buf(), Buf()]
        srcs_p = [xp, y_sc[0], y_sc[1], y_sc[0]]
        Bsrcs_p = [[Buf() for _ in range(NT)], B_y[0], B_y[1], B_y[0]]
        dsts_p = [y_sc[0], y_sc[1], y_sc[0], o_yp]
        Bdsts_p = [B_y[0], B_y[1], B_y[0], None]
        srcs_s = [xs, ys_sc[0], ys_sc[1], ys_sc[0]]
        Bsrcs_s = [Buf(), B_ys[0], B_ys[1], B_ys[0]]
        dsts_s = [ys_sc[0], ys_sc[1], ys_sc[0], o_ys]
        Bdsts_s = [B_ys[0], B_ys[1], B_ys[0], None]
        for l in range(NL):
            last = l == NL - 1
            dp = o_yp if last else dsts_p[l]
            ds = o_ys if last else dsts_s[l]
            Bdp = [Buf() for _ in range(NT)] if last else Bdsts_p[l]
            Bds = Buf() if last else Bdsts_s[l]
            if l % 2 == 0:
                fox_layer(l, srcs_p[l], Bsrcs_p[l], srcs_s[l], Bsrcs_s[l], dp, Bdp, ds, Bds)
            else:
                conv_layer(l, srcs_p[l], Bsrcs_p[l], srcs_s[l], Bsrcs_s[l], dp, Bdp, ds, Bds)
    build.last_nsem = P.nsem
    return nc


_CACHE = {}


def run_cores(S, PAST, NL, in_maps, with_sample=True):
    key = (S, PAST, NL, with_sample)
    nc = build(S, PAST, NL, with_sample)
    res = run_bass_kernel_spmd(nc, in_maps, core_ids=list(range(len(in_maps))))
    return res.results


def make_in_map(inp, core, S, PAST):
    b = core // 2
    f = lambda a: np.ascontiguousarray(a, dtype=np.float32)
    sl = slice(2 * core, 2 * core + 2)
    m = {"xp": f(inp["x_prompt"][b]), "xs": f(inp["x_sample"][sl].reshape(32, D))}
    m["norm0"], m["w_in0"], m["w_out0"] = f(inp["norm_l0"]), f(inp["w_in_l0"]), f(inp["w_out_l0"])
    m["b_f0"], m["qn0"], m["kn0"] = f(inp["b_f_l0"]), f(inp["qnorm_l0"]), f(inp["knorm_l0"])
    m["ck0"] = f(inp["cache_k_l0"][sl].reshape(2, PAST, D))
    m["cv0"] = f(inp["cache_v_l0"][sl].reshape(2, PAST, D))
    m["clf0"] = f(inp["cache_logf_l0"][sl])
    m["norm2"], m["w_in2"], m["w_out2"] = f(inp["norm_l2"]), f(inp["w_in_l2"]), f(inp["w_out_l2"])
    m["b_f2"], m["qn2"], m["kn2"] = f(inp["b_f_l2"]), f(inp["qnorm_l2"]), f(inp["knorm_l2"])
    m["ck2"] = f(inp["cache_k_l2"][sl].reshape(2, PAST, D))
    m["cv2"] = f(inp["cache_v_l2"][sl].reshape(2, PAST, D))
    m["clf2"] = f(inp["cache_logf_l2"][sl])
    m["norm1"], m["w_in1"], m["w_out1"] = f(inp["norm_l1"]), f(inp["w_in_l1"]), f(inp["w_out_l1"])
    m["cw1"], m["sc1"] = f(inp["conv_w_l1"]), f(inp["state_conv_l1"][sl])
    m["norm3"], m["w_in3"], m["w_out3"] = f(inp["norm_l3"]), f(inp["w_in_l3"]), f(inp["w_out_l3"])
    m["cw3"], m["sc3"] = f(inp["conv_w_l3"]), f(inp["state_conv_l3"][sl])
    return m


def assemble(results, B, S):
    nb = B
    DB = 2 * len(results)
    yp = np.stack([results[2 * b]["o_yp"] for b in range(nb)])
    ys = np.concatenate([r["o_ys"].reshape(2, 16, D) for r in results])
    outs = [yp, ys]
    for l in range(4):
        if l % 2 == 0:
            outs.append(np.stack([results[2 * b]["o_kp%d" % l].reshape(S, H, DH) for b in range(nb)]))
            outs.append(np.stack([results[2 * b]["o_vp%d" % l].reshape(S, H, DH) for b in range(nb)]))
            outs.append(np.stack([results[2 * b]["o_lfp%d" % l] for b in range(nb)]))
            outs.append(np.concatenate([r["o_ks%d" % l].reshape(2, 16, H, DH) for r in results]))
            outs.append(np.concatenate([r["o_vs%d" % l].reshape(2, 16, H, DH) for r in results]))
            outs.append(np.concatenate([r["o_lfs%d" % l].reshape(2, 16, H) for r in results]))
        else:
            outs.append(np.stack([results[2 * b]["o_cp%d" % l] for b in range(nb)]))
            outs.append(np.concatenate([r["o_cs%d" % l] for r in results]))
    return tuple(np.ascontiguousarray(o, dtype=np.float32) for o in outs)


def kernel(**inputs):
    inp = {k: np.asarray(v) for k, v in inputs.items()}
    B, S, _ = inp["x_prompt"].shape
    PAST = inp["cache_k_l0"].shape[1]
    in_maps = [make_in_map(inp, c, S, PAST) for c in range(8)]
    results = run_cores(S, PAST, 4, in_maps)
    return assemble(results, B, S)
```

```python
import numpy as np
from contextlib import ExitStack
import concourse.bass as bass
import concourse.mybir as mybir
from concourse.bass_utils import run_bass_kernel_spmd

F32 = mybir.dt.float32
BF16 = mybir.dt.bfloat16
AF = mybir.ActivationFunctionType
ALU = mybir.AluOpType
AX = mybir.AxisListType

D = 1024
H = 8
DH = 128
EPS = 1e-6
SCALE = DH ** -0.5
ENGS = ("sync", "scalar", "vector", "gpsimd", "tensor")
DEBUG = False
STQ = "gpsimd"
SEM_LIMIT = 8000
DMA_LIMIT = 1500


class Tok:
    __slots__ = ("eng", "sem", "val", "needed", "dma", "phase")

    def __init__(self, eng, phase, dma=False):
        self.eng = eng
        self.sem = None
        self.val = None
        self.needed = dma
        self.dma = dma
        self.phase = phase


class Buf:
    __slots__ = ("name", "w", "r", "rp", "psum")

    def __init__(self, name="", psum=False):
        self.name = name
        self.w = []
        self.r = []
        self.rp = []
        self.psum = psum


class Prog:
    def __init__(self, nc, es):
        self.nc = nc
        self.es = es
        self.nsem = 0
        self.phase = 0
        self.cur = {}
        self.waited = {e: {} for e in ENGS}
        self.ring = []
        self.ring_i = 0
        self.ops = {e: [] for e in ENGS}
        self.last = {e: None for e in ENGS}

    def _newsem(self):
        self.nsem += 1
        return self.es.enter_context(self.nc.semaphore("s%d" % self.nsem))

    def _deps(self, r, w, a):
        deps = []
        raw = []
        for b in r:
            deps.extend(b.w)
            raw.extend(b.w)
            if b.psum:
                deps.extend(b.r)
        for b in w:
            deps.extend(b.w)
            deps.extend(b.r)
        for b in a:
            if b.r:
                b.rp = b.r
                b.r = []
                b.w = []
            deps.extend(b.rp)
        return deps, raw

    def _commit(self, tok, r, w, a):
        for b in r:
            b.r.append(tok)
        for b in w:
            b.w = [tok]
            b.r = []
            b.rp = []
        for b in a:
            b.w.append(tok)

    def op(self, eng, fn, r=(), w=(), a=()):
        deps, raw = self._deps(r, w, a)
        tok = Tok(eng, self.phase)
        waits = []
        seen = set()
        for d in deps:
            if id(d) in seen or d.phase != self.phase:
                continue
            seen.add(id(d))
            if d.eng == eng and not d.dma:
                if eng == "tensor":
                    continue
                if not any(d is x for x in raw):
                    continue
            d.needed = True
            waits.append(d)
        self.ops[eng].append((fn, waits, tok))
        self._commit(tok, r, w, a)
        self.last[eng] = tok
        return tok

    def dma(self, eng, fn, r=(), w=(), a=()):
        deps, _ = self._deps(r, w, a)
        tok = Tok(eng, self.phase, dma=True)
        if len(self.ring) < 14:
            self.ring.append([self._newsem(), 0, None])
            slot = self.ring[-1]
        else:
            slot = self.ring[self.ring_i % len(self.ring)]
            self.ring_i += 1
        if slot[1] >= DMA_LIMIT:
            old = slot[2]
            slot[0] = self._newsem()
            slot[1] = 0
            if old is not None:
                deps.append(old)
            slot[2] = None
        if slot[2] is not None:
            deps.append(slot[2])
        slot[1] += 1
        tok.sem = slot[0]
        tok.val = 16 * slot[1]
        slot[2] = tok
        waits = []
        seen = set()
        for d in deps:
            if id(d) in seen or d.phase != self.phase:
                continue
            seen.add(id(d))
            d.needed = True
            waits.append(d)
        self.ops[eng].append((fn, waits, tok))
        self._commit(tok, r, w, a)
        return tok

    def barrier(self):
        toks = [t for t in self.last.values() if t is not None and t.phase == self.phase]
        toks += [sl[2] for sl in self.ring if sl[2] is not None and sl[2].phase == self.phase]
        for t in toks:
            t.needed = True
        for e in ENGS:
            self.ops[e].append((None, list(toks), None))

    def flush(self):
        for e in ENGS:
            for fn, waits, tok in self.ops[e]:
                if tok is None or tok.dma or not tok.needed:
                    continue
                cur = self.cur.get(e)
                if cur is None or cur[1] >= SEM_LIMIT:
                    cur = [self._newsem(), 0]
                    self.cur[e] = cur
                cur[1] += 1
                tok.sem = cur[0]
                tok.val = cur[1]
        ops = self.ops
        waited = self.waited

        def run(ename):
            def body(eng):
                wd = waited[ename]
                for fn, waits, tok in ops[ename]:
                    best = {}
                    for d in waits:
                        key = id(d.sem)
                        if key not in best or d.val > best[key].val:
                            best[key] = d
                    for key, d in best.items():
                        if wd.get(key, 0) >= d.val:
                            continue
                        eng.wait_ge(d.sem, d.val)
                        wd[key] = d.val
                    if fn is None:
                        continue
                    ins = fn(eng)
                    if tok is not None and tok.needed:
                        ins.then_inc(tok.sem, 16 if tok.dma else 1)
            return body

        with self.nc.Block() as blk:
            blk.sync(run("sync"))
            blk.scalar(run("scalar"))
            blk.vector(run("vector"))
            blk.gpsimd(run("gpsimd"))
            blk.tensor(run("tensor"))
        self.ops = {e: [] for e in ENGS}
        self.phase += 1


class Rot:
    def __init__(self, tiles, psum=False):
        self.tiles = tiles
        self.bufs = [Buf(psum=psum) for _ in tiles]
        self.i = 0

    def next(self):
        k = self.i % len(self.tiles)
        self.i += 1
        return self.tiles[k], self.bufs[k]


def build(S, PAST, NL=4, with_sample=True):
    NT = S // 128
    NM = S // 512
    NQ = S // 256
    PT = PAST // 128
    NS = 32
    WIN_F = 4 * D + H
    assert PT * 16 <= 512 and PT % 4 == 0

    nc = bass.Bass("TRN2", target_bir_lowering=False)
    _uid = [0]

    def uq(name):
        _uid[0] += 1
        return "%s_%d" % (name, _uid[0])

    def din(name, shape, dt=F32):
        return nc.dram_tensor(name, list(shape), dt, kind="ExternalInput").ap()

    def dout(name, shape, dt=F32):
        return nc.dram_tensor(name, list(shape), dt, kind="ExternalOutput").ap()

    def dint(name, shape, dt=F32):
        return nc.dram_tensor(name, list(shape), dt, kind="Internal").ap()

    xp = din("xp", [S, D])
    xs = din("xs", [NS, D])
    w_in, w_out, g_norm, b_f, g_q, g_k, conv_w = {}, {}, {}, {}, {}, {}, {}
    ck, cv, clf, sconv = {}, {}, {}, {}
    for l in range(4):
        fox = l % 2 == 0
        g_norm[l] = din("norm%d" % l, [D])
        w_in[l] = din("w_in%d" % l, [D, WIN_F if fox else 4 * D])
        w_out[l] = din("w_out%d" % l, [D, D])
        if fox:
            b_f[l] = din("b_f%d" % l, [H])
            g_q[l] = din("qn%d" % l, [DH])
            g_k[l] = din("kn%d" % l, [DH])
            ck[l] = din("ck%d" % l, [2, PAST, D])
            cv[l] = din("cv%d" % l, [2, PAST, D])
            clf[l] = din("clf%d" % l, [2, PAST, H])
        else:
            conv_w[l] = din("cw%d" % l, [3, D])
            sconv[l] = din("sc%d" % l, [2, 2, D])

    o_yp = dout("o_yp", [S, D])
    o_ys = dout("o_ys", [NS, D])
    o_kp, o_vp, o_lfp, o_ks, o_vs, o_lfs, o_cp, o_cs = {}, {}, {}, {}, {}, {}, {}, {}
    for l in range(4):
        if l % 2 == 0:
            o_kp[l] = dout("o_kp%d" % l, [S, D])
            o_vp[l] = dout("o_vp%d" % l, [S, D])
            o_lfp[l] = dout("o_lfp%d" % l, [S, H])
            o_ks[l] = dout("o_ks%d" % l, [NS, D])
            o_vs[l] = dout("o_vs%d" % l, [NS, D])
            o_lfs[l] = dout("o_lfs%d" % l, [NS, H])
        else:
            o_cp[l] = dout("o_cp%d" % l, [2, D])
            o_cs[l] = dout("o_cs%d" % l, [2, 2, D])

    DBG = {}
    if DEBUG:
        DBG["d_cnew"] = dout("d_cnew", [2, 16, H])
        DBG["d_cref"] = dout("d_cref", [2, 128, H])
        DBG["d_rls"] = dout("d_rls", [2, 16, H])
        DBG["d_pTn"] = dout("d_pTn", [2, 16, H * 16])
        DBG["d_cc"] = dout("d_cc", [2, 128, PT * H])
        DBG["d_lfn"] = dout("d_lfn", [2, 16, H])
        DBG["d_cbn"] = dout("d_cbn", [2, 16, H])
        DBG["d_os"] = dout("d_os", [2, 16, D], BF16)
        DBG["d_pT"] = dout("d_pT", [2, 128, H * PT * 16])
    y_sc = [dint("y_a", [S, D]), dint("y_b", [S, D])]
    ys_sc = [dint("ys_a", [NS, D]), dint("ys_b", [NS, D])]
    qT_s = dint("qT_s", [H, 128, S], BF16)
    kT_s = dint("kT_s", [H, 128, S], BF16)
    sgT_s = dint("sgT_s", [H, 128, S], BF16)
    ogT_s = dint("ogT_s", [H, 128, S], BF16)
    v16_s = dint("v16_s", [H, 128, NT * 129], BF16)

    es = ExitStack()
    with es:
        P = Prog(nc, es)

        def sb(name, shape, dt):
            return es.enter_context(nc.sbuf_tensor(uq(name), list(shape), dt))

        ident = sb("ident", [128, 128], BF16)
        identf = sb("identf", [128, 128], F32)
        tri = sb("tri", [128, 128], F32)
        trix = sb("trix", [128, 128], F32)
        trib = sb("trib", [128, 128], BF16)
        sel_last = sb("sel_last", [128, 128], F32)
        ones_f = sb("ones_f", [128, 128], F32)
        sel15 = sb("sel15", [128, 1], F32)
        s_qT = sb("s_qT", [128, H, NS], BF16)
        s_kT = sb("s_kT", [128, H, NS], BF16)
        s_sgT = sb("s_sgT", [128, H, NS], BF16)
        s_ogT = sb("s_ogT", [128, H, NS], BF16)
        s_lf = sb("s_lf", [128, H], F32)
        c_all = sb("c_all", [128, NT, H], F32)
        cref_all = sb("cref_all", [128, NT, H], F32)

        def mk_consts(e):
            def sel(t, pat, op, base, cm):
                e.memset(t[:], 1.0)
                return e.affine_select(out=t[:], in_=t[:], pattern=pat, compare_op=op, fill=0.0, base=base,
                                       channel_multiplier=cm)
            sel(ident, [[1, 128]], ALU.is_equal, 0, -1)
            sel(identf, [[1, 128]], ALU.is_equal, 0, -1)
            sel(tri, [[1, 128]], ALU.is_ge, 0, -1)
            sel(trix, [[1, 128]], ALU.is_gt, 0, -1)
            sel(trib, [[1, 128]], ALU.is_ge, 0, -1)
            sel(sel_last, [[0, 128]], ALU.is_equal, -127, 1)
            sel(sel15, [[0, 1]], ALU.is_equal, -15, 1)
            return e.memset(ones_f[:], 1.0)

        def consts_phase():
            P.op("gpsimd", mk_consts)
            P.barrier()
            P.flush()

        consts_phase()

        def load_weight_bf16(wdram, rows, cols, w16, B_w, stage_rot, col_chunk):
            k = 0
            for kc in range(rows // 128):
                c0 = 0
                while c0 < cols:
                    st, bst = stage_rot.next()
                    cw_ = min(int(st.shape[1]), cols - c0)
                    P.dma("sync", lambda e, st=st, kc=kc, c0=c0, cw_=cw_: e.dma_start(
                        out=st[:, 0:cw_], in_=wdram[kc * 128:(kc + 1) * 128, c0:c0 + cw_]), w=[bst])
                    if k % 2 == 0:
                        P.op("vector", lambda e, st=st, kc=kc, c0=c0, cw_=cw_: e.tensor_copy(
                            out=w16[:, kc, c0:c0 + cw_], in_=st[:, 0:cw_]), r=[bst], a=[B_w])
                    else:
                        P.op("scalar", lambda e, st=st, kc=kc, c0=c0, cw_=cw_: e.copy(
                            out=w16[:, kc, c0:c0 + cw_], in_=st[:, 0:cw_]), r=[bst], a=[B_w])
                    k += 1
                    c0 += cw_

        def rms_tile(x_t, Bx, nt, gtile, Bg, h16, Bh, junk, Bjunk, st_small, Bst):
            P.op("vector", lambda e: e.memset(st_small[0:nt, 0:1], 0.0), w=[Bst])
            P.op("scalar", lambda e: e.activation(out=junk[0:nt, :], in_=x_t[0:nt, :], func=AF.Square,
                                                  accum_out=st_small[0:nt, 0:1]), r=[Bx, Bst], w=[Bjunk, Bst])
            P.op("scalar", lambda e: e.activation(out=st_small[0:nt, 1:2], in_=st_small[0:nt, 0:1], func=AF.Ln,
                                                  bias=EPS, scale=1.0 / D), r=[Bst], w=[Bst])
            P.op("scalar", lambda e: e.activation(out=st_small[0:nt, 2:3], in_=st_small[0:nt, 1:2], func=AF.Exp,
                                                  scale=-0.5), r=[Bst], w=[Bst])
            P.op("vector", lambda e: e.scalar_tensor_tensor(out=h16[0:nt, :], in0=x_t[0:nt, :],
                                                            scalar=st_small[0:nt, 2:3], in1=gtile[0:nt, :],
                                                            op0=ALU.mult, op1=ALU.mult),
                 r=[Bx, Bst, Bg], w=[Bh])

        def transpose_rows(src16, Bsrc, nt, nchunks, pst, Bpst, dst_ap, Bdst, evac_eng, acc=False):
            for c in range(nchunks):
                P.op("tensor", lambda e, c=c: e.transpose(pst[:, c * 128:c * 128 + nt],
                                                          src16[0:nt, c * 128:(c + 1) * 128], ident[0:nt, 0:nt]),
                     r=[Bsrc], w=[Bpst] if c == 0 else [], a=[] if c == 0 else [Bpst])
            src_ap = pst[:, 0:nchunks * 128].rearrange("p (c t) -> p c t", t=128)[:, :, 0:nt]
            kw = dict(r=[Bpst], a=[Bdst]) if acc else dict(r=[Bpst], w=[Bdst])
            if evac_eng == "scalar":
                return P.op("scalar", lambda e: e.copy(out=dst_ap, in_=src_ap), **kw)
            return P.op(evac_eng, lambda e: e.tensor_copy(out=dst_ap, in_=src_ap), **kw)

        def fox_layer(l, src_p, Bsrc_p, src_s, Bsrc_s, dst_p, Bdst_p, dst_s, Bdst_s):
            B_qT = [Buf() for _ in range(NQ)]
            B_v16 = [Buf() for _ in range(NT)]
            B_ogT = [Buf() for _ in range(NM)]
            B_vs_out = Buf()
            B_sqT, B_skT, B_ssgT, B_sogT, B_slf, B_call, B_cref = (Buf() for _ in range(7))

            with ExitStack() as ph:
                def sbp(name, shape, dt):
                    return ph.enter_context(nc.sbuf_tensor(uq(name), list(shape), dt))

                def psp(name, shape, dt=F32):
                    return ph.enter_context(nc.psum_tensor(uq(name), list(shape), dt))

                w16 = sbp("w16", [128, 8, WIN_F], BF16)
                B_w = Buf()
                stage = Rot([sbp("wst%d" % i, [128, 1026], F32) for i in range(2)])
                gtile = sbp("gtile", [128, D], F32)
                gq = sbp("gq", [128, 512], F32)
                gk = sbp("gk", [128, 512], F32)
                bft = sbp("bft", [128, H], F32)
                B_g = Buf()
                P.dma("sync", lambda e: e.dma_start(out=gtile[:], in_=g_norm[l].partition_broadcast(128)), a=[B_g])
                for j in range(4):
                    P.dma("sync", lambda e, j=j: e.dma_start(out=gq[:, j * 128:(j + 1) * 128],
                                                             in_=g_q[l].partition_broadcast(128)), a=[B_g])
                    P.dma("sync", lambda e, j=j: e.dma_start(out=gk[:, j * 128:(j + 1) * 128],
                                                             in_=g_k[l].partition_broadcast(128)), a=[B_g])
                P.dma("sync", lambda e: e.dma_start(out=bft[:], in_=b_f[l].partition_broadcast(128)), a=[B_g])

                xrot = Rot([sbp("x%d" % i, [128, D], F32) for i in range(3)])
                junk = sbp("junk", [128, D], BF16)
                Bjunk = Buf()
                strot = Rot([sbp("st%d" % i, [128, 4], F32) for i in range(2)])
                hrot = Rot([sbp("h%d" % i, [128, D], BF16) for i in range(2)])
                hTrot = Rot([sbp("hT%d" % i, [128, 8, 128], BF16) for i in range(2)])
                sqrot = Rot([sbp("sq%d" % i, [128, 512], F32) for i in range(2)])
                ssrot = Rot([sbp("ss%d" % i, [128, 12], F32) for i in range(4)])
                kfrot = Rot([sbp("kf%d" % i, [128, D], F32) for i in range(2)])
                vfrot = Rot([sbp("vf%d" % i, [128, D], F32) for i in range(2)])
                q16rot = Rot([sbp("q16_%d" % i, [128, D], BF16) for i in range(2)])
                k16rot = Rot([sbp("k16_%d" % i, [128, D], BF16) for i in range(2)])
                sg16rot = Rot([sbp("sg16_%d" % i, [128, D], BF16) for i in range(2)])
                v16rot = Rot([sbp("v16_%d" % i, [128, H, 129], BF16) for i in range(2)])
                erot = Rot([sbp("e%d" % i, [128, 512], F32) for i in range(2)])
                qTst = Rot([sbp("qTst%d" % i, [128, H, 256], BF16) for i in range(2)])
                kTst = Rot([sbp("kTst%d" % i, [128, H, 256], BF16) for i in range(2)])
                sgTst = Rot([sbp("sgTst%d" % i, [128, H, 256], BF16) for i in range(2)])
                z_all = sbp("z_all", [128, NT + 1, H], F32)
                lf_all = sbp("lf_all", [128, NT + 1, H], F32)
                tsum = sbp("tsum", [1, NT + 1, H], F32)
                B_z = Buf()
                B_lf = Buf()
                B_ts = Buf()
                big = Rot(stage.tiles + kfrot.tiles + vfrot.tiles + xrot.tiles)
                big.bufs = stage.bufs + kfrot.bufs + vfrot.bufs + xrot.bufs
                load_weight_bf16(w_in[l], D, WIN_F, w16, B_w, big, 1026)
                pmm = Rot([psp("pmm%d" % i, [128, 512], F32) for i in range(4)], psum=True)
                ptr = Rot([psp("ptr%d" % i, [128, 1024], BF16) for i in range(3)], psum=True)
                pz = psp("pz", [128, 512], F32)
                B_pz = Buf(psum=True)
                assert NT * H <= 512

                for i in range(2):
                    P.op("gpsimd", lambda e, t_=v16rot.tiles[i]: e.memset(t_[:, :, 128:129], 1.0),
                         a=[v16rot.bufs[i]])
                P.op("gpsimd", lambda e: e.memset(z_all[:, :, :], 0.0), w=[B_z])

                tiles = [(t, 128) for t in range(NT)] + ([(NT, NS)] if with_sample else [])
                cur_box = [None]

                def tile_body(t, nt):
                    is_s = t == NT
                    x_t, Bx = xrot.next()
                    if is_s:
                        P.dma("sync", lambda e, x_t=x_t: e.dma_start(out=x_t[0:NS, :], in_=src_s[:, :]),
                              r=[Bsrc_s], w=[Bx])
                    else:
                        P.dma("sync", lambda e, x_t=x_t, t=t: e.dma_start(out=x_t[:, :],
                                                                          in_=src_p[t * 128:(t + 1) * 128, :]),
                              r=[Bsrc_p[t]], w=[Bx])
                    st_s, Bst = strot.next()
                    h16, Bh = hrot.next()
                    rms_tile(x_t, Bx, nt, gtile, B_g, h16, Bh, junk, Bjunk, st_s, Bst)
                    yield "fa"
                    hT, BhT = hTrot.next()
                    pt, Bpt = ptr.next()
                    transpose_rows(h16, Bh, nt, 8, pt, Bpt, hT[:, :, 0:nt], BhT, "vector")
                    yield "fb"

                    def mm_slab(c0, ncols, pso, Bpso):
                        for kc in range(8):
                            P.op("tensor", lambda e, kc=kc: e.matmul(pso[0:nt, 0:ncols], lhsT=hT[:, kc, 0:nt],
                                                                      rhs=w16[:, kc, c0:c0 + ncols],
                                                                      start=(kc == 0), stop=(kc == 7)),
                                 r=[BhT, B_w], w=[Bpso])

                    kf, Bkf = kfrot.next()
                    vf, Bvf = vfrot.next()
                    q16, Bq16 = q16rot.next()
                    k16, Bk16 = k16rot.next()
                    sg16, Bsg16 = sg16rot.next()
                    v16t, Bv16 = v16rot.next()
                    for which in range(2):
                        for half in range(2):
                            pso, Bpso = pmm.next()
                            mm_slab(which * D + half * 512, 512, pso, Bpso)
                            sq, Bsq = sqrot.next()
                            ss, Bss = ssrot.next()
                            P.op("scalar", lambda e, pso=pso, sq=sq: e.activation(out=sq[0:nt, :], in_=pso[0:nt, :],
                                                                                  func=AF.Square),
                                 r=[Bpso], w=[Bsq])
                            P.op("vector", lambda e, sq=sq, ss=ss: e.tensor_reduce(
                                out=ss[0:nt, 0:4], in_=sq[0:nt, :].rearrange("p (h d) -> p h d", d=128),
                                axis=AX.X, op=ALU.add), r=[Bsq], w=[Bss])
                            P.op("scalar", lambda e, ss=ss: e.activation(out=ss[0:nt, 4:8], in_=ss[0:nt, 0:4],
                                                                         func=AF.Ln, bias=EPS, scale=1.0 / DH),
                                 r=[Bss], w=[Bss])
                            P.op("scalar", lambda e, ss=ss: e.activation(out=ss[0:nt, 8:12], in_=ss[0:nt, 4:8],
                                                                         func=AF.Exp, scale=-0.5),
                                 r=[Bss], w=[Bss])
                            for hh in range(4):
                                c0_ = half * 512 + hh * 128
                                if which == 0:
                                    P.op("vector", lambda e, pso=pso, ss=ss, hh=hh, c0_=c0_: e.scalar_tensor_tensor(
                                        out=q16[0:nt, c0_:c0_ + 128], in0=pso[0:nt, hh * 128:(hh + 1) * 128],
                                        scalar=ss[0:nt, 8 + hh:9 + hh], in1=gq[0:nt, 0:128], op0=ALU.mult,
                                        op1=ALU.mult), r=[Bpso, Bss, B_g], a=[Bq16])
                                else:
                                    P.op("vector", lambda e, pso=pso, ss=ss, hh=hh, c0_=c0_: e.scalar_tensor_tensor(
                                        out=kf[0:nt, c0_:c0_ + 128], in0=pso[0:nt, hh * 128:(hh + 1) * 128],
                                        scalar=ss[0:nt, 8 + hh:9 + hh], in1=gk[0:nt, 0:128], op0=ALU.mult,
                                        op1=ALU.mult), r=[Bpso, Bss, B_g], a=[Bkf])
                    P.op("scalar", lambda e: e.copy(out=k16[0:nt, :], in_=kf[0:nt, :]), r=[Bkf], w=[Bk16])
                    yield "A"
                    for half in range(2):
                        pso, Bpso = pmm.next()
                        mm_slab(2 * D + half * 512, 512, pso, Bpso)
                        P.op("scalar", lambda e, pso=pso, half=half: e.copy(out=vf[0:nt, half * 512:(half + 1) * 512],
                                                                           in_=pso[0:nt, :]), r=[Bpso], a=[Bvf])
                        P.op("vector", lambda e, pso=pso, half=half: e.tensor_copy(
                            out=v16t[0:nt, half * 4:(half + 1) * 4, 0:128],
                            in_=pso[0:nt, :].rearrange("p (h d) -> p h d", d=128)), r=[Bpso], a=[Bv16])
                    for half in range(2):
                        pso, Bpso = pmm.next()
                        mm_slab(3 * D + half * 512, 512, pso, Bpso)
                        ee, Be = erot.next()
                        P.op("scalar", lambda e, pso=pso, ee=ee: e.activation(out=ee[0:nt, :], in_=pso[0:nt, :],
                                                                             func=AF.Exp, scale=-1.0),
                             r=[Bpso], w=[Be])
                        P.op("scalar", lambda e, ee=ee: e.activation(out=ee[0:nt, :], in_=ee[0:nt, :], func=AF.Ln,
                                                                    bias=1.0, scale=1.0), r=[Be], w=[Be])
                        P.op("scalar", lambda e, ee=ee: e.activation(out=ee[0:nt, :], in_=ee[0:nt, :], func=AF.Exp,
                                                                    scale=-1.0), r=[Be], w=[Be])
                        P.op("vector", lambda e, pso=pso, ee=ee, half=half: e.tensor_tensor(
                            out=sg16[0:nt, half * 512:(half + 1) * 512], in0=pso[0:nt, :], in1=ee[0:nt, :],
                            op=ALU.mult), r=[Bpso, Be], a=[Bsg16])
                        if half == 0:
                            yield "B1"
                    pso, Bpso = pmm.next()
                    mm_slab(4 * D, H, pso, Bpso)
                    P.op("scalar", lambda e, pso=pso, t=t: e.copy(out=z_all[0:nt, t, :], in_=pso[0:nt, 0:H]),
                         r=[Bpso], a=[B_z])

                    if is_s:
                        P.dma(STQ, lambda e: e.dma_start(out=o_ks[l][:, :], in_=kf[0:NS, :]), r=[Bkf])
                        P.dma(STQ, lambda e: e.dma_start(out=o_vs[l][:, :], in_=vf[0:NS, :]), r=[Bvf],
                              w=[B_vs_out])
                    else:
                        P.dma(STQ, lambda e, t=t: e.dma_start(out=o_kp[l][t * 128:(t + 1) * 128, :], in_=kf[:, :]),
                              r=[Bkf])
                        P.dma(STQ, lambda e, t=t: e.dma_start(out=o_vp[l][t * 128:(t + 1) * 128, :], in_=vf[:, :]),
                              r=[Bvf])
                        P.dma(STQ, lambda e, t=t: e.dma_start(
                            out=v16_s[:, :, t * 129:(t + 1) * 129].rearrange("h p c -> p h c"), in_=v16t[:, :, :]),
                              r=[Bv16], w=[B_v16[t]])
                    yield "B2"
                    if is_s:
                        for k_, (src, Bs_, dst, Bd) in enumerate(((q16, Bq16, s_qT, B_sqT), (k16, Bk16, s_kT, B_skT),
                                                                  (sg16, Bsg16, s_sgT, B_ssgT))):
                            pt, Bpt = ptr.next()
                            transpose_rows(src, Bs_, nt, 8, pt, Bpt, dst[:, :, 0:nt], Bd, "vector")
                    else:
                        sub = t % 2
                        if sub == 0:
                            cur_box[0] = (qTst.next(), kTst.next(), sgTst.next())
                        cur = cur_box[0]
                        for k_, (src, Bs_) in enumerate(((q16, Bq16), (k16, Bk16), (sg16, Bsg16))):
                            pt, Bpt = ptr.next()
                            stt, Bstt = cur[k_]
                            transpose_rows(src, Bs_, nt, 8, pt, Bpt, stt[:, :, sub * 128:(sub + 1) * 128], Bstt,
                                           "scalar" if k_ == 1 else "vector", acc=True)
                        if sub == 1:
                            m = t // 2
                            for k_, dst in enumerate((qT_s, kT_s, sgT_s)):
                                stt, Bstt = cur[k_]
                                P.dma(STQ, lambda e, dst=dst, stt=stt, m=m: e.dma_start(
                                    out=dst[:, :, m * 256:(m + 1) * 256].rearrange("h p c -> p h c"), in_=stt[:, :, :]),
                                      r=[Bstt], a=[B_qT[m]])


                gens = [tile_body(t_, nt_) for (t_, nt_) in tiles]
                ng = len(gens)

                def adv(ti, want):
                    if 0 <= ti < ng:
                        got = next(gens[ti])
                        assert got == want, (got, want)

                def fin_(ti):
                    if 0 <= ti < ng:
                        for _ in gens[ti]:
                            pass
                adv(0, "fa")
                adv(0, "fb")
                adv(1, "fa")
                for ti in range(ng):
                    adv(ti, "A")
                    adv(ti + 2, "fa")
                    fin_(ti - 1)
                    adv(ti, "B1")
                    adv(ti + 1, "fb")
                    adv(ti, "B2")
                fin_(ng - 1)
                nz = NT + (1 if with_sample else 0)
                P.op("vector", lambda e: e.tensor_tensor(out=z_all[:, 0:nz, :], in0=z_all[:, 0:nz, :],
                                                         in1=bft[:, :].unsqueeze(1).to_broadcast([128, nz, H]),
                                                         op=ALU.add), r=[B_z, B_g], w=[B_z])
                P.op("scalar", lambda e: e.activation(out=lf_all[:, 0:nz, :], in_=z_all[:, 0:nz, :], func=AF.Exp,
                                                      scale=-1.0), r=[B_z], w=[B_lf])
                P.op("scalar", lambda e: e.activation(out=lf_all[:, 0:nz, :], in_=lf_all[:, 0:nz, :], func=AF.Ln,
                                                      bias=1.0, scale=1.0), r=[B_lf], w=[B_lf])
                P.op("vector", lambda e: e.tensor_scalar(out=lf_all[:, 0:nz, :], in0=lf_all[:, 0:nz, :],
                                                         scalar1=-1.0, scalar2=None, op0=ALU.mult),
                     r=[B_lf], w=[B_lf])
                nchunk = max(1, NT // 8)
                for c0 in range(0, NT, nchunk):
                    P.dma(STQ, lambda e, c0=c0: e.dma_start(
                        out=o_lfp[l][c0 * 128:(c0 + nchunk) * 128, :].rearrange("(t p) h -> p t h", p=128),
                        in_=lf_all[:, c0:c0 + nchunk, :]), r=[B_lf])
                if with_sample:
                    P.dma(STQ, lambda e: e.dma_start(out=o_lfs[l][:, :], in_=lf_all[0:NS, NT, :]), r=[B_lf])
                    P.op("gpsimd", lambda e: e.tensor_copy(out=s_lf[0:NS, :], in_=lf_all[0:NS, NT, :]),
                         r=[B_lf], w=[B_slf])
                NF = NT * H
                lf_flat = lf_all[:, 0:NT, :].rearrange("p t h -> p (t h)")
                P.op("tensor", lambda e: e.matmul(pz[0:1, 0:NF], lhsT=ones_f[:, 0:1], rhs=lf_flat,
                                                  start=True, stop=True), r=[B_lf], w=[B_pz])
                P.op("vector", lambda e: e.memset(tsum[:, 0, :], 0.0), w=[B_ts])
                P.op("vector", lambda e: e.tensor_copy(out=tsum[:, 1:NT + 1, :].rearrange("p t h -> p (t h)"),
                                                       in_=pz[0:1, 0:NF]), r=[B_pz], a=[B_ts])
                for t in range(1, NT):
                    P.op("vector", lambda e, t=t: e.tensor_tensor(out=tsum[:, t, :], in0=tsum[:, t, :],
                                                                  in1=tsum[:, t - 1, :], op=ALU.add),
                         r=[B_ts], w=[B_ts])
                P.op("tensor", lambda e: e.matmul(pz[:, 0:NF], lhsT=tri[:, :], rhs=lf_flat, start=True, stop=False),
                     r=[B_lf, B_ts], w=[B_pz])
                P.op("tensor", lambda e: e.matmul(pz[:, 0:NF], lhsT=ones_f[0:1, :],
                                                  rhs=tsum[:, 0:NT, :].rearrange("p t h -> p (t h)"),
                                                  start=False, stop=True), r=[B_ts], w=[B_pz])
                P.op("vector", lambda e: e.tensor_copy(out=c_all[:, :, :].rearrange("p t h -> p (t h)"),
                                                       in_=pz[:, 0:NF]), r=[B_pz], w=[B_call])
                P.op("tensor", lambda e: e.matmul(pz[:, 0:NF], lhsT=sel_last[:, :],
                                                  rhs=c_all[:, :, :].rearrange("p t h -> p (t h)"),
                                                  start=True, stop=True), r=[B_call], w=[B_pz])
                P.op("vector", lambda e: e.tensor_copy(out=cref_all[:, :, :].rearrange("p t h -> p (t h)"),
                                                       in_=pz[:, 0:NF]), r=[B_pz], w=[B_cref])
                P.barrier()
                P.flush()

            with ExitStack() as ph:
                def sbp(name, shape, dt):
                    return ph.enter_context(nc.sbuf_tensor(uq(name), list(shape), dt))

                def psp(name, shape, dt=F32):
                    return ph.enter_context(nc.psum_tensor(uq(name), list(shape), dt))

                hd = Rot([(sbp("KT%d" % i, [128, S], BF16), sbp("QT%d" % i, [128, S], BF16),
                           sbp("SG%d" % i, [128, S], BF16), sbp("V%d" % i, [128, NT * 129], BF16)) for i in range(2)])
                cbrot = Rot([sbp("cb%d" % i, [128, NT], F32) for i in range(2)])
                prot = Rot([sbp("p%d" % i, [128, 512], BF16) for i in range(4)])
                pss = Rot([psp("pss%d" % i, [128, 512], F32) for i in range(3)], psum=True)
                pso_r = Rot([psp("pso%d" % i, [128, 2, 256], F32) for i in range(4)], psum=True)
                pst = psp("pst", [128, 512], BF16)
                B_pst = Buf(psum=True)
                rlrot = Rot([sbp("rl%d" % i, [128, 4], F32) for i in range(2)])
                o16rot = Rot([sbp("o16_%d" % i, [128, 512], BF16) for i in range(2)])
                ogrot = Rot([sbp("og%d" % i, [128, 512], BF16) for i in range(2)])

                LA = 2
                blocks = [(h, i, j) for h in range(H) for i in range(NM) for j in range(4 * (i + 1))]
                heads = {}
                qstate = {}
                bstate = {}
                deferred = []

                def load_head(h):
                    (KT, QT, SG, V), Bhd = hd.next()
                    P.dma("sync", lambda e: e.dma_start(out=KT[:, :], in_=kT_s[h]), a=[Bhd])
                    P.dma("sync", lambda e: e.dma_start(out=QT[:, :], in_=qT_s[h]), a=[Bhd])
                    P.dma("sync", lambda e: e.dma_start(out=SG[:, :], in_=sgT_s[h]), a=[Bhd])
                    P.dma("sync", lambda e: e.dma_start(out=V[:, :], in_=v16_s[h]), a=[Bhd])
                    heads[h] = (KT, QT, SG, V, Bhd)

                def stage_a(h, i, j):
                    KT, QT, SG, V, Bhd = heads[h]
                    nk = 4 * (i + 1)
                    if j == 0:
                        cb, Bcb = cbrot.next()
                        P.op("vector", lambda e: e.tensor_scalar(
                            out=cb[:, 0:nk], in0=c_all[:, 0:nk, h], scalar1=-1.0,
                            scalar2=cref_all[:, 4 * i + 3, h:h + 1], op0=ALU.mult, op1=ALU.add), w=[Bcb])
                        qstate[(h, i)] = (cb, Bcb, [pso_r.next(), pso_r.next()])
                    cb, Bcb, po = qstate[(h, i)]
                    m = j - 4 * i
                    c_lo = 128 * m if m > 0 else 0
                    s_ps, Bs_ps = pss.next()
                    P.op("tensor", lambda e: e.matmul(
                        s_ps[:, c_lo:512], lhsT=KT[:, j * 128:(j + 1) * 128],
                        rhs=QT[:, i * 512 + c_lo:(i + 1) * 512], start=True, stop=True), r=[Bhd], w=[Bs_ps])
                    pt_, Bp = prot.next()
                    P.op("scalar", lambda e: e.activation(
                        out=pt_[:, c_lo:512], in_=s_ps[:, c_lo:512], func=AF.Exp, bias=cb[:, j:j + 1],
                        scale=SCALE), r=[Bs_ps, Bcb], w=[Bp])
                    if m >= 0:
                        P.op("gpsimd", lambda e: e.tensor_tensor(
                            out=pt_[:, c_lo:c_lo + 128], in0=pt_[:, c_lo:c_lo + 128], in1=trib[:, :],
                            op=ALU.mult), r=[Bp], w=[Bp])
                    bstate[(h, i, j)] = (pt_, Bp)

                def stage_b(h, i, j, step):
                    KT, QT, SG, V, Bhd = heads[h]
                    cb, Bcb, po = qstate[(h, i)]
                    pt_, Bp = bstate.pop((h, i, j))
                    m = j - 4 * i
                    for c in range(max(m, 0), 4):
                        (pot, Bpo) = po[c // 2]
                        first = (j == 0 and c % 2 == 0)
                        P.op("tensor", lambda e, pot=pot, c=c, first=first: e.matmul(
                            pot[:, c % 2, 0:129], lhsT=pt_[:, c * 128:(c + 1) * 128],
                            rhs=V[:, j * 129:(j + 1) * 129], start=first, stop=(j == 4 * i + c),
                            skip_group_check=True), r=[Bp, Bhd], w=[Bpo] if first else [], a=[] if first else [Bpo])
                    if j != 4 * i + 3:
                        return
                    rl, Brl = rlrot.next()
                    o16, Bo16 = o16rot.next()
                    for c in range(4):
                        (pot, Bpo) = po[c // 2]
                        P.op("vector", lambda e, pot=pot, c=c: e.reciprocal(
                            out=rl[:, c:c + 1], in_=pot[:, c % 2, 128:129]), r=[Bpo], a=[Brl])
                    for c in range(4):
                        (pot, Bpo) = po[c // 2]
                        P.op("vector", lambda e, pot=pot, c=c: e.tensor_scalar(
                            out=o16[:, c * 128:(c + 1) * 128], in0=pot[:, c % 2, 0:128], scalar1=rl[:, c:c + 1],
                            scalar2=None, op0=ALU.mult), r=[Bpo, Brl], a=[Bo16])
                    del qstate[(h, i)]

                    def fin():
                        for c in range(4):
                            P.op("tensor", lambda e, c=c: e.transpose(
                                pst[:, c * 128:(c + 1) * 128], o16[:, c * 128:(c + 1) * 128], ident[:, :]),
                                 r=[Bo16], w=[B_pst] if c == 0 else [], a=[] if c == 0 else [B_pst])
                        og, Bog = ogrot.next()
                        P.op("vector", lambda e: e.tensor_tensor(
                            out=og[:, :], in0=pst[:, :], in1=SG[:, i * 512:(i + 1) * 512], op=ALU.mult),
                             r=[B_pst, Bhd], w=[Bog])
                        P.dma("sync", lambda e: e.dma_start(
                            out=ogT_s[h, :, i * 512:(i + 1) * 512], in_=og[:, :]), r=[Bog], a=[B_ogT[i]])
                    deferred.append((step + 3, fin))

                NB = len(blocks)
                per_head = NB // H
                assert per_head > LA + 5
                load_head(0)
                load_head(1)
                for step in range(NB + LA + 4):
                    if step < NB:
                        stage_a(*blocks[step])
                    if 0 <= step - LA < NB:
                        stage_b(*blocks[step - LA], step)
                    while deferred and deferred[0][0] <= step:
                        deferred.pop(0)[1]()
                    hh, off = divmod(step, per_head)
                    if off == LA + 4 and 1 <= hh and hh + 1 < H:
                        load_head(hh + 1)
                assert not deferred and not bstate
                P.barrier()
                P.flush()

            if with_sample:
                sample_attention(l, B_sqT, B_skT, B_ssgT, B_sogT, B_slf, B_vs_out)

            with ExitStack() as ph:
                def sbp(name, shape, dt):
                    return ph.enter_context(nc.sbuf_tensor(uq(name), list(shape), dt))

                def psp(name, shape, dt=F32):
                    return ph.enter_context(nc.psum_tensor(uq(name), list(shape), dt))

                wo16 = sbp("wo16", [128, 8, D], BF16)
                B_wo = Buf()
                stage = Rot([sbp("wst%d" % i, [128, 1024], F32) for i in range(2)])
                load_weight_bf16(w_out[l], D, D, wo16, B_wo, stage, 1024)
                ogmrot = Rot([sbp("ogm%d" % i, [128, H, 512], BF16) for i in range(2)])
                xrot = Rot([sbp("xr%d" % i, [128, D], F32) for i in range(3)])
                yrot = Rot([sbp("yr%d" % i, [128, D], F32) for i in range(3)])
                pmm = Rot([psp("pmo%d" % i, [128, 512], F32) for i in range(4)], psum=True)

                def outproj_tile(ogm, Bogm, c0, nt, x_t, Bx, y_t, By):
                    for half in range(2):
                        pso, Bpso = pmm.next()
                        for kc in range(8):
                            P.op("tensor", lambda e, kc=kc, pso=pso, half=half: e.matmul(
                                pso[0:nt, :], lhsT=ogm[:, kc, c0:c0 + nt], rhs=wo16[:, kc, half * 512:(half + 1) * 512],
                                start=(kc == 0), stop=(kc == 7)), r=[Bogm, B_wo], w=[Bpso])
                        P.op("vector", lambda e, pso=pso, half=half: e.tensor_tensor(
                            out=y_t[0:nt, half * 512:(half + 1) * 512], in0=pso[0:nt, :],
                            in1=x_t[0:nt, half * 512:(half + 1) * 512], op=ALU.add), r=[Bpso, Bx], a=[By])

                for m in range(NM):
                    ogm, Bogm = ogmrot.next()
                    P.dma("sync", lambda e, ogm=ogm, m=m: e.dma_start(
                        out=ogm[:, :, :], in_=ogT_s[:, :, m * 512:(m + 1) * 512].rearrange("h p c -> p h c")),
                          r=[B_ogT[m]], w=[Bogm])
                    for sub in range(4):
                        t = m * 4 + sub
                        x_t, Bx = xrot.next()
                        P.dma("sync", lambda e, x_t=x_t, t=t: e.dma_start(out=x_t[:, :],
                                                                          in_=src_p[t * 128:(t + 1) * 128, :]),
                              r=[Bsrc_p[t]], w=[Bx])
                        y_t, By = yrot.next()
                        outproj_tile(ogm, Bogm, sub * 128, 128, x_t, Bx, y_t, By)
                        P.dma(STQ, lambda e, y_t=y_t, t=t: e.dma_start(out=dst_p[t * 128:(t + 1) * 128, :],
                                                                          in_=y_t[:, :]), r=[By], w=[Bdst_p[t]])
                if with_sample:
                    x_t, Bx = xrot.next()
                    P.dma("sync", lambda e, x_t=x_t: e.dma_start(out=x_t[0:NS, :], in_=src_s[:, :]),
                          r=[Bsrc_s], w=[Bx])
                    y_t, By = yrot.next()
                    outproj_tile(s_ogT, B_sogT, 0, NS, x_t, Bx, y_t, By)
                    P.dma(STQ, lambda e, y_t=y_t: e.dma_start(out=dst_s[:, :], in_=y_t[0:NS, :]), r=[By],
                          w=[Bdst_s])
                P.barrier()
                P.flush()

        def sample_attention(l, B_sqT, B_skT, B_ssgT, B_sogT, B_slf, B_vs_out):
            with ExitStack() as ph:
                def sbp(name, shape, dt):
                    return ph.enter_context(nc.sbuf_tensor(uq(name), list(shape), dt))

                def psp(name, shape, dt=F32):
                    return ph.enter_context(nc.psum_tensor(uq(name), list(shape), dt))

                CH = 4
                NCH = PT // CH
                kch = Rot([sbp("kch%d" % i, [128, CH, D], F32) for i in range(2)])
                vch = Rot([sbp("vch%d" % i, [128, CH, D], F32) for i in range(2)])
                KTs = sbp("KTs", [128, H, PT * 128], BF16)
                B_KTs = Buf()
                clf_t = sbp("clf_t", [128, PT, H], F32)
                B_clf = Buf()
                ccache = sbp("ccache", [128, PT, H], F32)
                B_cc = Buf()
                cb_c = sbp("cb_c", [128, H, PT], F32)
                B_cbc = Buf()
                pT_all = sbp("pT_all", [128, H, PT * 16], F32)
                B_pT = Buf()
                pT_new = sbp("pT_new", [16, H, 16], F32)
                B_pTn = Buf()
                vnew = sbp("vnew", [16, D], F32)
                B_vnew = Buf()
                lfn = sbp("lfn", [16, H], F32)
                B_lfn = Buf()
                small = sbp("small", [128, 16], F32)
                B_small = Buf()
                cnew = sbp("cnew", [16, H], F32)
                B_cnew = Buf()
                cref_s = sbp("cref_s", [128, H], F32)
                B_crefs = Buf()
                cbn = sbp("cbn", [16, H], F32)
                B_cbn = Buf()
                tmpS = sbp("tmpS", [128, PT * 16], F32)
                B_tmpS = Buf()
                o_s = sbp("o_s", [16, D], BF16)
                B_os = Buf()
                rls = sbp("rls", [16, H], F32)
                B_rls = Buf()
                ptr = Rot([psp("pstr%d" % i, [128, 512], F32) for i in range(2)], psum=True)
                psS = Rot([psp("psS%d" % i, [128, 512], F32) for i in range(2)], psum=True)
                psO = [psp("psO%d" % i, [16, 4, 128], F32) for i in range(2)]
                B_psO = [Buf(psum=True), Buf(psum=True)]
                psM = psp("psM", [16, 512], F32)
                psL = psM[:, 0:H]
                psN = psM[:, 128:256].rearrange("p (h q) -> p h q", q=16)
                B_psL = Buf(psum=True)
                B_psN = B_psL
                psT = psp("psT", [128, 512], BF16)
                B_psT = Buf(psum=True)
                ckv = lambda a, s: a[l][s].rearrange("(p j) f -> p j f", j=PT)

                for s in range(2):
                    P.dma("sync", lambda e, s=s: e.dma_start(out=clf_t[:, :, :],
                                                             in_=clf[l][s].rearrange("(p j) h -> p j h", j=PT)),
                          w=[B_clf])
                    P.op("vector", lambda e: e.tensor_copy(out=ccache[:, 0, :], in_=clf_t[:, 0, :]), r=[B_clf],
                         w=[B_cc])
                    for j in range(1, PT):
                        P.op("vector", lambda e, j=j: e.tensor_tensor(out=ccache[:, j, :], in0=ccache[:, j - 1, :],
                                                                      in1=clf_t[:, j, :], op=ALU.add),
                             r=[B_clf, B_cc], w=[B_cc])
                    ps1, Bps1 = psS.next()
                    P.op("tensor", lambda e, ps1=ps1: e.matmul(ps1[:, 0:H], lhsT=trix[:, :], rhs=ccache[:, PT - 1, :],
                                                              start=True, stop=True), r=[B_cc], w=[Bps1])
                    P.op("vector", lambda e, ps1=ps1: e.tensor_copy(out=small[:, 0:H], in_=ps1[:, 0:H]),
                         r=[Bps1], w=[B_small])
                    P.op("vector", lambda e: e.tensor_tensor(
                        out=ccache[:, :, :], in0=ccache[:, :, :],
                        in1=small[:, 0:H].unsqueeze(1).to_broadcast([128, PT, H]), op=ALU.add),
                         r=[B_small, B_cc], w=[B_cc])
                    P.dma("sync", lambda e, s=s: e.dma_start(out=lfn[:, :], in_=s_lf[s * 16:(s + 1) * 16, :]),
                          r=[B_slf], w=[B_lfn])
                    ps2, Bps2 = psS.next()
                    P.op("tensor", lambda e, ps2=ps2: e.matmul(ps2[0:16, 0:H], lhsT=sel_last[:, 0:16],
                                                              rhs=ccache[:, PT - 1, :], start=True, stop=False),
                         r=[B_cc], w=[Bps2])
                    P.op("tensor", lambda e, ps2=ps2: e.matmul(ps2[0:16, 0:H], lhsT=tri[0:16, 0:16], rhs=lfn[:, :],
                                                              start=False, stop=True), r=[B_lfn], a=[Bps2])
                    P.op("vector", lambda e, ps2=ps2: e.tensor_copy(out=cnew[:, :], in_=ps2[0:16, 0:H]),
                         r=[Bps2], w=[B_cnew])
                    P.op("vector", lambda e: e.tensor_scalar(out=small[0:16, 8:8 + H], in0=cnew[:, :],
                                                             scalar1=sel15[0:16, 0:1], scalar2=None, op0=ALU.mult),
                         r=[B_cnew], w=[B_small])
                    ps3, Bps3 = psS.next()
                    P.op("tensor", lambda e, ps3=ps3: e.matmul(ps3[:, 0:H], lhsT=ones_f[0:16, :],
                                                              rhs=small[0:16, 8:8 + H], start=True, stop=True),
                         r=[B_small], w=[Bps3])
                    P.op("vector", lambda e, ps3=ps3: e.tensor_copy(out=cref_s[:, :], in_=ps3[:, 0:H]),
                         r=[Bps3], w=[B_crefs])
                    P.op("vector", lambda e: e.tensor_tensor(
                        out=cb_c[:, :, :], in0=cref_s[:, :].unsqueeze(2).to_broadcast([128, H, PT]),
                        in1=ccache[:, :, :].rearrange("p j h -> p h j"), op=ALU.subtract),
                         r=[B_crefs, B_cc], w=[B_cbc])
                    P.op("vector", lambda e: e.tensor_tensor(out=cbn[:, :], in0=cref_s[0:16, :], in1=cnew[:, :],
                                                             op=ALU.subtract), r=[B_crefs, B_cnew], w=[B_cbn])
                    for ch in range(NCH):
                        kc_t, Bkc = kch.next()
                        P.dma("sync", lambda e, kc_t=kc_t, ch=ch, s=s: e.dma_start(
                            out=kc_t[:, :, :], in_=ckv(ck, s)[:, ch * CH:(ch + 1) * CH, :]), w=[Bkc])
                        for jj in range(CH):
                            j = ch * CH + jj
                            for hg in range(2):
                                pt, Bpt = ptr.next()
                                for hh in range(4):
                                    h = hg * 4 + hh
                                    P.op("tensor", lambda e, pt=pt, hh=hh, h=h, jj=jj, kc_t=kc_t: e.transpose(
                                        pt[:, hh * 128:(hh + 1) * 128], kc_t[:, jj, h * 128:(h + 1) * 128],
                                        identf[:, :]), r=[Bkc], w=[Bpt] if hh == 0 else [],
                                         a=[] if hh == 0 else [Bpt])
                                eng = "vector" if (j + hg) % 2 == 0 else "scalar"
                                dst = KTs[:, hg * 4:(hg + 1) * 4, j * 128:(j + 1) * 128]
                                srcp = pt[:, :].rearrange("p (h k) -> p h k", k=128)
                                if eng == "vector":
                                    P.op("vector", lambda e, dst=dst, srcp=srcp: e.tensor_copy(out=dst, in_=srcp),
                                         r=[Bpt], a=[B_KTs])
                                else:
                                    P.op("scalar", lambda e, dst=dst, srcp=srcp: e.copy(out=dst, in_=srcp),
                                         r=[Bpt], a=[B_KTs])
                    P.dma("sync", lambda e, s=s: e.dma_start(out=vnew[:, :], in_=o_vs[l][s * 16:(s + 1) * 16, :]),
                          r=[B_vs_out], w=[B_vnew])
                    for h in range(H):
                        pS, BpS = psS.next()
                        for j in range(PT):
                            P.op("tensor", lambda e, pS=pS, j=j, h=h, s=s: e.matmul(
                                pS[:, j * 16:(j + 1) * 16], lhsT=KTs[:, h, j * 128:(j + 1) * 128],
                                rhs=s_qT[:, h, s * 16:(s + 1) * 16], start=True, stop=True),
                                 r=[B_KTs, B_sqT], w=[BpS] if j == 0 else [], a=[] if j == 0 else [BpS])
                        P.op("vector", lambda e, pS=pS, h=h: e.scalar_tensor_tensor(
                            out=tmpS[:, :].rearrange("p (j q) -> p j q", q=16),
                            in0=pS[:, 0:PT * 16].rearrange("p (j q) -> p j q", q=16), scalar=SCALE,
                            in1=cb_c[:, h, :].unsqueeze(2).to_broadcast([128, PT, 16]), op0=ALU.mult, op1=ALU.add),
                             r=[BpS, B_cbc], w=[B_tmpS])
                        P.op("scalar", lambda e, h=h: e.activation(out=pT_all[:, h, :], in_=tmpS[:, :], func=AF.Exp),
                             r=[B_tmpS], a=[B_pT])
                        P.op("tensor", lambda e, h=h, s=s: e.matmul(
                            psN[:, h, :], lhsT=s_kT[:, h, s * 16:(s + 1) * 16], rhs=s_qT[:, h, s * 16:(s + 1) * 16],
                            start=True, stop=True), r=[B_skT, B_sqT], a=[B_psN])
                    for h in range(H):
                        P.op("scalar", lambda e, h=h: e.activation(out=pT_new[:, h, :], in_=psN[:, h, :], func=AF.Exp,
                                                                   bias=cbn[:, h:h + 1], scale=SCALE),
                             r=[B_psN, B_cbn], a=[B_pTn])
                    P.op("vector", lambda e: e.tensor_tensor(
                        out=pT_new[:, :, :], in0=pT_new[:, :, :],
                        in1=tri[0:16, 0:16].unsqueeze(1).to_broadcast([16, H, 16]), op=ALU.mult),
                         r=[B_pTn], w=[B_pTn])
                    for h in range(H):
                        for j in range(PT):
                            P.op("tensor", lambda e, h=h, j=j: e.matmul(
                                psL[:, h:h + 1], lhsT=pT_all[:, h, j * 16:(j + 1) * 16], rhs=ones_f[:, 0:1],
                                start=(j == 0 and h == 0), stop=False, skip_group_check=True), r=[B_pT],
                                     w=[B_psL] if (j == 0 and h == 0) else [], a=[] if (j == 0 and h == 0) else [B_psL])
                        P.op("tensor", lambda e, h=h: e.matmul(psL[:, h:h + 1], lhsT=pT_new[:, h, :],
                                                                rhs=ones_f[0:16, 0:1], start=False, stop=True,
                                                                skip_group_check=True),
                             r=[B_pTn], a=[B_psL])
                    for ch in range(NCH):
                        vc_t, Bvc = vch.next()
                        P.dma("sync", lambda e, vc_t=vc_t, ch=ch, s=s: e.dma_start(
                            out=vc_t[:, :, :], in_=ckv(cv, s)[:, ch * CH:(ch + 1) * CH, :]), w=[Bvc])
                        for jj in range(CH):
                            j = ch * CH + jj
                            for h in range(H):
                                P.op("tensor", lambda e, h=h, j=j, jj=jj, vc_t=vc_t: e.matmul(
                                    psO[h // 4][:, h % 4, :], lhsT=pT_all[:, h, j * 16:(j + 1) * 16],
                                    rhs=vc_t[:, jj, h * 128:(h + 1) * 128], start=(j == 0 and h % 4 == 0), stop=False,
                                    skip_group_check=True),
                                     r=[B_pT, Bvc], a=[B_psO[h // 4]])
                    for h in range(H):
                        P.op("tensor", lambda e, h=h: e.matmul(
                            psO[h // 4][:, h % 4, :], lhsT=pT_new[:, h, :], rhs=vnew[:, h * 128:(h + 1) * 128],
                            start=False, stop=True, skip_group_check=True), r=[B_pTn, B_vnew], a=[B_psO[h // 4]])
                    P.op("vector", lambda e: e.reciprocal(out=rls[:, :], in_=psL), r=[B_psL], w=[B_rls])
                    for hg in range(2):
                        P.op("vector", lambda e, hg=hg: e.tensor_tensor(
                            out=o_s[:, hg * 512:(hg + 1) * 512].rearrange("p (h d) -> p h d", d=128),
                            in0=psO[hg][:, :, :],
                            in1=rls[:, hg * 4:(hg + 1) * 4].unsqueeze(2).to_broadcast([16, 4, 128]), op=ALU.mult),
                             r=[B_psO[hg], B_rls], a=[B_os])
                    if DEBUG and l == 0:
                        for nm, t_, B_ in (("d_cnew", cnew[:, :], B_cnew), ("d_cref", cref_s[:, :], B_crefs),
                                           ("d_rls", rls[:, :], B_rls),
                                           ("d_pTn", pT_new[:, :, :].rearrange("p h q -> p (h q)"), B_pTn),
                                           ("d_cc", ccache[:, :, :].rearrange("p j h -> p (j h)"), B_cc),
                                           ("d_lfn", lfn[:, :], B_lfn), ("d_cbn", cbn[:, :], B_cbn),
                                           ("d_os", o_s[:, :], B_os),
                                           ("d_pT", pT_all[:, :, :].rearrange("p h q -> p (h q)"), B_pT)):
                            P.dma("sync", lambda e, nm=nm, t_=t_, s=s: e.dma_start(out=DBG[nm][s], in_=t_), r=[B_])
                    for hg in range(2):
                        for hh in range(4):
                            h = hg * 4 + hh
                            P.op("tensor", lambda e, hh=hh, h=h: e.transpose(
                                psT[:, hh * 128:hh * 128 + 16], o_s[:, h * 128:(h + 1) * 128], ident[0:16, 0:16]),
                                 r=[B_os], w=[B_psT] if hh == 0 else [], a=[] if hh == 0 else [B_psT])
                        P.op("vector", lambda e, hg=hg, s=s: e.tensor_tensor(
                            out=s_ogT[:, hg * 4:(hg + 1) * 4, s * 16:(s + 1) * 16],
                            in0=psT[:, :].rearrange("p (h k) -> p h k", k=128)[:, :, 0:16],
                            in1=s_sgT[:, hg * 4:(hg + 1) * 4, s * 16:(s + 1) * 16], op=ALU.mult),
                             r=[B_psT, B_ssgT], a=[B_sogT])
                P.barrier()
                P.flush()

        def conv_layer(l, src_p, Bsrc_p, src_s, Bsrc_s, dst_p, Bdst_p, dst_s, Bdst_s):
            with ExitStack() as ph:
                def sbp(name, shape, dt):
                    return ph.enter_context(nc.sbuf_tensor(uq(name), list(shape), dt))

                def psp(name, shape, dt=F32):
                    return ph.enter_context(nc.psum_tensor(uq(name), list(shape), dt))

                w16 = sbp("cw16", [128, 8, 4 * D], BF16)
                B_w = Buf()
                wo16 = sbp("cwo16", [128, 8, D], BF16)
                B_wo = Buf()
                stage = Rot([sbp("cwst%d" % i, [128, 512], F32) for i in range(2)])
                gtile = sbp("cgtile", [128, D], F32)
                cwt = sbp("cwt", [128, 8, 3], F32)
                B_g = Buf()
                P.dma("sync", lambda e: e.dma_start(out=gtile[:], in_=g_norm[l].partition_broadcast(128)), a=[B_g])
                for j in range(3):
                    P.dma("sync", lambda e, j=j: e.dma_start(out=cwt[:, :, j],
                                                             in_=conv_w[l][j].rearrange("(c p) -> p c", p=128),
                                                             allow_slow_non_contiguous=True), a=[B_g])

                xm_rot = Rot([sbp("cxm%d" % i, [128, 4, D], F32) for i in range(2)])
                junk = sbp("cjunk", [128, D], BF16)
                Bjunk = Buf()
                strot = Rot([sbp("cst%d" % i, [128, 4], F32) for i in range(2)])
                hrot = Rot([sbp("ch%d" % i, [128, D], BF16) for i in range(2)])
                hTrot = Rot([sbp("chT%d" % i, [128, 8, 512], BF16) for i in range(2)])
                cu = sbp("cu", [128, 8, 2 + 512], F32)
                B_cu = [Buf() for _ in range(8)]
                cu_s = sbp("cu_s", [128, 8, 2, 2 + 16], F32)
                B_cus = Buf()
                c_sb_rot = Rot([sbp("c_sb%d" % i, [128, 512], F32) for i in range(2)])
                cv_rot = Rot([sbp("cvv%d" % i, [128, 512], F32) for i in range(2)])
                e_rot = Rot([sbp("ce%d" % i, [128, 512], F32) for i in range(2)])
                ogT_rot = Rot([sbp("cogT%d" % i, [128, 8, 512], BF16) for i in range(2)])
                y_rot = Rot([sbp("cy%d" % i, [128, D], F32) for i in range(2)])
                big = Rot(stage.tiles + y_rot.tiles + c_sb_rot.tiles + cv_rot.tiles + e_rot.tiles)
                big.bufs = stage.bufs + y_rot.bufs + c_sb_rot.bufs + cv_rot.bufs + e_rot.bufs
                load_weight_bf16(w_in[l], D, 4 * D, w16, B_w, big, 512)
                load_weight_bf16(w_out[l], D, D, wo16, B_wo, big, 512)
                pmm = Rot([psp("cpm%d" % i, [128, 512], F32) for i in range(6)], psum=True)
                ptr = Rot([psp("cpt%d" % i, [128, 1024], BF16) for i in range(2)], psum=True)

                for cc in range(8):
                    P.op("gpsimd", lambda e, cc=cc: e.memset(cu[:, cc, 0:2], 0.0), w=[B_cu[cc]])

                def macro(ntok, nsub, segs, cu_t, B_cu_l, is_s, m):
                    xm, Bxm = xm_rot.next()
                    if is_s:
                        P.dma("sync", lambda e: e.dma_start(out=xm[0:NS, 0, :], in_=src_s[:, :]), r=[Bsrc_s], w=[Bxm])
                    else:
                        P.dma("sync", lambda e: e.dma_start(
                            out=xm[:, :, :],
                            in_=src_p[m * 512:(m + 1) * 512, :].rearrange("(s p) f -> p s f", p=128)),
                              r=[Bsrc_p[m * 4 + k] for k in range(4)], w=[Bxm])
                    hT, BhT = hTrot.next()
                    for sub in range(nsub):
                        nt = min(128, ntok - sub * 128)
                        st_s, Bst = strot.next()
                        h16, Bh = hrot.next()
                        rms_tile(xm[:, sub, :], Bxm, nt, gtile, B_g, h16, Bh, junk, Bjunk, st_s, Bst)
                        pt, Bpt = ptr.next()
                        transpose_rows(h16, Bh, nt, 8, pt, Bpt, hT[:, :, sub * 128:sub * 128 + nt], BhT, "vector",
                                       acc=True)
                    yield "front"
                    ogT, BogT = ogT_rot.next()
                    for cc in range(8):
                        def mm_feat(f, pso, Bpso):
                            for kc in range(8):
                                P.op("tensor", lambda e, kc=kc: e.matmul(
                                    pso[:, 0:ntok], lhsT=w16[:, kc, f * 128:(f + 1) * 128], rhs=hT[:, kc, 0:ntok],
                                    start=(kc == 0), stop=(kc == 7)), r=[BhT, B_w], w=[Bpso])
                        pc, Bpc = pmm.next()
                        mm_feat(8 + cc, pc, Bpc)
                        pu, Bpu = pmm.next()
                        mm_feat(16 + cc, pu, Bpu)
                        c_sb, Bc_sb = c_sb_rot.next()
                        P.op("scalar", lambda e, pc=pc, c_sb=c_sb: e.copy(out=c_sb[:, 0:ntok], in_=pc[:, 0:ntok]),
                             r=[Bpc], w=[Bc_sb])
                        cvv, Bcv = cv_rot.next()
                        for (c0, ln, seg_i) in segs:
                            base = (lambda o, seg_i=seg_i, ln=ln, cc=cc: cu_t[:, cc, seg_i, o:o + ln]) if is_s else \
                                   (lambda o, ln=ln, cc=cc: cu_t[:, cc, o:o + ln])
                            P.op("vector", lambda e, pu=pu, c_sb=c_sb, c0=c0, ln=ln, base=base: e.tensor_tensor(
                                out=base(2), in0=pu[:, c0:c0 + ln], in1=c_sb[:, c0:c0 + ln], op=ALU.mult),
                                 r=[Bpu, Bc_sb], a=[B_cu_l[cc]])
                            P.op("scalar", lambda e, cvv=cvv, base=base, c0=c0, ln=ln, cc=cc: e.activation(
                                out=cvv[:, c0:c0 + ln], in_=base(2), func=AF.Identity, scale=cwt[:, cc, 2:3]),
                                 r=[B_cu_l[cc], B_g], a=[Bcv])
                            P.op("vector", lambda e, cvv=cvv, base=base, c0=c0, ln=ln, cc=cc: e.scalar_tensor_tensor(
                                out=cvv[:, c0:c0 + ln], in0=base(1), scalar=cwt[:, cc, 1:2], in1=cvv[:, c0:c0 + ln],
                                op0=ALU.mult, op1=ALU.add), r=[B_cu_l[cc], B_g, Bcv], a=[Bcv])
                            P.op("vector", lambda e, cvv=cvv, base=base, c0=c0, ln=ln, cc=cc: e.scalar_tensor_tensor(
                                out=cvv[:, c0:c0 + ln], in0=base(0), scalar=cwt[:, cc, 0:1], in1=cvv[:, c0:c0 + ln],
                                op0=ALU.mult, op1=ALU.add), r=[B_cu_l[cc], B_g, Bcv], a=[Bcv])
                        if not is_s:
                            P.op("gpsimd", lambda e, cc=cc: e.tensor_copy(out=cu_t[:, cc, 0:2],
                                                                          in_=cu_t[:, cc, 512:514]),
                                 r=[B_cu_l[cc]], w=[B_cu_l[cc]])
                        pg, Bpg = pmm.next()
                        mm_feat(24 + cc, pg, Bpg)
                        pb, Bpb = pmm.next()
                        mm_feat(cc, pb, Bpb)
                        ee, Be = e_rot.next()
                        P.op("scalar", lambda e, pg=pg, ee=ee: e.activation(out=ee[:, 0:ntok], in_=pg[:, 0:ntok],
                                                                           func=AF.Exp, scale=-1.0), r=[Bpg], w=[Be])
                        P.op("scalar", lambda e, ee=ee: e.activation(out=ee[:, 0:ntok], in_=ee[:, 0:ntok], func=AF.Ln,
                                                                    bias=1.0, scale=1.0), r=[Be], w=[Be])
                        P.op("scalar", lambda e, ee=ee: e.activation(out=ee[:, 0:ntok], in_=ee[:, 0:ntok], func=AF.Exp,
                                                                    scale=-1.0), r=[Be], w=[Be])
                        P.op("vector", lambda e, ee=ee, pg=pg: e.tensor_tensor(out=ee[:, 0:ntok], in0=pg[:, 0:ntok],
                                                                               in1=ee[:, 0:ntok], op=ALU.mult),
                             r=[Be, Bpg], w=[Be])
                        P.op("vector", lambda e, ee=ee, cvv=cvv: e.tensor_tensor(out=ee[:, 0:ntok], in0=ee[:, 0:ntok],
                                                                                 in1=cvv[:, 0:ntok], op=ALU.mult),
                             r=[Be, Bcv], w=[Be])
                        P.op("vector", lambda e, ee=ee, pb=pb, cc=cc: e.tensor_tensor(
                            out=ogT[:, cc, 0:ntok], in0=pb[:, 0:ntok], in1=ee[:, 0:ntok], op=ALU.mult),
                             r=[Be, Bpb], a=[BogT])
                    for sub in range(nsub):
                        nt = min(128, ntok - sub * 128)
                        y_t, By = y_rot.next()
                        for half in range(2):
                            pso, Bpso = pmm.next()
                            for kc in range(8):
                                P.op("tensor", lambda e, kc=kc, pso=pso, half=half, sub=sub, nt=nt: e.matmul(
                                    pso[0:nt, :], lhsT=ogT[:, kc, sub * 128:sub * 128 + nt],
                                    rhs=wo16[:, kc, half * 512:(half + 1) * 512], start=(kc == 0), stop=(kc == 7)),
                                     r=[BogT, B_wo], w=[Bpso])
                            P.op("vector", lambda e, pso=pso, half=half, sub=sub, nt=nt, y_t=y_t: e.tensor_tensor(
                                out=y_t[0:nt, half * 512:(half + 1) * 512], in0=pso[0:nt, :],
                                in1=xm[0:nt, sub, half * 512:(half + 1) * 512], op=ALU.add),
                                 r=[Bpso, Bxm], a=[By])
                        if is_s:
                            P.dma(STQ, lambda e, y_t=y_t: e.dma_start(out=dst_s[:, :], in_=y_t[0:NS, :]),
                                  r=[By], w=[Bdst_s])
                        else:
                            t = m * 4 + sub
                            P.dma(STQ, lambda e, y_t=y_t, t=t: e.dma_start(out=dst_p[t * 128:(t + 1) * 128, :],
                                                                              in_=y_t[:, :]), r=[By], w=[Bdst_p[t]])

                mg = [macro(512, 4, [(0, 512, 0)], cu, B_cu, False, m) for m in range(NM)]
                if with_sample:
                    def sample_pre():
                        for s_ in range(2):
                            for j in range(2):
                                P.dma("sync", lambda e, s_=s_, j=j: e.dma_start(
                                    out=cu_s[:, :, s_, j], in_=sconv[l][s_, j].rearrange("(c p) -> p c", p=128),
                                    allow_slow_non_contiguous=True), a=[B_cus])
                    sample_pre()
                    mg.append(macro(NS, 1, [(0, 16, 0), (16, 16, 1)], cu_s, [B_cus] * 8, True, 0))
                next(mg[0])
                for mi in range(len(mg)):
                    if mi + 1 < len(mg):
                        next(mg[mi + 1])
                    for _ in mg[mi]:
                        pass
                    if mi == NM - 1:
                        for j in range(2):
                            P.dma(STQ, lambda e, j=j: e.dma_start(out=o_cp[l][j].rearrange("(c p) -> p c", p=128),
                                                                     in_=cu[:, :, j], allow_slow_non_contiguous=True),
                                  r=B_cu)
                if with_sample:
                    for s_ in range(2):
                        for j in range(2):
                            P.dma(STQ, lambda e, s_=s_, j=j: e.dma_start(
                                out=o_cs[l][s_, j].rearrange("(c p) -> p c", p=128), in_=cu_s[:, :, s_, 16 + j],
                                allow_slow_non_contiguous=True), r=[B_cus])
                P.barrier()
                P.flush()

        B_y = [[Buf() for _ in range(NT)] for _ in range(2)]
        B_ys = [Buf(), Buf()]
        srcs_p = [xp, y_sc[0], y_sc[1], y_sc[0]]
        Bsrcs_p = [[Buf() for _ in range(NT)], B_y[0], B_y[1], B_y[0]]
        dsts_p = [y_sc[0], y_sc[1], y_sc[0], o_yp]
        Bdsts_p = [B_y[0], B_y[1], B_y[0], None]
        srcs_s = [xs, ys_sc[0], ys_sc[1], ys_sc[0]]
        Bsrcs_s = [Buf(), B_ys[0], B_ys[1], B_ys[0]]
        dsts_s = [ys_sc[0], ys_sc[1], ys_sc[0], o_ys]
        Bdsts_s = [B_ys[0], B_ys[1], B_ys[0], None]
        for l in range(NL):
            last = l == NL - 1
            dp = o_yp if last else dsts_p[l]
            ds = o_ys if last else dsts_s[l]
            Bdp = [Buf() for _ in range(NT)] if last else Bdsts_p[l]
            Bds = Buf() if last else Bdsts_s[l]
            if l % 2 == 0:
                fox_layer(l, srcs_p[l], Bsrcs_p[l], srcs_s[l], Bsrcs_s[l], dp, Bdp, ds, Bds)
            else:
                conv_layer(l, srcs_p[l], Bsrcs_p[l], srcs_s[l], Bsrcs_s[l], dp, Bdp, ds, Bds)
    build.last_nsem = P.nsem
    return nc


_CACHE = {}


def run_cores(S, PAST, NL, in_maps, with_sample=True):
    key = (S, PAST, NL, with_sample)
    nc = build(S, PAST, NL, with_sample)
    res = run_bass_kernel_spmd(nc, in_maps, core_ids=list(range(len(in_maps))))
    return res.results


def make_in_map(inp, core, S, PAST):
    b = core // 2
    f = lambda a: np.ascontiguousarray(a, dtype=np.float32)
    sl = slice(2 * core, 2 * core + 2)
    m = {"xp": f(inp["x_prompt"][b]), "xs": f(inp["x_sample"][sl].reshape(32, D))}
    m["norm0"], m["w_in0"], m["w_out0"] = f(inp["norm_l0"]), f(inp["w_in_l0"]), f(inp["w_out_l0"])
    m["b_f0"], m["qn0"], m["kn0"] = f(inp["b_f_l0"]), f(inp["qnorm_l0"]), f(inp["knorm_l0"])
    m["ck0"] = f(inp["cache_k_l0"][sl].reshape(2, PAST, D))
    m["cv0"] = f(inp["cache_v_l0"][sl].reshape(2, PAST, D))
    m["clf0"] = f(inp["cache_logf_l0"][sl])
    m["norm2"], m["w_in2"], m["w_out2"] = f(inp["norm_l2"]), f(inp["w_in_l2"]), f(inp["w_out_l2"])
    m["b_f2"], m["qn2"], m["kn2"] = f(inp["b_f_l2"]), f(inp["qnorm_l2"]), f(inp["knorm_l2"])
    m["ck2"] = f(inp["cache_k_l2"][sl].reshape(2, PAST, D))
    m["cv2"] = f(inp["cache_v_l2"][sl].reshape(2, PAST, D))
    m["clf2"] = f(inp["cache_logf_l2"][sl])
    m["norm1"], m["w_in1"], m["w_out1"] = f(inp["norm_l1"]), f(inp["w_in_l1"]), f(inp["w_out_l1"])
    m["cw1"], m["sc1"] = f(inp["conv_w_l1"]), f(inp["state_conv_l1"][sl])
    m["norm3"], m["w_in3"], m["w_out3"] = f(inp["norm_l3"]), f(inp["w_in_l3"]), f(inp["w_out_l3"])
    m["cw3"], m["sc3"] = f(inp["conv_w_l3"]), f(inp["state_conv_l3"][sl])
    return m


def assemble(results, B, S):
    nb = B
    DB = 2 * len(results)
    yp = np.stack([results[2 * b]["o_yp"] for b in range(nb)])
    ys = np.concatenate([r["o_ys"].reshape(2, 16, D) for r in results])
    outs = [yp, ys]
    for l in range(4):
        if l % 2 == 0:
            outs.append(np.stack([results[2 * b]["o_kp%d" % l].reshape(S, H, DH) for b in range(nb)]))
            outs.append(np.stack([results[2 * b]["o_vp%d" % l].reshape(S, H, DH) for b in range(nb)]))
            outs.append(np.stack([results[2 * b]["o_lfp%d" % l] for b in range(nb)]))
            outs.append(np.concatenate([r["o_ks%d" % l].reshape(2, 16, H, DH) for r in results]))
            outs.append(np.concatenate([r["o_vs%d" % l].reshape(2, 16, H, DH) for r in results]))
            outs.append(np.concatenate([r["o_lfs%d" % l].reshape(2, 16, H) for r in results]))
        else:
            outs.append(np.stack([results[2 * b]["o_cp%d" % l] for b in range(nb)]))
            outs.append(np.concatenate([r["o_cs%d" % l] for r in results]))
    return tuple(np.ascontiguousarray(o, dtype=np.float32) for o in outs)


def kernel(**inputs):
    inp = {k: np.asarray(v) for k, v in inputs.items()}
    B, S, _ = inp["x_prompt"].shape
    PAST = inp["cache_k_l0"].shape[1]
    in_maps = [make_in_map(inp, c, S, PAST) for c in range(8)]
    results = run_cores(S, PAST, 4, in_maps)
    return assemble(results, B, S)
```

```python
import numpy as np
from contextlib import ExitStack
import concourse.bass as bass
import concourse.mybir as mybir
from concourse.bass_utils import run_bass_kernel_spmd

F32 = mybir.dt.float32
BF16 = mybir.dt.bfloat16
AF = mybir.ActivationFunctionType
ALU = mybir.AluOpType
AX = mybir.AxisListType

D = 1024
H = 8
DH = 128
EPS = 1e-6
SCALE = DH ** -0.5
ENGS = ("sync", "scalar", "vector", "gpsimd", "tensor")
DEBUG = False
STQ = "gpsimd"
SEM_LIMIT = 8000
DMA_LIMIT = 1500


class Tok:
    __slots__ = ("eng", "sem", "val", "needed", "dma", "phase")

    def __init__(self, eng, phase, dma=False):
        self.eng = eng
        self.sem = None
        self.val = None
        self.needed = dma
        self.dma = dma
        self.phase = phase


class Buf:
    __slots__ = ("name", "w", "r", "rp", "psum")

    def __init__(self, name="", psum=False):
        self.name = name
        self.w = []
        self.r = []
        self.rp = []
        self.psum = psum


class Prog:
    def __init__(self, nc, es):
        self.nc = nc
        self.es = es
        self.nsem = 0
        self.phase = 0
        self.cur = {}
        self.waited = {e: {} for e in ENGS}
        self.ring = []
        self.ring_i = 0
        self.ops = {e: [] for e in ENGS}
        self.last = {e: None for e in ENGS}

    def _newsem(self):
        self.nsem += 1
        return self.es.enter_context(self.nc.semaphore("s%d" % self.nsem))

    def _deps(self, r, w, a):
        deps = []
        raw = []
        for b in r:
            deps.extend(b.w)
            raw.extend(b.w)
            if b.psum:
                deps.extend(b.r)
        for b in w:
            deps.extend(b.w)
            deps.extend(b.r)
        for b in a:
            if b.r:
                b.rp = b.r
                b.r = []
                b.w = []
            deps.extend(b.rp)
        return deps, raw

    def _commit(self, tok, r, w, a):
        for b in r:
            b.r.append(tok)
        for b in w:
            b.w = [tok]
            b.r = []
            b.rp = []
        for b in a:
            b.w.append(tok)

    def op(self, eng, fn, r=(), w=(), a=()):
        deps, raw = self._deps(r, w, a)
        tok = Tok(eng, self.phase)
        waits = []
        seen = set()
        for d in deps:
            if id(d) in seen or d.phase != self.phase:
                continue
            seen.add(id(d))
            if d.eng == eng and not d.dma:
                if eng == "tensor":
                    continue
                if not any(d is x for x in raw):
                    continue
            d.needed = True
            waits.append(d)
        self.ops[eng].append((fn, waits, tok))
        self._commit(tok, r, w, a)
        self.last[eng] = tok
        return tok

    def dma(self, eng, fn, r=(), w=(), a=()):
        deps, _ = self._deps(r, w, a)
        tok = Tok(eng, self.phase, dma=True)
        if len(self.ring) < 14:
            self.ring.append([self._newsem(), 0, None])
            slot = self.ring[-1]
        else:
            slot = self.ring[self.ring_i % len(self.ring)]
            self.ring_i += 1
        if slot[1] >= DMA_LIMIT:
            old = slot[2]
            slot[0] = self._newsem()
            slot[1] = 0
            if old is not None:
                deps.append(old)
            slot[2] = None
        if slot[2] is not None:
            deps.append(slot[2])
        slot[1] += 1
        tok.sem = slot[0]
        tok.val = 16 * slot[1]
        slot[2] = tok
        waits = []
        seen = set()
        for d in deps:
            if id(d) in seen or d.phase != self.phase:
                continue
            seen.add(id(d))
            d.needed = True
            waits.append(d)
        self.ops[eng].append((fn, waits, tok))
        self._commit(tok, r, w, a)
        return tok

    def barrier(self):
        toks = [t for t in self.last.values() if t is not None and t.phase == self.phase]
        toks += [sl[2] for sl in self.ring if sl[2] is not None and sl[2].phase == self.phase]
        for t in toks:
            t.needed = True
        for e in ENGS:
            self.ops[e].append((None, list(toks), None))

    def flush(self):
        for e in ENGS:
            for fn, waits, tok in self.ops[e]:
                if tok is None or tok.dma or not tok.needed:
                    continue
                cur = self.cur.get(e)
                if cur is None or cur[1] >= SEM_LIMIT:
                    cur = [self._newsem(), 0]
                    self.cur[e] = cur
                cur[1] += 1
                tok.sem = cur[0]
                tok.val = cur[1]
        ops = self.ops
        waited = self.waited

        def run(ename):
            def body(eng):
                wd = waited[ename]
                for fn, waits, tok in ops[ename]:
                    best = {}
                    for d in waits:
                        key = id(d.sem)
                        if key not in best or d.val > best[key].val:
                            best[key] = d
                    for key, d in best.items():
                        if wd.get(key, 0) >= d.val:
                            continue
                        eng.wait_ge(d.sem, d.val)
                        wd[key] = d.val
                    if fn is None:
                        continue
                    ins = fn(eng)
                    if tok is not None and tok.needed:
                        ins.then_inc(tok.sem, 16 if tok.dma else 1)
            return body

        with self.nc.Block() as blk:
            blk.sync(run("sync"))
            blk.scalar(run("scalar"))
            blk.vector(run("vector"))
            blk.gpsimd(run("gpsimd"))
            blk.tensor(run("tensor"))
        self.ops = {e: [] for e in ENGS}
        self.phase += 1


class Rot:
    def __init__(self, tiles, psum=False):
        self.tiles = tiles
        self.bufs = [Buf(psum=psum) for _ in tiles]
        self.i = 0

    def next(self):
        k = self.i % len(self.tiles)
        self.i += 1
        return self.tiles[k], self.bufs[k]


def build(S, PAST, NL=4, with_sample=True):
    NT = S // 128
    NM = S // 512
    NQ = S // 256
    PT = PAST // 128
    NS = 32
    WIN_F = 4 * D + H
    assert PT * 16 <= 512 and PT % 4 == 0

    nc = bass.Bass("TRN2", target_bir_lowering=False)
    _uid = [0]

    def uq(name):
        _uid[0] += 1
        return "%s_%d" % (name, _uid[0])

    def din(name, shape, dt=F32):
        return nc.dram_tensor(name, list(shape), dt, kind="ExternalInput").ap()

    def dout(name, shape, dt=F32):
        return nc.dram_tensor(name, list(shape), dt, kind="ExternalOutput").ap()

    def dint(name, shape, dt=F32):
        return nc.dram_tensor(name, list(shape), dt, kind="Internal").ap()

    xp = din("xp", [S, D])
    xs = din("xs", [NS, D])
    w_in, w_out, g_norm, b_f, g_q, g_k, conv_w = {}, {}, {}, {}, {}, {}, {}
    ck, cv, clf, sconv = {}, {}, {}, {}
    for l in range(4):
        fox = l % 2 == 0
        g_norm[l] = din("norm%d" % l, [D])
        w_in[l] = din("w_in%d" % l, [D, WIN_F if fox else 4 * D])
        w_out[l] = din("w_out%d" % l, [D, D])
        if fox:
            b_f[l] = din("b_f%d" % l, [H])
            g_q[l] = din("qn%d" % l, [DH])
            g_k[l] = din("kn%d" % l, [DH])
            ck[l] = din("ck%d" % l, [2, PAST, D])
            cv[l] = din("cv%d" % l, [2, PAST, D])
            clf[l] = din("clf%d" % l, [2, PAST, H])
        else:
            conv_w[l] = din("cw%d" % l, [3, D])
            sconv[l] = din("sc%d" % l, [2, 2, D])

    o_yp = dout("o_yp", [S, D])
    o_ys = dout("o_ys", [NS, D])
    o_kp, o_vp, o_lfp, o_ks, o_vs, o_lfs, o_cp, o_cs = {}, {}, {}, {}, {}, {}, {}, {}
    for l in range(4):
        if l % 2 == 0:
            o_kp[l] = dout("o_kp%d" % l, [S, D])
            o_vp[l] = dout("o_vp%d" % l, [S, D])
            o_lfp[l] = dout("o_lfp%d" % l, [S, H])
            o_ks[l] = dout("o_ks%d" % l, [NS, D])
            o_vs[l] = dout("o_vs%d" % l, [NS, D])
            o_lfs[l] = dout("o_lfs%d" % l, [NS, H])
        else:
            o_cp[l] = dout("o_cp%d" % l, [2, D])
            o_cs[l] = dout("o_cs%d" % l, [2, 2, D])

    DBG = {}
    if DEBUG:
        DBG["d_cnew"] = dout("d_cnew", [2, 16, H])
        DBG["d_cref"] = dout("d_cref", [2, 128, H])
        DBG["d_rls"] = dout("d_rls", [2, 16, H])
        DBG["d_pTn"] = dout("d_pTn", [2, 16, H * 16])
        DBG["d_cc"] = dout("d_cc", [2, 128, PT * H])
        DBG["d_lfn"] = dout("d_lfn", [2, 16, H])
        DBG["d_cbn"] = dout("d_cbn", [2, 16, H])
        DBG["d_os"] = dout("d_os", [2, 16, D], BF16)
        DBG["d_pT"] = dout("d_pT", [2, 128, H * PT * 16])
    y_sc = [dint("y_a", [S, D]), dint("y_b", [S, D])]
    ys_sc = [dint("ys_a", [NS, D]), dint("ys_b", [NS, D])]
    qT_s = dint("qT_s", [H, 128, S], BF16)
    kT_s = dint("kT_s", [H, 128, S], BF16)
    sgT_s = dint("sgT_s", [H, 128, S], BF16)
    ogT_s = dint("ogT_s", [H, 128, S], BF16)
    v16_s = dint("v16_s", [H, 128, NT * 129], BF16)

    es = ExitStack()
    with es:
        P = Prog(nc, es)

        def sb(name, shape, dt):
            return es.enter_context(nc.sbuf_tensor(uq(name), list(shape), dt))

        ident = sb("ident", [128, 128], BF16)
        identf = sb("identf", [128, 128], F32)
        tri = sb("tri", [128, 128], F32)
        trix = sb("trix", [128, 128], F32)
        trib = sb("trib", [128, 128], BF16)
        sel_last = sb("sel_last", [128, 128], F32)
        ones_f = sb("ones_f", [128, 128], F32)
        sel15 = sb("sel15", [128, 1], F32)
        s_qT = sb("s_qT", [128, H, NS], BF16)
        s_kT = sb("s_kT", [128, H, NS], BF16)
        s_sgT = sb("s_sgT", [128, H, NS], BF16)
        s_ogT = sb("s_ogT", [128, H, NS], BF16)
        s_lf = sb("s_lf", [128, H], F32)
        c_all = sb("c_all", [128, NT, H], F32)
        cref_all = sb("cref_all", [128, NT, H], F32)

        def mk_consts(e):
            def sel(t, pat, op, base, cm):
                e.memset(t[:], 1.0)
                return e.affine_select(out=t[:], in_=t[:], pattern=pat, compare_op=op, fill=0.0, base=base,
                                       channel_multiplier=cm)
            sel(ident, [[1, 128]], ALU.is_equal, 0, -1)
            sel(identf, [[1, 128]], ALU.is_equal, 0, -1)
            sel(tri, [[1, 128]], ALU.is_ge, 0, -1)
            sel(trix, [[1, 128]], ALU.is_gt, 0, -1)
            sel(trib, [[1, 128]], ALU.is_ge, 0, -1)
            sel(sel_last, [[0, 128]], ALU.is_equal, -127, 1)
            sel(sel15, [[0, 1]], ALU.is_equal, -15, 1)
            return e.memset(ones_f[:], 1.0)

        def consts_phase():
            P.op("gpsimd", mk_consts)
            P.barrier()
            P.flush()

        consts_phase()

        def load_weight_bf16(wdram, rows, cols, w16, B_w, stage_rot, col_chunk):
            k = 0
            for kc in range(rows // 128):
                c0 = 0
                while c0 < cols:
                    st, bst = stage_rot.next()
                    cw_ = min(int(st.shape[1]), cols - c0)
                    P.dma("sync", lambda e, st=st, kc=kc, c0=c0, cw_=cw_: e.dma_start(
                        out=st[:, 0:cw_], in_=wdram[kc * 128:(kc + 1) * 128, c0:c0 + cw_]), w=[bst])
                    if k % 2 == 0:
                        P.op("vector", lambda e, st=st, kc=kc, c0=c0, cw_=cw_: e.tensor_copy(
                            out=w16[:, kc, c0:c0 + cw_], in_=st[:, 0:cw_]), r=[bst], a=[B_w])
                    else:
                        P.op("scalar", lambda e, st=st, kc=kc, c0=c0, cw_=cw_: e.copy(
                            out=w16[:, kc, c0:c0 + cw_], in_=st[:, 0:cw_]), r=[bst], a=[B_w])
                    k += 1
                    c0 += cw_

        def rms_tile(x_t, Bx, nt, gtile, Bg, h16, Bh, junk, Bjunk, st_small, Bst):
            P.op("vector", lambda e: e.memset(st_small[0:nt, 0:1], 0.0), w=[Bst])
            P.op("scalar", lambda e: e.activation(out=junk[0:nt, :], in_=x_t[0:nt, :], func=AF.Square,
                                                  accum_out=st_small[0:nt, 0:1]), r=[Bx, Bst], w=[Bjunk, Bst])
            P.op("scalar", lambda e: e.activation(out=st_small[0:nt, 1:2], in_=st_small[0:nt, 0:1], func=AF.Ln,
                                                  bias=EPS, scale=1.0 / D), r=[Bst], w=[Bst])
            P.op("scalar", lambda e: e.activation(out=st_small[0:nt, 2:3], in_=st_small[0:nt, 1:2], func=AF.Exp,
                                                  scale=-0.5), r=[Bst], w=[Bst])
            P.op("vector", lambda e: e.scalar_tensor_tensor(out=h16[0:nt, :], in0=x_t[0:nt, :],
                                                            scalar=st_small[0:nt, 2:3], in1=gtile[0:nt, :],
                                                            op0=ALU.mult, op1=ALU.mult),
                 r=[Bx, Bst, Bg], w=[Bh])

        def transpose_rows(src16, Bsrc, nt, nchunks, pst, Bpst, dst_ap, Bdst, evac_eng, acc=False):
            for c in range(nchunks):
                P.op("tensor", lambda e, c=c: e.transpose(pst[:, c * 128:c * 128 + nt],
                                                          src16[0:nt, c * 128:(c + 1) * 128], ident[0:nt, 0:nt]),
                     r=[Bsrc], w=[Bpst] if c == 0 else [], a=[] if c == 0 else [Bpst])
            src_ap = pst[:, 0:nchunks * 128].rearrange("p (c t) -> p c t", t=128)[:, :, 0:nt]
            kw = dict(r=[Bpst], a=[Bdst]) if acc else dict(r=[Bpst], w=[Bdst])
            if evac_eng == "scalar":
                return P.op("scalar", lambda e: e.copy(out=dst_ap, in_=src_ap), **kw)
            return P.op(evac_eng, lambda e: e.tensor_copy(out=dst_ap, in_=src_ap), **kw)

        def fox_layer(l, src_p, Bsrc_p, src_s, Bsrc_s, dst_p, Bdst_p, dst_s, Bdst_s):
            B_qT = [Buf() for _ in range(NQ)]
            B_v16 = [Buf() for _ in range(NT)]
            B_ogT = [Buf() for _ in range(NM)]
            B_vs_out = Buf()
            B_sqT, B_skT, B_ssgT, B_sogT, B_slf, B_call, B_cref = (Buf() for _ in range(7))

            with ExitStack() as ph:
                def sbp(name, shape, dt):
                    return ph.enter_context(nc.sbuf_tensor(uq(name), list(shape), dt))

                def psp(name, shape, dt=F32):
                    return ph.enter_context(nc.psum_tensor(uq(name), list(shape), dt))

                w16 = sbp("w16", [128, 8, WIN_F], BF16)
                B_w = Buf()
                stage = Rot([sbp("wst%d" % i, [128, 1026], F32) for i in range(2)])
                gtile = sbp("gtile", [128, D], F32)
                gq = sbp("gq", [128, 512], F32)
                gk = sbp("gk", [128, 512], F32)
                bft = sbp("bft", [128, H], F32)
                B_g = Buf()
                P.dma("sync", lambda e: e.dma_start(out=gtile[:], in_=g_norm[l].partition_broadcast(128)), a=[B_g])
                for j in range(4):
                    P.dma("sync", lambda e, j=j: e.dma_start(out=gq[:, j * 128:(j + 1) * 128],
                                                             in_=g_q[l].partition_broadcast(128)), a=[B_g])
                    P.dma("sync", lambda e, j=j: e.dma_start(out=gk[:, j * 128:(j + 1) * 128],
                                                             in_=g_k[l].partition_broadcast(128)), a=[B_g])
                P.dma("sync", lambda e: e.dma_start(out=bft[:], in_=b_f[l].partition_broadcast(128)), a=[B_g])

                xrot = Rot([sbp("x%d" % i, [128, D], F32) for i in range(3)])
                junk = sbp("junk", [128, D], BF16)
                Bjunk = Buf()
                strot = Rot([sbp("st%d" % i, [128, 4], F32) for i in range(2)])
                hrot = Rot([sbp("h%d" % i, [128, D], BF16) for i in range(2)])
                hTrot = Rot([sbp("hT%d" % i, [128, 8, 128], BF16) for i in range(2)])
                sqrot = Rot([sbp("sq%d" % i, [128, 512], F32) for i in range(2)])
                ssrot = Rot([sbp("ss%d" % i, [128, 12], F32) for i in range(4)])
                kfrot = Rot([sbp("kf%d" % i, [128, D], F32) for i in range(2)])
                vfrot = Rot([sbp("vf%d" % i, [128, D], F32) for i in range(2)])
                q16rot = Rot([sbp("q16_%d" % i, [128, D], BF16) for i in range(2)])
                k16rot = Rot([sbp("k16_%d" % i, [128, D], BF16) for i in range(2)])
                sg16rot = Rot([sbp("sg16_%d" % i, [128, D], BF16) for i in range(2)])
                v16rot = Rot([sbp("v16_%d" % i, [128, H, 129], BF16) for i in range(2)])
                erot = Rot([sbp("e%d" % i, [128, 512], F32) for i in range(2)])
                qTst = Rot([sbp("qTst%d" % i, [128, H, 256], BF16) for i in range(2)])
                kTst = Rot([sbp("kTst%d" % i, [128, H, 256], BF16) for i in range(2)])
                sgTst = Rot([sbp("sgTst%d" % i, [128, H, 256], BF16) for i in range(2)])
                z_all = sbp("z_all", [128, NT + 1, H], F32)
                lf_all = sbp("lf_all", [128, NT + 1, H], F32)
                tsum = sbp("tsum", [1, NT + 1, H], F32)
                B_z = Buf()
                B_lf = Buf()
                B_ts = Buf()
                big = Rot(stage.tiles + kfrot.tiles + vfrot.tiles + xrot.tiles)
                big.bufs = stage.bufs + kfrot.bufs + vfrot.bufs + xrot.bufs
                load_weight_bf16(w_in[l], D, WIN_F, w16, B_w, big, 1026)
                pmm = Rot([psp("pmm%d" % i, [128, 512], F32) for i in range(4)], psum=True)
                ptr = Rot([psp("ptr%d" % i, [128, 1024], BF16) for i in range(3)], psum=True)
                pz = psp("pz", [128, 512], F32)
                B_pz = Buf(psum=True)
                assert NT * H <= 512

                for i in range(2):
                    P.op("gpsimd", lambda e, t_=v16rot.tiles[i]: e.memset(t_[:, :, 128:129], 1.0),
                         a=[v16rot.bufs[i]])
                P.op("gpsimd", lambda e: e.memset(z_all[:, :, :], 0.0), w=[B_z])

                tiles = [(t, 128) for t in range(NT)] + ([(NT, NS)] if with_sample else [])
                cur_box = [None]

                def tile_body(t, nt):
                    is_s = t == NT
                    x_t, Bx = xrot.next()
                    if is_s:
                        P.dma("sync", lambda e, x_t=x_t: e.dma_start(out=x_t[0:NS, :], in_=src_s[:, :]),
                              r=[Bsrc_s], w=[Bx])
                    else:
                        P.dma("sync", lambda e, x_t=x_t, t=t: e.dma_start(out=x_t[:, :],
                                                                          in_=src_p[t * 128:(t + 1) * 128, :]),
                              r=[Bsrc_p[t]], w=[Bx])
                    st_s, Bst = strot.next()
                    h16, Bh = hrot.next()
                    rms_tile(x_t, Bx, nt, gtile, B_g, h16, Bh, junk, Bjunk, st_s, Bst)
                    yield "fa"
                    hT, BhT = hTrot.next()
                    pt, Bpt = ptr.next()
                    transpose_rows(h16, Bh, nt, 8, pt, Bpt, hT[:, :, 0:nt], BhT, "vector")
                    yield "fb"

                    def mm_slab(c0, ncols, pso, Bpso):
                        for kc in range(8):
                            P.op("tensor", lambda e, kc=kc: e.matmul(pso[0:nt, 0:ncols], lhsT=hT[:, kc, 0:nt],
                                                                      rhs=w16[:, kc, c0:c0 + ncols],
                                                                      start=(kc == 0), stop=(kc == 7)),
                                 r=[BhT, B_w], w=[Bpso])

                    kf, Bkf = kfrot.next()
                    vf, Bvf = vfrot.next()
                    q16, Bq16 = q16rot.next()
                    k16, Bk16 = k16rot.next()
                    sg16, Bsg16 = sg16rot.next()
                    v16t, Bv16 = v16rot.next()
                    for which in range(2):
                        for half in range(2):
                            pso, Bpso = pmm.next()
                            mm_slab(which * D + half * 512, 512, pso, Bpso)
                            sq, Bsq = sqrot.next()
                            ss, Bss = ssrot.next()
                            P.op("scalar", lambda e, pso=pso, sq=sq: e.activation(out=sq[0:nt, :], in_=pso[0:nt, :],
                                                                                  func=AF.Square),
                                 r=[Bpso], w=[Bsq])
                            P.op("vector", lambda e, sq=sq, ss=ss: e.tensor_reduce(
                                out=ss[0:nt, 0:4], in_=sq[0:nt, :].rearrange("p (h d) -> p h d", d=128),
                                axis=AX.X, op=ALU.add), r=[Bsq], w=[Bss])
                            P.op("scalar", lambda e, ss=ss: e.activation(out=ss[0:nt, 4:8], in_=ss[0:nt, 0:4],
                                                                         func=AF.Ln, bias=EPS, scale=1.0 / DH),
                                 r=[Bss], w=[Bss])
                            P.op("scalar", lambda e, ss=ss: e.activation(out=ss[0:nt, 8:12], in_=ss[0:nt, 4:8],
                                                                         func=AF.Exp, scale=-0.5),
                                 r=[Bss], w=[Bss])
                            for hh in range(4):
                                c0_ = half * 512 + hh * 128
                                if which == 0:
                                    P.op("vector", lambda e, pso=pso, ss=ss, hh=hh, c0_=c0_: e.scalar_tensor_tensor(
                                        out=q16[0:nt, c0_:c0_ + 128], in0=pso[0:nt, hh * 128:(hh + 1) * 128],
                                        scalar=ss[0:nt, 8 + hh:9 + hh], in1=gq[0:nt, 0:128], op0=ALU.mult,
                                        op1=ALU.mult), r=[Bpso, Bss, B_g], a=[Bq16])
                                else:
                                    P.op("vector", lambda e, pso=pso, ss=ss, hh=hh, c0_=c0_: e.scalar_tensor_tensor(
                                        out=kf[0:nt, c0_:c0_ + 128], in0=pso[0:nt, hh * 128:(hh + 1) * 128],
                                        scalar=ss[0:nt, 8 + hh:9 + hh], in1=gk[0:nt, 0:128], op0=ALU.mult,
                                        op1=ALU.mult), r=[Bpso, Bss, B_g], a=[Bkf])
                    P.op("scalar", lambda e: e.copy(out=k16[0:nt, :], in_=kf[0:nt, :]), r=[Bkf], w=[Bk16])
                    yield "A"
                    for half in range(2):
                        pso, Bpso = pmm.next()
                        mm_slab(2 * D + half * 512, 512, pso, Bpso)
                        P.op("scalar", lambda e, pso=pso, half=half: e.copy(out=vf[0:nt, half * 512:(half + 1) * 512],
                                                                           in_=pso[0:nt, :]), r=[Bpso], a=[Bvf])
                        P.op("vector", lambda e, pso=pso, half=half: e.tensor_copy(
                            out=v16t[0:nt, half * 4:(half + 1) * 4, 0:128],
                            in_=pso[0:nt, :].rearrange("p (h d) -> p h d", d=128)), r=[Bpso], a=[Bv16])
                    for half in range(2):
                        pso, Bpso = pmm.next()
                        mm_slab(3 * D + half * 512, 512, pso, Bpso)
                        ee, Be = erot.next()
                        P.op("scalar", lambda e, pso=pso, ee=ee: e.activation(out=ee[0:nt, :], in_=pso[0:nt, :],
                                                                             func=AF.Exp, scale=-1.0),
                             r=[Bpso], w=[Be])
                        P.op("scalar", lambda e, ee=ee: e.activation(out=ee[0:nt, :], in_=ee[0:nt, :], func=AF.Ln,
                                                                    bias=1.0, scale=1.0), r=[Be], w=[Be])
                        P.op("scalar", lambda e, ee=ee: e.activation(out=ee[0:nt, :], in_=ee[0:nt, :], func=AF.Exp,
                                                                    scale=-1.0), r=[Be], w=[Be])
                        P.op("vector", lambda e, pso=pso, ee=ee, half=half: e.tensor_tensor(
                            out=sg16[0:nt, half * 512:(half + 1) * 512], in0=pso[0:nt, :], in1=ee[0:nt, :],
                            op=ALU.mult), r=[Bpso, Be], a=[Bsg16])
                        if half == 0:
                            yield "B1"
                    pso, Bpso = pmm.next()
                    mm_slab(4 * D, H, pso, Bpso)
                    P.op("scalar", lambda e, pso=pso, t=t: e.copy(out=z_all[0:nt, t, :], in_=pso[0:nt, 0:H]),
                         r=[Bpso], a=[B_z])

                    if is_s:
                        P.dma(STQ, lambda e: e.dma_start(out=o_ks[l][:, :], in_=kf[0:NS, :]), r=[Bkf])
                        P.dma(STQ, lambda e: e.dma_start(out=o_vs[l][:, :], in_=vf[0:NS, :]), r=[Bvf],
                              w=[B_vs_out])
                    else:
                        P.dma(STQ, lambda e, t=t: e.dma_start(out=o_kp[l][t * 128:(t + 1) * 128, :], in_=kf[:, :]),
                              r=[Bkf])
                        P.dma(STQ, lambda e, t=t: e.dma_start(out=o_vp[l][t * 128:(t + 1) * 128, :], in_=vf[:, :]),
                              r=[Bvf])
                        P.dma(STQ, lambda e, t=t: e.dma_start(
                            out=v16_s[:, :, t * 129:(t + 1) * 129].rearrange("h p c -> p h c"), in_=v16t[:, :, :]),
                              r=[Bv16], w=[B_v16[t]])
                    yield "B2"
                    if is_s:
                        for k_, (src, Bs_, dst, Bd) in enumerate(((q16, Bq16, s_qT, B_sqT), (k16, Bk16, s_kT, B_skT),
                                                                  (sg16, Bsg16, s_sgT, B_ssgT))):
                            pt, Bpt = ptr.next()
                            transpose_rows(src, Bs_, nt, 8, pt, Bpt, dst[:, :, 0:nt], Bd, "vector")
                    else:
                        sub = t % 2
                        if sub == 0:
                            cur_box[0] = (qTst.next(), kTst.next(), sgTst.next())
                        cur = cur_box[0]
                        for k_, (src, Bs_) in enumerate(((q16, Bq16), (k16, Bk16), (sg16, Bsg16))):
                            pt, Bpt = ptr.next()
                            stt, Bstt = cur[k_]
                            transpose_rows(src, Bs_, nt, 8, pt, Bpt, stt[:, :, sub * 128:(sub + 1) * 128], Bstt,
                                           "scalar" if k_ == 1 else "vector", acc=True)
                        if sub == 1:
                            m = t // 2
                            for k_, dst in enumerate((qT_s, kT_s, sgT_s)):
                                stt, Bstt = cur[k_]
                                P.dma(STQ, lambda e, dst=dst, stt=stt, m=m: e.dma_start(
                                    out=dst[:, :, m * 256:(m + 1) * 256].rearrange("h p c -> p h c"), in_=stt[:, :, :]),
                                      r=[Bstt], a=[B_qT[m]])


                gens = [tile_body(t_, nt_) for (t_, nt_) in tiles]
                ng = len(gens)

                def adv(ti, want):
                    if 0 <= ti < ng:
                        got = next(gens[ti])
                        assert got == want, (got, want)

                def fin_(ti):
                    if 0 <= ti < ng:
                        for _ in gens[ti]:
                            pass
                adv(0, "fa")
                adv(0, "fb")
                adv(1, "fa")
                for ti in range(ng):
                    adv(ti, "A")
                    adv(ti + 2, "fa")
                    fin_(ti - 1)
                    adv(ti, "B1")
                    adv(ti + 1, "fb")
                    adv(ti, "B2")
                fin_(ng - 1)
                nz = NT + (1 if with_sample else 0)
                P.op("vector", lambda e: e.tensor_tensor(out=z_all[:, 0:nz, :], in0=z_all[:, 0:nz, :],
                                                         in1=bft[:, :].unsqueeze(1).to_broadcast([128, nz, H]),
                                                         op=ALU.add), r=[B_z, B_g], w=[B_z])
                P.op("scalar", lambda e: e.activation(out=lf_all[:, 0:nz, :], in_=z_all[:, 0:nz, :], func=AF.Exp,
                                                      scale=-1.0), r=[B_z], w=[B_lf])
                P.op("scalar", lambda e: e.activation(out=lf_all[:, 0:nz, :], in_=lf_all[:, 0:nz, :], func=AF.Ln,
                                                      bias=1.0, scale=1.0), r=[B_lf], w=[B_lf])
                P.op("vector", lambda e: e.tensor_scalar(out=lf_all[:, 0:nz, :], in0=lf_all[:, 0:nz, :],
                                                         scalar1=-1.0, scalar2=None, op0=ALU.mult),
                     r=[B_lf], w=[B_lf])
                nchunk = max(1, NT // 8)
                for c0 in range(0, NT, nchunk):
                    P.dma(STQ, lambda e, c0=c0: e.dma_start(
                        out=o_lfp[l][c0 * 128:(c0 + nchunk) * 128, :].rearrange("(t p) h -> p t h", p=128),
                        in_=lf_all[:, c0:c0 + nchunk, :]), r=[B_lf])
                if with_sample:
                    P.dma(STQ, lambda e: e.dma_start(out=o_lfs[l][:, :], in_=lf_all[0:NS, NT, :]), r=[B_lf])
                    P.op("gpsimd", lambda e: e.tensor_copy(out=s_lf[0:NS, :], in_=lf_all[0:NS, NT, :]),
                         r=[B_lf], w=[B_slf])
                NF = NT * H
                lf_flat = lf_all[:, 0:NT, :].rearrange("p t h -> p (t h)")
                P.op("tensor", lambda e: e.matmul(pz[0:1, 0:NF], lhsT=ones_f[:, 0:1], rhs=lf_flat,
                                                  start=True, stop=True), r=[B_lf], w=[B_pz])
                P.op("vector", lambda e: e.memset(tsum[:, 0, :], 0.0), w=[B_ts])
                P.op("vector", lambda e: e.tensor_copy(out=tsum[:, 1:NT + 1, :].rearrange("p t h -> p (t h)"),
                                                       in_=pz[0:1, 0:NF]), r=[B_pz], a=[B_ts])
                for t in range(1, NT):
                    P.op("vector", lambda e, t=t: e.tensor_tensor(out=tsum[:, t, :], in0=tsum[:, t, :],
                                                                  in1=tsum[:, t - 1, :], op=ALU.add),
                         r=[B_ts], w=[B_ts])
                P.op("tensor", lambda e: e.matmul(pz[:, 0:NF], lhsT=tri[:, :], rhs=lf_flat, start=True, stop=False),
                     r=[B_lf, B_ts], w=[B_pz])
                P.op("tensor", lambda e: e.matmul(pz[:, 0:NF], lhsT=ones_f[0:1, :],
                                                  rhs=tsum[:, 0:NT, :].rearrange("p t h -> p (t h)"),
                                                  start=False, stop=True), r=[B_ts], w=[B_pz])
                P.op("vector", lambda e: e.tensor_copy(out=c_all[:, :, :].rearrange("p t h -> p (t h)"),
                                                       in_=pz[:, 0:NF]), r=[B_pz], w=[B_call])
                P.op("tensor", lambda e: e.matmul(pz[:, 0:NF], lhsT=sel_last[:, :],
                                                  rhs=c_all[:, :, :].rearrange("p t h -> p (t h)"),
                                                  start=True, stop=True), r=[B_call], w=[B_pz])
                P.op("vector", lambda e: e.tensor_copy(out=cref_all[:, :, :].rearrange("p t h -> p (t h)"),
                                                       in_=pz[:, 0:NF]), r=[B_pz], w=[B_cref])
                P.barrier()
                P.flush()

            with ExitStack() as ph:
                def sbp(name, shape, dt):
                    return ph.enter_context(nc.sbuf_tensor(uq(name), list(shape), dt))

                def psp(name, shape, dt=F32):
                    return ph.enter_context(nc.psum_tensor(uq(name), list(shape), dt))

                hd = Rot([(sbp("KT%d" % i, [128, S], BF16), sbp("QT%d" % i, [128, S], BF16),
                           sbp("SG%d" % i, [128, S], BF16), sbp("V%d" % i, [128, NT * 129], BF16)) for i in range(2)])
                cbrot = Rot([sbp("cb%d" % i, [128, NT], F32) for i in range(2)])
                prot = Rot([sbp("p%d" % i, [128, 512], BF16) for i in range(4)])
                pss = Rot([psp("pss%d" % i, [128, 512], F32) for i in range(3)], psum=True)
                pso_r = Rot([psp("pso%d" % i, [128, 2, 256], F32) for i in range(4)], psum=True)
                pst = psp("pst", [128, 512], BF16)
                B_pst = Buf(psum=True)
                rlrot = Rot([sbp("rl%d" % i, [128, 4], F32) for i in range(2)])
                o16rot = Rot([sbp("o16_%d" % i, [128, 512], BF16) for i in range(2)])
                ogrot = Rot([sbp("og%d" % i, [128, 512], BF16) for i in range(2)])

                LA = 2
                blocks = [(h, i, j) for h in range(H) for i in range(NM) for j in range(4 * (i + 1))]
                heads = {}
                qstate = {}
                bstate = {}
                deferred = []

                def load_head(h):
                    (KT, QT, SG, V), Bhd = hd.next()
                    P.dma("sync", lambda e: e.dma_start(out=KT[:, :], in_=kT_s[h]), a=[Bhd])
                    P.dma("sync", lambda e: e.dma_start(out=QT[:, :], in_=qT_s[h]), a=[Bhd])
                    P.dma("sync", lambda e: e.dma_start(out=SG[:, :], in_=sgT_s[h]), a=[Bhd])
                    P.dma("sync", lambda e: e.dma_start(out=V[:, :], in_=v16_s[h]), a=[Bhd])
                    heads[h] = (KT, QT, SG, V, Bhd)

                def stage_a(h, i, j):
                    KT, QT, SG, V, Bhd = heads[h]
                    nk = 4 * (i + 1)
                    if j == 0:
                        cb, Bcb = cbrot.next()
                        P.op("vector", lambda e: e.tensor_scalar(
                            out=cb[:, 0:nk], in0=c_all[:, 0:nk, h], scalar1=-1.0,
                            scalar2=cref_all[:, 4 * i + 3, h:h + 1], op0=ALU.mult, op1=ALU.add), w=[Bcb])
                        qstate[(h, i)] = (cb, Bcb, [pso_r.next(), pso_r.next()])
                    cb, Bcb, po = qstate[(h, i)]
                    m = j - 4 * i
                    c_lo = 128 * m if m > 0 else 0
                    s_ps, Bs_ps = pss.next()
                    P.op("tensor", lambda e: e.matmul(
                        s_ps[:, c_lo:512], lhsT=KT[:, j * 128:(j + 1) * 128],
                        rhs=QT[:, i * 512 + c_lo:(i + 1) * 512], start=True, stop=True), r=[Bhd], w=[Bs_ps])
                    pt_, Bp = prot.next()
                    P.op("scalar", lambda e: e.activation(
                        out=pt_[:, c_lo:512], in_=s_ps[:, c_lo:512], func=AF.Exp, bias=cb[:, j:j + 1],
                        scale=SCALE), r=[Bs_ps, Bcb], w=[Bp])
                    if m >= 0:
                        P.op("gpsimd", lambda e: e.tensor_tensor(
                            out=pt_[:, c_lo:c_lo + 128], in0=pt_[:, c_lo:c_lo + 128], in1=trib[:, :],
                            op=ALU.mult), r=[Bp], w=[Bp])
                    bstate[(h, i, j)] = (pt_, Bp)

                def stage_b(h, i, j, step):
                    KT, QT, SG, V, Bhd = heads[h]
                    cb, Bcb, po = qstate[(h, i)]
                    pt_, Bp = bstate.pop((h, i, j))
                    m = j - 4 * i
                    for c in range(max(m, 0), 4):
                        (pot, Bpo) = po[c // 2]
                        first = (j == 0 and c % 2 == 0)
                        P.op("tensor", lambda e, pot=pot, c=c, first=first: e.matmul(
                            pot[:, c % 2, 0:129], lhsT=pt_[:, c * 128:(c + 1) * 128],
                            rhs=V[:, j * 129:(j + 1) * 129], start=first, stop=(j == 4 * i + c),
                            skip_group_check=True), r=[Bp, Bhd], w=[Bpo] if first else [], a=[] if first else [Bpo])
                    if j != 4 * i + 3:
                        return
                    rl, Brl = rlrot.next()
                    o16, Bo16 = o16rot.next()
                    for c in range(4):
                        (pot, Bpo) = po[c // 2]
                        P.op("vector", lambda e, pot=pot, c=c: e.reciprocal(
                            out=rl[:, c:c + 1], in_=pot[:, c % 2, 128:129]), r=[Bpo], a=[Brl])
                    for c in range(4):
                        (pot, Bpo) = po[c // 2]
                        P.op("vector", lambda e, pot=pot, c=c: e.tensor_scalar(
                            out=o16[:, c * 128:(c + 1) * 128], in0=pot[:, c % 2, 0:128], scalar1=rl[:, c:c + 1],
                            scalar2=None, op0=ALU.mult), r=[Bpo, Brl], a=[Bo16])
                    del qstate[(h, i)]

                    def fin():
                        for c in range(4):
                            P.op("tensor", lambda e, c=c: e.transpose(
                                pst[:, c * 128:(c + 1) * 128], o16[:, c * 128:(c + 1) * 128], ident[:, :]),
                                 r=[Bo16], w=[B_pst] if c == 0 else [], a=[] if c == 0 else [B_pst])
                        og, Bog = ogrot.next()
                        P.op("vector", lambda e: e.tensor_tensor(
                            out=og[:, :], in0=pst[:, :], in1=SG[:, i * 512:(i + 1) * 512], op=ALU.mult),
                             r=[B_pst, Bhd], w=[Bog])
                        P.dma("sync", lambda e: e.dma_start(
                            out=ogT_s[h, :, i * 512:(i + 1) * 512], in_=og[:, :]), r=[Bog], a=[B_ogT[i]])
                    deferred.append((step + 3, fin))

                NB = len(blocks)
                per_head = NB // H
                assert per_head > LA + 5
                load_head(0)
                load_head(1)
                for step in range(NB + LA + 4):
                    if step < NB:
                        stage_a(*blocks[step])
                    if 0 <= step - LA < NB:
                        stage_b(*blocks[step - LA], step)
                    while deferred and deferred[0][0] <= step:
                        deferred.pop(0)[1]()
                    hh, off = divmod(step, per_head)
                    if off == LA + 4 and 1 <= hh and hh + 1 < H:
                        load_head(hh + 1)
                assert not deferred and not bstate
                P.barrier()
                P.flush()

            if with_sample:
                sample_attention(l, B_sqT, B_skT, B_ssgT, B_sogT, B_slf, B_vs_out)

            with ExitStack() as ph:
                def sbp(name, shape, dt):
                    return ph.enter_context(nc.sbuf_tensor(uq(name), list(shape), dt))

                def psp(name, shape, dt=F32):
                    return ph.enter_context(nc.psum_tensor(uq(name), list(shape), dt))

                wo16 = sbp("wo16", [128, 8, D], BF16)
                B_wo = Buf()
                stage = Rot([sbp("wst%d" % i, [128, 1024], F32) for i in range(2)])
                load_weight_bf16(w_out[l], D, D, wo16, B_wo, stage, 1024)
                ogmrot = Rot([sbp("ogm%d" % i, [128, H, 512], BF16) for i in range(2)])
                xrot = Rot([sbp("xr%d" % i, [128, D], F32) for i in range(3)])
                yrot = Rot([sbp("yr%d" % i, [128, D], F32) for i in range(3)])
                pmm = Rot([psp("pmo%d" % i, [128, 512], F32) for i in range(4)], psum=True)

                def outproj_tile(ogm, Bogm, c0, nt, x_t, Bx, y_t, By):
                    for half in range(2):
                        pso, Bpso = pmm.next()
                        for kc in range(8):
                            P.op("tensor", lambda e, kc=kc, pso=pso, half=half: e.matmul(
                                pso[0:nt, :], lhsT=ogm[:, kc, c0:c0 + nt], rhs=wo16[:, kc, half * 512:(half + 1) * 512],
                                start=(kc == 0), stop=(kc == 7)), r=[Bogm, B_wo], w=[Bpso])
                        P.op("vector", lambda e, pso=pso, half=half: e.tensor_tensor(
                            out=y_t[0:nt, half * 512:(half + 1) * 512], in0=pso[0:nt, :],
                            in1=x_t[0:nt, half * 512:(half + 1) * 512], op=ALU.add), r=[Bpso, Bx], a=[By])

                for m in range(NM):
                    ogm, Bogm = ogmrot.next()
                    P.dma("sync", lambda e, ogm=ogm, m=m: e.dma_start(
                        out=ogm[:, :, :], in_=ogT_s[:, :, m * 512:(m + 1) * 512].rearrange("h p c -> p h c")),
                          r=[B_ogT[m]], w=[Bogm])
                    for sub in range(4):
                        t = m * 4 + sub
                        x_t, Bx = xrot.next()
                        P.dma("sync", lambda e, x_t=x_t, t=t: e.dma_start(out=x_t[:, :],
                                                                          in_=src_p[t * 128:(t + 1) * 128, :]),
                              r=[Bsrc_p[t]], w=[Bx])
                        y_t, By = yrot.next()
                        outproj_tile(ogm, Bogm, sub * 128, 128, x_t, Bx, y_t, By)
                        P.dma(STQ, lambda e, y_t=y_t, t=t: e.dma_start(out=dst_p[t * 128:(t + 1) * 128, :],
                                                                          in_=y_t[:, :]), r=[By], w=[Bdst_p[t]])
                if with_sample:
                    x_t, Bx = xrot.next()
                    P.dma("sync", lambda e, x_t=x_t: e.dma_start(out=x_t[0:NS, :], in_=src_s[:, :]),
                          r=[Bsrc_s], w=[Bx])
                    y_t, By = yrot.next()
                    outproj_tile(s_ogT, B_sogT, 0, NS, x_t, Bx, y_t, By)
                    P.dma(STQ, lambda e, y_t=y_t: e.dma_start(out=dst_s[:, :], in_=y_t[0:NS, :]), r=[By],
                          w=[Bdst_s])
                P.barrier()
                P.flush()

        def sample_attention(l, B_sqT, B_skT, B_ssgT, B_sogT, B_slf, B_vs_out):
            with ExitStack() as ph:
                def sbp(name, shape, dt):
                    return ph.enter_context(nc.sbuf_tensor(uq(name), list(shape), dt))

                def psp(name, shape, dt=F32):
                    return ph.enter_context(nc.psum_tensor(uq(name), list(shape), dt))

                CH = 4
                NCH = PT // CH
                kch = Rot([sbp("kch%d" % i, [128, CH, D], F32) for i in range(2)])
                vch = Rot([sbp("vch%d" % i, [128, CH, D], F32) for i in range(2)])
                KTs = sbp("KTs", [128, H, PT * 128], BF16)
                B_KTs = Buf()
                clf_t = sbp("clf_t", [128, PT, H], F32)
                B_clf = Buf()
                ccache = sbp("ccache", [128, PT, H], F32)
                B_cc = Buf()
                cb_c = sbp("cb_c", [128, H, PT], F32)
                B_cbc = Buf()
                pT_all = sbp("pT_all", [128, H, PT * 16], F32)
                B_pT = Buf()
                pT_new = sbp("pT_new", [16, H, 16], F32)
                B_pTn = Buf()
                vnew = sbp("vnew", [16, D], F32)
                B_vnew = Buf()
                lfn = sbp("lfn", [16, H], F32)
                B_lfn = Buf()
                small = sbp("small", [128, 16], F32)
                B_small = Buf()
                cnew = sbp("cnew", [16, H], F32)
                B_cnew = Buf()
                cref_s = sbp("cref_s", [128, H], F32)
                B_crefs = Buf()
                cbn = sbp("cbn", [16, H], F32)
                B_cbn = Buf()
                tmpS = sbp("tmpS", [128, PT * 16], F32)
                B_tmpS = Buf()
                o_s = sbp("o_s", [16, D], BF16)
                B_os = Buf()
                rls = sbp("rls", [16, H], F32)
                B_rls = Buf()
                ptr = Rot([psp("pstr%d" % i, [128, 512], F32) for i in range(2)], psum=True)
                psS = Rot([psp("psS%d" % i, [128, 512], F32) for i in range(2)], psum=True)
                psO = [psp("psO%d" % i, [16, 4, 128], F32) for i in range(2)]
                B_psO = [Buf(psum=True), Buf(psum=True)]
                psM = psp("psM", [16, 512], F32)
                psL = psM[:, 0:H]
                psN = psM[:, 128:256].rearrange("p (h q) -> p h q", q=16)
                B_psL = Buf(psum=True)
                B_psN = B_psL
                psT = psp("psT", [128, 512], BF16)
                B_psT = Buf(psum=True)
                ckv = lambda a, s: a[l][s].rearrange("(p j) f -> p j f", j=PT)

                for s in range(2):
                    P.dma("sync", lambda e, s=s: e.dma_start(out=clf_t[:, :, :],
                                                             in_=clf[l][s].rearrange("(p j) h -> p j h", j=PT)),
                          w=[B_clf])
                    P.op("vector", lambda e: e.tensor_copy(out=ccache[:, 0, :], in_=clf_t[:, 0, :]), r=[B_clf],
                         w=[B_cc])
                    for j in range(1, PT):
                        P.op("vector", lambda e, j=j: e.tensor_tensor(out=ccache[:, j, :], in0=ccache[:, j - 1, :],
                                                                      in1=clf_t[:, j, :], op=ALU.add),
                             r=[B_clf, B_cc], w=[B_cc])
                    ps1, Bps1 = psS.next()
                    P.op("tensor", lambda e, ps1=ps1: e.matmul(ps1[:, 0:H], lhsT=trix[:, :], rhs=ccache[:, PT - 1, :],
                                                              start=True, stop=True), r=[B_cc], w=[Bps1])
                    P.op("vector", lambda e, ps1=ps1: e.tensor_copy(out=small[:, 0:H], in_=ps1[:, 0:H]),
                         r=[Bps1], w=[B_small])
                    P.op("vector", lambda e: e.tensor_tensor(
                        out=ccache[:, :, :], in0=ccache[:, :, :],
                        in1=small[:, 0:H].unsqueeze(1).to_broadcast([128, PT, H]), op=ALU.add),
                         r=[B_small, B_cc], w=[B_cc])
                    P.dma("sync", lambda e, s=s: e.dma_start(out=lfn[:, :], in_=s_lf[s * 16:(s + 1) * 16, :]),
                          r=[B_slf], w=[B_lfn])
                    ps2, Bps2 = psS.next()
                    P.op("tensor", lambda e, ps2=ps2: e.matmul(ps2[0:16, 0:H], lhsT=sel_last[:, 0:16],
                                                              rhs=ccache[:, PT - 1, :], start=True, stop=False),
                         r=[B_cc], w=[Bps2])
                    P.op("tensor", lambda e, ps2=ps2: e.matmul(ps2[0:16, 0:H], lhsT=tri[0:16, 0:16], rhs=lfn[:, :],
                                                              start=False, stop=True), r=[B_lfn], a=[Bps2])
                    P.op("vector", lambda e, ps2=ps2: e.tensor_copy(out=cnew[:, :], in_=ps2[0:16, 0:H]),
                         r=[Bps2], w=[B_cnew])
                    P.op("vector", lambda e: e.tensor_scalar(out=small[0:16, 8:8 + H], in0=cnew[:, :],
                                                             scalar1=sel15[0:16, 0:1], scalar2=None, op0=ALU.mult),
                         r=[B_cnew], w=[B_small])
                    ps3, Bps3 = psS.next()
                    P.op("tensor", lambda e, ps3=ps3: e.matmul(ps3[:, 0:H], lhsT=ones_f[0:16, :],
                                                              rhs=small[0:16, 8:8 + H], start=True, stop=True),
                         r=[B_small], w=[Bps3])
                    P.op("vector", lambda e, ps3=ps3: e.tensor_copy(out=cref_s[:, :], in_=ps3[:, 0:H]),
                         r=[Bps3], w=[B_crefs])
                    P.op("vector", lambda e: e.tensor_tensor(
                        out=cb_c[:, :, :], in0=cref_s[:, :].unsqueeze(2).to_broadcast([128, H, PT]),
                        in1=ccache[:, :, :].rearrange("p j h -> p h j"), op=ALU.subtract),
                         r=[B_crefs, B_cc], w=[B_cbc])
                    P.op("vector", lambda e: e.tensor_tensor(out=cbn[:, :], in0=cref_s[0:16, :], in1=cnew[:, :],
                                                             op=ALU.subtract), r=[B_crefs, B_cnew], w=[B_cbn])
                    for ch in range(NCH):
                        kc_t, Bkc = kch.next()
                        P.dma("sync", lambda e, kc_t=kc_t, ch=ch, s=s: e.dma_start(
                            out=kc_t[:, :, :], in_=ckv(ck, s)[:, ch * CH:(ch + 1) * CH, :]), w=[Bkc])
                        for jj in range(CH):
                            j = ch * CH + jj
                            for hg in range(2):
                                pt, Bpt = ptr.next()
                                for hh in range(4):
                                    h = hg * 4 + hh
                                    P.op("tensor", lambda e, pt=pt, hh=hh, h=h, jj=jj, kc_t=kc_t: e.transpose(
                                        pt[:, hh * 128:(hh + 1) * 128], kc_t[:, jj, h * 128:(h + 1) * 128],
                                        identf[:, :]), r=[Bkc], w=[Bpt] if hh == 0 else [],
                                         a=[] if hh == 0 else [Bpt])
                                eng = "vector" if (j + hg) % 2 == 0 else "scalar"
                                dst = KTs[:, hg * 4:(hg + 1) * 4, j * 128:(j + 1) * 128]
                                srcp = pt[:, :].rearrange("p (h k) -> p h k", k=128)
                                if eng == "vector":
                                    P.op("vector", lambda e, dst=dst, srcp=srcp: e.tensor_copy(out=dst, in_=srcp),
                                         r=[Bpt], a=[B_KTs])
                                else:
                                    P.op("scalar", lambda e, dst=dst, srcp=srcp: e.copy(out=dst, in_=srcp),
                                         r=[Bpt], a=[B_KTs])
                    P.dma("sync", lambda e, s=s: e.dma_start(out=vnew[:, :], in_=o_vs[l][s * 16:(s + 1) * 16, :]),
                          r=[B_vs_out], w=[B_vnew])
                    for h in range(H):
                        pS, BpS = psS.next()
                        for j in range(PT):
                            P.op("tensor", lambda e, pS=pS, j=j, h=h, s=s: e.matmul(
                                pS[:, j * 16:(j + 1) * 16], lhsT=KTs[:, h, j * 128:(j + 1) * 128],
                                rhs=s_qT[:, h, s * 16:(s + 1) * 16], start=True, stop=True),
                                 r=[B_KTs, B_sqT], w=[BpS] if j == 0 else [], a=[] if j == 0 else [BpS])
                        P.op("vector", lambda e, pS=pS, h=h: e.scalar_tensor_tensor(
                            out=tmpS[:, :].rearrange("p (j q) -> p j q", q=16),
                            in0=pS[:, 0:PT * 16].rearrange("p (j q) -> p j q", q=16), scalar=SCALE,
                            in1=cb_c[:, h, :].unsqueeze(2).to_broadcast([128, PT, 16]), op0=ALU.mult, op1=ALU.add),
                             r=[BpS, B_cbc], w=[B_tmpS])
                        P.op("scalar", lambda e, h=h: e.activation(out=pT_all[:, h, :], in_=tmpS[:, :], func=AF.Exp),
                             r=[B_tmpS], a=[B_pT])
                        P.op("tensor", lambda e, h=h, s=s: e.matmul(
                            psN[:, h, :], lhsT=s_kT[:, h, s * 16:(s + 1) * 16], rhs=s_qT[:, h, s * 16:(s + 1) * 16],
                            start=True, stop=True), r=[B_skT, B_sqT], a=[B_psN])
                    for h in range(H):
                        P.op("scalar", lambda e, h=h: e.activation(out=pT_new[:, h, :], in_=psN[:, h, :], func=AF.Exp,
                                                                   bias=cbn[:, h:h + 1], scale=SCALE),
                             r=[B_psN, B_cbn], a=[B_pTn])
                    P.op("vector", lambda e: e.tensor_tensor(
                        out=pT_new[:, :, :], in0=pT_new[:, :, :],
                        in1=tri[0:16, 0:16].unsqueeze(1).to_broadcast([16, H, 16]), op=ALU.mult),
                         r=[B_pTn], w=[B_pTn])
                    for h in range(H):
                        for j in range(PT):
                            P.op("tensor", lambda e, h=h, j=j: e.matmul(
                                psL[:, h:h + 1], lhsT=pT_all[:, h, j * 16:(j + 1) * 16], rhs=ones_f[:, 0:1],
                                start=(j == 0 and h == 0), stop=False, skip_group_check=True), r=[B_pT],
                                     w=[B_psL] if (j == 0 and h == 0) else [], a=[] if (j == 0 and h == 0) else [B_psL])
                        P.op("tensor", lambda e, h=h: e.matmul(psL[:, h:h + 1], lhsT=pT_new[:, h, :],
                                                                rhs=ones_f[0:16, 0:1], start=False, stop=True,
                                                                skip_group_check=True),
                             r=[B_pTn], a=[B_psL])
                    for ch in range(NCH):
                        vc_t, Bvc = vch.next()
                        P.dma("sync", lambda e, vc_t=vc_t, ch=ch, s=s: e.dma_start(
                            out=vc_t[:, :, :], in_=ckv(cv, s)[:, ch * CH:(ch + 1) * CH, :]), w=[Bvc])
                        for jj in range(CH):
                            j = ch * CH + jj
                            for h in range(H):
                                P.op("tensor", lambda e, h=h, j=j, jj=jj, vc_t=vc_t: e.matmul(
                                    psO[h // 4][:, h % 4, :], lhsT=pT_all[:, h, j * 16:(j + 1) * 16],
                                    rhs=vc_t[:, jj, h * 128:(h + 1) * 128], start=(j == 0 and h % 4 == 0), stop=False,
                                    skip_group_check=True),
                                     r=[B_pT, Bvc], a=[B_psO[h // 4]])
                    for h in range(H):
                        P.op("tensor", lambda e, h=h: e.matmul(
                            psO[h // 4][:, h % 4, :], lhsT=pT_new[:, h, :], rhs=vnew[:, h * 128:(h + 1) * 128],
                            start=False, stop=True, skip_group_check=True), r=[B_pTn, B_vnew], a=[B_psO[h // 4]])
                    P.op("vector", lambda e: e.reciprocal(out=rls[:, :], in_=psL), r=[B_psL], w=[B_rls])
                    for hg in range(2):
                        P.op("vector", lambda e, hg=hg: e.tensor_tensor(
                            out=o_s[:, hg * 512:(hg + 1) * 512].rearrange("p (h d) -> p h d", d=128),
                            in0=psO[hg][:, :, :],
                            in1=rls[:, hg * 4:(hg + 1) * 4].unsqueeze(2).to_broadcast([16, 4, 128]), op=ALU.mult),
                             r=[B_psO[hg], B_rls], a=[B_os])
                    if DEBUG and l == 0:
                        for nm, t_, B_ in (("d_cnew", cnew[:, :], B_cnew), ("d_cref", cref_s[:, :], B_crefs),
                                           ("d_rls", rls[:, :], B_rls),
                                           ("d_pTn", pT_new[:, :, :].rearrange("p h q -> p (h q)"), B_pTn),
                                           ("d_cc", ccache[:, :, :].rearrange("p j h -> p (j h)"), B_cc),
                                           ("d_lfn", lfn[:, :], B_lfn), ("d_cbn", cbn[:, :], B_cbn),
                                           ("d_os", o_s[:, :], B_os),
                                           ("d_pT", pT_all[:, :, :].rearrange("p h q -> p (h q)"), B_pT)):
                            P.dma("sync", lambda e, nm=nm, t_=t_, s=s: e.dma_start(out=DBG[nm][s], in_=t_), r=[B_])
                    for hg in range(2):
                        for hh in range(4):
                            h = hg * 4 + hh
                            P.op("tensor", lambda e, hh=hh, h=h: e.transpose(
                                psT[:, hh * 128:hh * 128 + 16], o_s[:, h * 128:(h + 1) * 128], ident[0:16, 0:16]),
                                 r=[B_os], w=[B_psT] if hh == 0 else [], a=[] if hh == 0 else [B_psT])
                        P.op("vector", lambda e, hg=hg, s=s: e.tensor_tensor(
                            out=s_ogT[:, hg * 4:(hg + 1) * 4, s * 16:(s + 1) * 16],
                            in0=psT[:, :].rearrange("p (h k) -> p h k", k=128)[:, :, 0:16],
                            in1=s_sgT[:, hg * 4:(hg + 1) * 4, s * 16:(s + 1) * 16], op=ALU.mult),
                             r=[B_psT, B_ssgT], a=[B_sogT])
                P.barrier()
                P.flush()

        def conv_layer(l, src_p, Bsrc_p, src_s, Bsrc_s, dst_p, Bdst_p, dst_s, Bdst_s):
            with ExitStack() as ph:
                def sbp(name, shape, dt):
                    return ph.enter_context(nc.sbuf_tensor(uq(name), list(shape), dt))

                def psp(name, shape, dt=F32):
                    return ph.enter_context(nc.psum_tensor(uq(name), list(shape), dt))

                w16 = sbp("cw16", [128, 8, 4 * D], BF16)
                B_w = Buf()
                wo16 = sbp("cwo16", [128, 8, D], BF16)
                B_wo = Buf()
                stage = Rot([sbp("cwst%d" % i, [128, 128], F32) for i in range(2)])
                gtile = sbp("cgtile", [128, D], F32)
                cwt = sbp("cwt", [128, 8, 3], F32)
                B_g = Buf()
                P.dma("sync", lambda e: e.dma_start(out=gtile[:], in_=g_norm[l].partition_broadcast(128)), a=[B_g])
                for j in range(3):
                    P.dma("sync", lambda e, j=j: e.dma_start(out=cwt[:, :, j],
                                                             in_=conv_w[l][j].rearrange("(c p) -> p c", p=128),
                                                             allow_slow_non_contiguous=True), a=[B_g])

                xm_rot = Rot([sbp("cxm%d" % i, [128, 4, D], F32) for i in range(2)])
                junk = sbp("cjunk", [128, D], BF16)
                Bjunk = Buf()
                strot = Rot([sbp("cst%d" % i, [128, 4], F32) for i in range(4)])
                hrot = Rot([sbp("ch%d" % i, [128, D], BF16) for i in range(4)])
                hTrot = Rot([sbp("chT%d" % i, [128, 8, 512], BF16) for i in range(2)])
                cu = sbp("cu", [128, 8, 2 + 512], F32)
                B_cu = [Buf() for _ in range(8)]
                cu_s = sbp("cu_s", [128, 8, 2, 2 + 16], F32)
                B_cus = Buf()
                c_sb_rot = Rot([sbp("c_sb%d" % i, [128, 512], F32) for i in range(2)])
                cv_rot = Rot([sbp("cvv%d" % i, [128, 512], F32) for i in range(2)])
                e_rot = Rot([sbp("ce%d" % i, [128, 512], F32) for i in range(2)])
                ogT_rot = Rot([sbp("cogT%d" % i, [128, 8, 512], BF16) for i in range(2)])
                y_rot = Rot([sbp("cy%d" % i, [128, D], F32) for i in range(2)])
                big = Rot(stage.tiles + y_rot.tiles + c_sb_rot.tiles + cv_rot.tiles + e_rot.tiles)
                big.bufs = stage.bufs + y_rot.bufs + c_sb_rot.bufs + cv_rot.bufs + e_rot.bufs
                load_weight_bf16(w_in[l], D, 4 * D, w16, B_w, big, 512)
                load_weight_bf16(w_out[l], D, D, wo16, B_wo, big, 512)
                pmm = Rot([psp("cpm%d" % i, [128, 512], F32) for i in range(6)], psum=True)
                ptr = Rot([psp("cpt%d" % i, [128, 1024], BF16) for i in range(2)], psum=True)

                for cc in range(8):
                    P.op("gpsimd", lambda e, cc=cc: e.memset(cu[:, cc, 0:2], 0.0), w=[B_cu[cc]])

                def macro(ntok, nsub, segs, cu_t, B_cu_l, is_s, m):
                    xm, Bxm = xm_rot.next()
                    if is_s:
                        P.dma("sync", lambda e: e.dma_start(out=xm[0:NS, 0, :], in_=src_s[:, :]), r=[Bsrc_s], w=[Bxm])
                    else:
                        P.dma("sync", lambda e: e.dma_start(
                            out=xm[:, :, :],
                            in_=src_p[m * 512:(m + 1) * 512, :].rearrange("(s p) f -> p s f", p=128)),
                              r=[Bsrc_p[m * 4 + k] for k in range(4)], w=[Bxm])
                    hs = []
                    for sub in range(nsub):
                        nt = min(128, ntok - sub * 128)
                        st_s, Bst = strot.next()
                        h16, Bh = hrot.next()
                        rms_tile(xm[:, sub, :], Bxm, nt, gtile, B_g, h16, Bh, junk, Bjunk, st_s, Bst)
                        hs.append((h16, Bh, nt))
                    yield "fa"
                    hT, BhT = hTrot.next()
                    for sub in range(nsub):
                        h16, Bh, nt = hs[sub]
                        pt, Bpt = ptr.next()
                        transpose_rows(h16, Bh, nt, 8, pt, Bpt, hT[:, :, sub * 128:sub * 128 + nt], BhT, "vector",
                                       acc=True)
                    yield "fb"
                    ogT, BogT = ogT_rot.next()
                    for cc in range(8):
                        if cc == 4:
                            yield "A"
                        def mm_feat(f, pso, Bpso):
                            for kc in range(8):
                                P.op("tensor", lambda e, kc=kc: e.matmul(
                                    pso[:, 0:ntok], lhsT=w16[:, kc, f * 128:(f + 1) * 128], rhs=hT[:, kc, 0:ntok],
                                    start=(kc == 0), stop=(kc == 7)), r=[BhT, B_w], w=[Bpso])
                        pc, Bpc = pmm.next()
                        mm_feat(8 + cc, pc, Bpc)
                        pu, Bpu = pmm.next()
                        mm_feat(16 + cc, pu, Bpu)
                        c_sb, Bc_sb = c_sb_rot.next()
                        P.op("scalar", lambda e, pc=pc, c_sb=c_sb: e.copy(out=c_sb[:, 0:ntok], in_=pc[:, 0:ntok]),
                             r=[Bpc], w=[Bc_sb])
                        cvv, Bcv = cv_rot.next()
                        for (c0, ln, seg_i) in segs:
                            base = (lambda o, seg_i=seg_i, ln=ln, cc=cc: cu_t[:, cc, seg_i, o:o + ln]) if is_s else \
                                   (lambda o, ln=ln, cc=cc: cu_t[:, cc, o:o + ln])
                            P.op("vector", lambda e, pu=pu, c_sb=c_sb, c0=c0, ln=ln, base=base: e.tensor_tensor(
                                out=base(2), in0=pu[:, c0:c0 + ln], in1=c_sb[:, c0:c0 + ln], op=ALU.mult),
                                 r=[Bpu, Bc_sb], a=[B_cu_l[cc]])
                            P.op("scalar", lambda e, cvv=cvv, base=base, c0=c0, ln=ln, cc=cc: e.activation(
                                out=cvv[:, c0:c0 + ln], in_=base(2), func=AF.Identity, scale=cwt[:, cc, 2:3]),
                                 r=[B_cu_l[cc], B_g], a=[Bcv])
                            P.op("vector", lambda e, cvv=cvv, base=base, c0=c0, ln=ln, cc=cc: e.scalar_tensor_tensor(
                                out=cvv[:, c0:c0 + ln], in0=base(1), scalar=cwt[:, cc, 1:2], in1=cvv[:, c0:c0 + ln],
                                op0=ALU.mult, op1=ALU.add), r=[B_cu_l[cc], B_g, Bcv], a=[Bcv])
                            P.op("vector", lambda e, cvv=cvv, base=base, c0=c0, ln=ln, cc=cc: e.scalar_tensor_tensor(
                                out=cvv[:, c0:c0 + ln], in0=base(0), scalar=cwt[:, cc, 0:1], in1=cvv[:, c0:c0 + ln],
                                op0=ALU.mult, op1=ALU.add), r=[B_cu_l[cc], B_g, Bcv], a=[Bcv])
                        if not is_s:
                            P.op("gpsimd", lambda e, cc=cc: e.tensor_copy(out=cu_t[:, cc, 0:2],
                                                                          in_=cu_t[:, cc, 512:514]),
                                 r=[B_cu_l[cc]], w=[B_cu_l[cc]])
                        pg, Bpg = pmm.next()
                        mm_feat(24 + cc, pg, Bpg)
                        pb, Bpb = pmm.next()
                        mm_feat(cc, pb, Bpb)
                        ee, Be = e_rot.next()
                        P.op("scalar", lambda e, pg=pg, ee=ee: e.activation(out=ee[:, 0:ntok], in_=pg[:, 0:ntok],
                                                                           func=AF.Exp, scale=-1.0), r=[Bpg], w=[Be])
                        P.op("scalar", lambda e, ee=ee: e.activation(out=ee[:, 0:ntok], in_=ee[:, 0:ntok], func=AF.Ln,
                                                                    bias=1.0, scale=1.0), r=[Be], w=[Be])
                        P.op("scalar", lambda e, ee=ee: e.activation(out=ee[:, 0:ntok], in_=ee[:, 0:ntok], func=AF.Exp,
                                                                    scale=-1.0), r=[Be], w=[Be])
                        P.op("vector", lambda e, ee=ee, pg=pg: e.tensor_tensor(out=ee[:, 0:ntok], in0=pg[:, 0:ntok],
                                                                               in1=ee[:, 0:ntok], op=ALU.mult),
                             r=[Be, Bpg], w=[Be])
                        P.op("vector", lambda e, ee=ee, cvv=cvv: e.tensor_tensor(out=ee[:, 0:ntok], in0=ee[:, 0:ntok],
                                                                                 in1=cvv[:, 0:ntok], op=ALU.mult),
                             r=[Be, Bcv], w=[Be])
                        P.op("vector", lambda e, ee=ee, pb=pb, cc=cc: e.tensor_tensor(
                            out=ogT[:, cc, 0:ntok], in0=pb[:, 0:ntok], in1=ee[:, 0:ntok], op=ALU.mult),
                             r=[Be, Bpb], a=[BogT])
                    yield "B1"
                    for sub in range(nsub):
                        nt = min(128, ntok - sub * 128)
                        y_t, By = y_rot.next()
                        for half in range(2):
                            pso, Bpso = pmm.next()
                            for kc in range(8):
                                P.op("tensor", lambda e, kc=kc, pso=pso, half=half, sub=sub, nt=nt: e.matmul(
                                    pso[0:nt, :], lhsT=ogT[:, kc, sub * 128:sub * 128 + nt],
                                    rhs=wo16[:, kc, half * 512:(half + 1) * 512], start=(kc == 0), stop=(kc == 7)),
                                     r=[BogT, B_wo], w=[Bpso])
                            P.op("vector", lambda e, pso=pso, half=half, sub=sub, nt=nt, y_t=y_t: e.tensor_tensor(
                                out=y_t[0:nt, half * 512:(half + 1) * 512], in0=pso[0:nt, :],
                                in1=xm[0:nt, sub, half * 512:(half + 1) * 512], op=ALU.add),
                                 r=[Bpso, Bxm], a=[By])
                        if is_s:
                            P.dma(STQ, lambda e, y_t=y_t: e.dma_start(out=dst_s[:, :], in_=y_t[0:NS, :]),
                                  r=[By], w=[Bdst_s])
                        else:
                            t = m * 4 + sub
                            P.dma(STQ, lambda e, y_t=y_t, t=t: e.dma_start(out=dst_p[t * 128:(t + 1) * 128, :],
                                                                              in_=y_t[:, :]), r=[By], w=[Bdst_p[t]])

                mg = [macro(512, 4, [(0, 512, 0)], cu, B_cu, False, m) for m in range(NM)]
                if with_sample:
                    def sample_pre():
                        for s_ in range(2):
                            for j in range(2):
                                P.dma("sync", lambda e, s_=s_, j=j: e.dma_start(
                                    out=cu_s[:, :, s_, j], in_=sconv[l][s_, j].rearrange("(c p) -> p c", p=128),
                                    allow_slow_non_contiguous=True), a=[B_cus])
                    sample_pre()
                    mg.append(macro(NS, 1, [(0, 16, 0), (16, 16, 1)], cu_s, [B_cus] * 8, True, 0))
                nmg = len(mg)

                def cadv(mi, want):
                    if 0 <= mi < nmg:
                        got = next(mg[mi])
                        assert got == want, (got, want)
                cadv(0, "fa")
                cadv(0, "fb")
                for mi in range(nmg):
                    cadv(mi, "A")
                    cadv(mi + 1, "fa")
                    cadv(mi, "B1")
                    cadv(mi + 1, "fb")
                    for _ in mg[mi]:
                        pass
                    if mi == NM - 1:
                        for j in range(2):
                            P.dma(STQ, lambda e, j=j: e.dma_start(out=o_cp[l][j].rearrange("(c p) -> p c", p=128),
                                                                     in_=cu[:, :, j], allow_slow_non_contiguous=True),
                                  r=B_cu)
                if with_sample:
                    for s_ in range(2):
                        for j in range(2):
                            P.dma(STQ, lambda e, s_=s_, j=j: e.dma_start(
                                out=o_cs[l][s_, j].rearrange("(c p) -> p c", p=128), in_=cu_s[:, :, s_, 16 + j],
                                allow_slow_non_contiguous=True), r=[B_cus])
                P.barrier()
                P.flush()

        B_y = [[Buf() for _ in range(NT)] for _ in range(2)]
        B_ys = [Buf(), Buf()]
        srcs_p = [xp, y_sc[0], y_sc[1], y_sc[0]]
        Bsrcs_p = [[Buf() for _ in range(NT)], B_y[0], B_y[1], B_y[0]]
        dsts_p = [y_sc[0], y_sc[1], y_sc[0], o_yp]
        Bdsts_p = [B_y[0], B_y[1], B_y[0], None]
        srcs_s = [xs, ys_sc[0], ys_sc[1], ys_sc[0]]
        Bsrcs_s = [Buf(), B_ys[0], B_ys[1], B_ys[0]]
        dsts_s = [ys_sc[0], ys_sc[1], ys_sc[0], o_ys]
        Bdsts_s = [B_ys[0], B_ys[1], B_ys[0], None]
        for l in range(NL):
            last = l == NL - 1
            dp = o_yp if last else dsts_p[l]
            ds = o_ys if last else dsts_s[l]
            Bdp = [Buf() for _ in range(NT)] if last else Bdsts_p[l]
            Bds = Buf() if last else Bdsts_s[l]
            if l % 2 == 0:
                fox_layer(l, srcs_p[l], Bsrcs_p[l], srcs_s[l], Bsrcs_s[l], dp, Bdp, ds, Bds)
            else:
                conv_layer(l, srcs_p[l], Bsrcs_p[l], srcs_s[l], Bsrcs_s[l], dp, Bdp, ds, Bds)
    build.last_nsem = P.nsem
    return nc


_CACHE = {}


def run_cores(S, PAST, NL, in_maps, with_sample=True):
    key = (S, PAST, NL, with_sample)
    nc = build(S, PAST, NL, with_sample)
    res = run_bass_kernel_spmd(nc, in_maps, core_ids=list(range(len(in_maps))))
    return res.results


def make_in_map(inp, core, S, PAST):
    b = core // 2
    f = lambda a: np.ascontiguousarray(a, dtype=np.float32)
    sl = slice(2 * core, 2 * core + 2)
    m = {"xp": f(inp["x_prompt"][b]), "xs": f(inp["x_sample"][sl].reshape(32, D))}
    m["norm0"], m["w_in0"], m["w_out0"] = f(inp["norm_l0"]), f(inp["w_in_l0"]), f(inp["w_out_l0"])
    m["b_f0"], m["qn0"], m["kn0"] = f(inp["b_f_l0"]), f(inp["qnorm_l0"]), f(inp["knorm_l0"])
    m["ck0"] = f(inp["cache_k_l0"][sl].reshape(2, PAST, D))
    m["cv0"] = f(inp["cache_v_l0"][sl].reshape(2, PAST, D))
    m["clf0"] = f(inp["cache_logf_l0"][sl])
    m["norm2"], m["w_in2"], m["w_out2"] = f(inp["norm_l2"]), f(inp["w_in_l2"]), f(inp["w_out_l2"])
    m["b_f2"], m["qn2"], m["kn2"] = f(inp["b_f_l2"]), f(inp["qnorm_l2"]), f(inp["knorm_l2"])
    m["ck2"] = f(inp["cache_k_l2"][sl].reshape(2, PAST, D))
    m["cv2"] = f(inp["cache_v_l2"][sl].reshape(2, PAST, D))
    m["clf2"] = f(inp["cache_logf_l2"][sl])
    m["norm1"], m["w_in1"], m["w_out1"] = f(inp["norm_l1"]), f(inp["w_in_l1"]), f(inp["w_out_l1"])
    m["cw1"], m["sc1"] = f(inp["conv_w_l1"]), f(inp["state_conv_l1"][sl])
    m["norm3"], m["w_in3"], m["w_out3"] = f(inp["norm_l3"]), f(inp["w_in_l3"]), f(inp["w_out_l3"])
    m["cw3"], m["sc3"] = f(inp["conv_w_l3"]), f(inp["state_conv_l3"][sl])
    return m


def assemble(results, B, S):
    nb = B
    DB = 2 * len(results)
    yp = np.stack([results[2 * b]["o_yp"] for b in range(nb)])
    ys = np.concatenate([r["o_ys"].reshape(2, 16, D) for r in results])
    outs = [yp, ys]
    for l in range(4):
        if l % 2 == 0:
            outs.append(np.stack([results[2 * b]["o_kp%d" % l].reshape(S, H, DH) for b in range(nb)]))
            outs.append(np.stack([results[2 * b]["o_vp%d" % l].reshape(S, H, DH) for b in range(nb)]))
            outs.append(np.stack([results[2 * b]["o_lfp%d" % l] for b in range(nb)]))
            outs.append(np.concatenate([r["o_ks%d" % l].reshape(2, 16, H, DH) for r in results]))
            outs.append(np.concatenate([r["o_vs%d" % l].reshape(2, 16, H, DH) for r in results]))
            outs.append(np.concatenate([r["o_lfs%d" % l].reshape(2, 16, H) for r in results]))
        else:
            outs.append(np.stack([results[2 * b]["o_cp%d" % l] for b in range(nb)]))
            outs.append(np.concatenate([r["o_cs%d" % l] for r in results]))
    return tuple(np.ascontiguousarray(o, dtype=np.float32) for o in outs)


def kernel(**inputs):
    inp = {k: np.asarray(v) for k, v in inputs.items()}
    B, S, _ = inp["x_prompt"].shape
    PAST = inp["cache_k_l0"].shape[1]
    in_maps = [make_in_map(inp, c, S, PAST) for c in range(8)]
    results = run_cores(S, PAST, 4, in_maps)
    return assemble(results, B, S)
```
